# Optimizing a Trainium2 kernel written in Bass

```python
import math
import jax, jax.numpy as jnp
from jax import lax
import numpy as np

D_MODEL = 2048
BATCH = 2
SEQ = 4096
DEPTH = 2

N_GROUPS = 4
GROUP_W = D_MODEL // N_GROUPS
D_MIX = N_GROUPS * GROUP_W
N_IN_BLOCKS = 15
CONV_W = 3
HG_HEADS = 4
HG_DK = GROUP_W // HG_HEADS
HG_CHUNK = 64
F_FLOOR = 1e-30
DA_HEADS = 4
DA_DV = GROUP_W // DA_HEADS
DA_DQK = DA_DV // 2
Q_BLOCK = 128
MASK_VALUE = -1e30
SG_CHUNK = 128
SG_GROUPS = 4
SG_CH = GROUP_W // SG_GROUPS
PLE_DIM = 256
ALPHA = (2 * DEPTH) ** 0.25
BETA = (8 * DEPTH) ** -0.25
LN_EPS = 1e-5
RMS_EPS = 1e-6

kernel_name = "hybrid_parallel_groups_conv_hgrn2_diffattn_sgu"


def _layer_norm(x, g, b):
    xf = x.astype(jnp.float32)
    mu = jnp.mean(xf, axis=-1, keepdims=True)
    var = jnp.mean(jnp.square(xf - mu), axis=-1, keepdims=True)
    y = (xf - mu) * lax.rsqrt(var + LN_EPS) * g.astype(jnp.float32) + b.astype(jnp.float32)
    return y.astype(x.dtype)


def _rms_norm(x, g):
    xf = x.astype(jnp.float32)
    y = xf * lax.rsqrt(jnp.mean(jnp.square(xf), axis=-1, keepdims=True) + RMS_EPS) * g.astype(jnp.float32)
    return y.astype(x.dtype)


def short_conv_mixer(b, c, xv, w):
    s = xv.shape[1]
    z = c * xv
    zp = jnp.pad(z, ((0, 0), (CONV_W - 1, 0), (0, 0)))
    y = sum(w[j] * zp[:, j:j + s] for j in range(CONV_W))
    return b * y


def hgrn2_mixer(q, fz, iv, lb, norm_g):
    bsz, s, _ = q.shape
    n = s // HG_CHUNK
    fz32 = fz.astype(jnp.float32)
    lb = lb.astype(jnp.float32)
    f = lb + (1.0 - lb) * jax.nn.sigmoid(fz32)
    log_f = jnp.log(jnp.maximum(f, F_FLOOR))
    k = (1.0 - lb) * jax.nn.sigmoid(-fz32)

    def chunks(t):
        return t.astype(jnp.float32).reshape(bsz, n, HG_CHUNK, HG_HEADS, HG_DK).transpose(1, 0, 3, 2, 4)

    qc, kc, vc, gc = chunks(q), chunks(k), chunks(iv), chunks(log_f)
    bc = jnp.cumsum(gc, axis=3)
    causal = jnp.tril(jnp.ones((HG_CHUNK, HG_CHUNK), dtype=bool))[:, :, None]

    def step(state, inp):
        qt, kt, vt, bt = inp
        diff = bt[:, :, :, None, :] - bt[:, :, None, :, :]
        decay = jnp.where(causal, jnp.exp(jnp.where(causal, diff, 0.0)), 0.0)
        a = jnp.einsum('bhtk,bhsk,bhtsk->bhts', qt, kt, decay)
        o = jnp.einsum('bhts,bhsv->bhtv', a, vt) + jnp.einsum('bhtk,bhkv->bhtv', qt * jnp.exp(bt), state)
        b_last = bt[:, :, -1, :]
        new_state = jnp.exp(b_last)[..., None] * state + jnp.einsum(
            'bhsk,bhsv->bhkv', kt * jnp.exp(b_last[:, :, None, :] - bt), vt)
        return new_state, o

    state0 = jnp.zeros((bsz, HG_HEADS, HG_DK, HG_DK), jnp.float32)
    _, o = lax.scan(step, state0, (qc, kc, vc, bc))
    o = o.transpose(1, 0, 3, 2, 4).reshape(bsz, s, HG_HEADS, HG_DK)
    o = _rms_norm(o, norm_g.reshape(HG_HEADS, HG_DK))
    return o.reshape(bsz, s, GROUP_W).astype(q.dtype)


def diff_attn_mixer(q, k, v, lam, lam_init, norm_g):
    bsz, s, _ = q.shape
    nb = s // Q_BLOCK
    q = q.reshape(bsz, s, DA_HEADS, 2, DA_DQK)
    k = k.reshape(bsz, s, DA_HEADS, 2, DA_DQK)
    q1, q2 = q[..., 0, :].transpose(0, 2, 1, 3), q[..., 1, :].transpose(0, 2, 1, 3)
    k1, k2 = k[..., 0, :].transpose(0, 2, 1, 3), k[..., 1, :].transpose(0, 2, 1, 3)
    vh = v.reshape(bsz, s, DA_HEADS, DA_DV).transpose(0, 2, 1, 3)

    def to_blocks(t):
        return t.reshape(bsz, DA_HEADS, nb, Q_BLOCK, DA_DQK).transpose(2, 0, 1, 3, 4)

    qpos = jnp.arange(s).reshape(nb, Q_BLOCK)
    kpos = jnp.arange(s)
    scale = DA_DQK ** -0.5

    def one_block(args):
        q1b, q2b, pos = args
        mask = kpos[None, :] <= pos[:, None]

        def probs(qb, kk):
            sc = jnp.einsum('bhqd,bhkd->bhqk', qb, kk).astype(jnp.float32) * scale
            return jax.nn.softmax(jnp.where(mask, sc, MASK_VALUE), axis=-1)

        a = probs(q1b, k1) - lam * probs(q2b, k2)
        return jnp.einsum('bhqk,bhkv->bhqv', a.astype(vh.dtype), vh)

    o = lax.map(one_block, (to_blocks(q1), to_blocks(q2), qpos))
    o = o.transpose(1, 0, 3, 2, 4).reshape(bsz, s, DA_HEADS, DA_DV)
    o = _rms_norm(o, norm_g.reshape(DA_HEADS, DA_DV)) * (1.0 - lam_init)
    return o.reshape(bsz, s, GROUP_W).astype(q.dtype)


def spatial_gate_mixer(u, v, ln_g, ln_b, ws, bs):
    bsz, s, _ = u.shape
    n = s // SG_CHUNK
    vn = _layer_norm(v, ln_g, ln_b).reshape(bsz, n, SG_CHUNK, SG_GROUPS, SG_CH)
    w = ws * jnp.tril(jnp.ones((SG_CHUNK, SG_CHUNK), ws.dtype))
    sv = jnp.einsum('gts,bnsgc->bntgc', w, vn) + bs.T[:, :, None]
    return u * sv.reshape(bsz, s, GROUP_W)


def setup_inputs(seed: int = 0) -> dict:
    key = jax.random.key(seed)
    ks = jax.random.split(key, 20)
    f32 = jnp.float32
    x = jax.random.normal(ks[0], (BATCH, SEQ, D_MODEL), f32)
    p = jax.random.normal(ks[1], (DEPTH, BATCH, SEQ, PLE_DIM), f32)
    value_blocks = np.array([1, 1, BETA, 1, 1, BETA, 1, 1, BETA, BETA, 1, 1, 1, 1, 1], np.float32)
    col_scale = jnp.asarray(np.repeat(value_blocks, GROUP_W))
    w_in = jax.random.normal(ks[2], (DEPTH, D_MODEL, N_IN_BLOCKS * GROUP_W), f32) * (D_MODEL ** -0.5) * col_scale
    conv_w = jax.random.normal(ks[3], (DEPTH, CONV_W, GROUP_W), f32) * (CONV_W ** -0.5)
    hgrn_lb = jax.random.normal(ks[4], (DEPTH, GROUP_W), f32) * 0.5
    hgrn_norm_g = 1.0 + 0.02 * jax.random.normal(ks[5], (DEPTH, GROUP_W), f32)
    diff_lambda = 0.1 * jax.random.normal(ks[6], (DEPTH, 4, DA_DQK), f32)
    diff_norm_g = 1.0 + 0.02 * jax.random.normal(ks[7], (DEPTH, GROUP_W), f32)
    sg_ln_g = 1.0 + 0.02 * jax.random.normal(ks[8], (DEPTH, GROUP_W), f32)
    sg_ln_b = 0.02 * jax.random.normal(ks[9], (DEPTH, GROUP_W), f32)
    sg_w = jax.random.normal(ks[10], (DEPTH, SG_GROUPS, SG_CHUNK, SG_CHUNK), f32) * (SG_CHUNK ** -0.5)
    sg_b = 1.0 + 0.02 * jax.random.normal(ks[11], (DEPTH, SG_GROUPS, SG_CHUNK), f32)
    w_out = jax.random.normal(ks[12], (DEPTH, D_MIX, D_MODEL), f32) * (D_MIX ** -0.5) * BETA
    ln_g = 1.0 + 0.02 * jax.random.normal(ks[13], (DEPTH, D_MODEL), f32)
    ln_b = 0.02 * jax.random.normal(ks[14], (DEPTH, D_MODEL), f32)
    w_pe = jax.random.normal(ks[15], (DEPTH, PLE_DIM, D_MODEL), f32) * (PLE_DIM ** -0.5)
    w_pg = jax.random.normal(ks[16], (DEPTH, D_MODEL, D_MODEL), f32) * (D_MODEL ** -0.5)
    return {"x": x, "p": p, "w_in": w_in, "conv_w": conv_w, "hgrn_lb": hgrn_lb,
            "hgrn_norm_g": hgrn_norm_g, "diff_lambda": diff_lambda, "diff_norm_g": diff_norm_g,
            "sg_ln_g": sg_ln_g, "sg_ln_b": sg_ln_b, "sg_w": sg_w, "sg_b": sg_b,
            "w_out": w_out, "ln_g": ln_g, "ln_b": ln_b, "w_pe": w_pe, "w_pg": w_pg}


def reference(x, p, w_in, conv_w, hgrn_lb, hgrn_norm_g, diff_lambda, diff_norm_g,
              sg_ln_g, sg_ln_b, sg_w, sg_b, w_out, ln_g, ln_b, w_pe, w_pg):
    lb_sm = jax.nn.softmax(hgrn_lb.astype(jnp.float32), axis=0)
    lower_bounds = jnp.cumsum(lb_sm, axis=0) - lb_sm[0]
    for i in range(DEPTH):
        lam_init = 0.8 - 0.6 * math.exp(-0.3 * i)
        lq1, lk1, lq2, lk2 = (diff_lambda[i, j].astype(jnp.float32) for j in range(4))
        lam = jnp.exp(jnp.sum(lq1 * lk1)) - jnp.exp(jnp.sum(lq2 * lk2)) + lam_init

        h = x @ w_in[i]
        (a_b, a_c, a_x, b_q, b_f, b_i, c_q, c_k, c_v,
         d_u, d_v, g_a, g_b, g_c, g_d) = jnp.split(h, N_IN_BLOCKS, axis=-1)

        y_a = short_conv_mixer(a_b, a_c, a_x, conv_w[i])
        y_b = hgrn2_mixer(b_q, b_f, b_i, lower_bounds[i], hgrn_norm_g[i])
        y_c = diff_attn_mixer(c_q, c_k, c_v, lam, lam_init, diff_norm_g[i])
        y_d = spatial_gate_mixer(jax.nn.gelu(d_u), jax.nn.gelu(d_v), sg_ln_g[i], sg_ln_b[i], sg_w[i], sg_b[i])

        y = jnp.concatenate([y_a * jax.nn.silu(g_a), y_b * jax.nn.silu(g_b),
                             y_c * jax.nn.silu(g_c), y_d * jax.nn.silu(g_d)], axis=-1) @ w_out[i]
        x = _layer_norm(ALPHA * x + y, ln_g[i], ln_b[i])
        x = x + (p[i] @ w_pe[i]) * jax.nn.sigmoid(x @ w_pg[i])
    return x
```

```python
import contextlib
import numpy as np
import concourse.bass as bass
import concourse.mybir as mybir
from concourse.bass_utils import run_bass_kernel_spmd

F32 = mybir.dt.float32
BF16 = mybir.dt.bfloat16
AF = mybir.ActivationFunctionType
ALU = mybir.AluOpType

DEPTH = 2
DM = 2048
SEQ = 4096
NCORE = 8
TQ = 1024
NBLK = 8
NWC = 2304
ALPHA = (2 * DEPTH) ** 0.25
LN_EPS = 1e-5
RMS_EPS = 1e-6
F_FLOOR = 1e-30
GELU_C = 0.7978845608028654
NROW = 256 + 3 * 128 + 2 * DM
AW = 51200
SAME_ENG_RAW = True
CC_QOS = "P2"
DBG = {"nblk": NBLK, "parts": {"hprep", "conv", "sg", "attn", "hmm"}, "skip": set()}

FM_BQ, FM_BF, FM_AB, FM_AC, FM_AX, FM_DU, FM_GA, FM_GB, FM_GC, FM_GD, FM_CQ, FM_CK = range(12)
FM_SRC = {FM_BQ: 3, FM_BF: 4, FM_AB: 0, FM_AC: 1, FM_AX: 2, FM_DU: 9, FM_GA: 11, FM_GB: 12, FM_GC: 13,
          FM_GD: 14, FM_CQ: 6, FM_CK: 7}


def _consts():
    c = np.zeros((128, 128 * 4 + 512), np.float32)
    i = np.arange(128)
    c[:, 0:128] = np.eye(128, dtype=np.float32)
    c[:, 128:256] = (i[:, None] <= i[None, :]).astype(np.float32)
    c[:, 256:384] = ((i[:, None] <= i[None, :]) & ((i[:, None] // 64) == (i[None, :] // 64))).astype(np.float32)
    c[:, 384:512] = 1.0
    r = np.ones(512, np.float32)
    r[::64] = 0.0
    c[:, 512:1024] = r[None, :]
    return c


def _prep_inputs(inp):
    x = np.asarray(inp["x"], np.float32)
    p = np.asarray(inp["p"], np.float32)
    w_in = np.asarray(inp["w_in"], np.float32)
    w_out = np.asarray(inp["w_out"], np.float32)
    perm = np.array([g * 512 + 128 * hh + i for hh in range(4) for g in range(4) for i in range(128)])
    wout_p = np.ascontiguousarray(w_out[:, perm, :])
    wpg = np.ascontiguousarray(np.asarray(inp["w_pg"], np.float32))
    wpe = np.ascontiguousarray(np.asarray(inp["w_pe"], np.float32))
    cst = _consts()
    xTs = [np.ascontiguousarray(x[b].T) for b in range(2)]
    maps = []
    for c in range(NCORE):
        b, h = divmod(c, 4)
        sl = slice(128 * h, 128 * h + 128)
        cols = []
        for f in range(12):
            blk = FM_SRC[f]
            cols.append(np.arange(blk * 512 + 128 * h, blk * 512 + 128 * h + 128))
        cols.append(np.arange(5 * 512 + 128 * h, 5 * 512 + 128 * h + 128))
        cols.append(np.arange(8 * 512 + 128 * h, 8 * 512 + 128 * h + 128))
        for g in range(4):
            gg = (h + g) % 4
            cols.append(np.arange(10 * 512 + 128 * gg, 10 * 512 + 128 * gg + 128))
        cols = np.concatenate(cols)
        wc = np.ascontiguousarray(w_in[:, :, cols])
        pcol = np.zeros((DEPTH, 128, 8), np.float32)
        prow = np.zeros((DEPTH, 1, NROW), np.float32)
        sgwT = np.zeros((DEPTH, 128, 128), np.float32)
        for i in range(DEPTH):
            pcol[i, :, 0:3] = np.asarray(inp["conv_w"])[i][:, sl].T
            pcol[i, :, 3] = np.asarray(inp["hgrn_lb"])[0, sl]
            pcol[i, :, 4] = np.asarray(inp["hgrn_lb"])[1, sl]
            pcol[i, :, 5] = np.asarray(inp["hgrn_norm_g"])[i, sl]
            pcol[i, :, 6] = np.asarray(inp["diff_norm_g"])[i, sl]
            prow[i, 0, 0:256] = np.asarray(inp["diff_lambda"])[i].reshape(256)
            prow[i, 0, 256:384] = np.asarray(inp["sg_b"])[i, h]
            prow[i, 0, 384:512] = np.asarray(inp["sg_ln_g"])[i, sl]
            prow[i, 0, 512:640] = np.asarray(inp["sg_ln_b"])[i, sl]
            prow[i, 0, 640:640 + DM] = np.asarray(inp["ln_g"])[i]
            prow[i, 0, 640 + DM:640 + 2 * DM] = np.asarray(inp["ln_b"])[i]
            sgwT[i] = np.asarray(inp["sg_w"])[i, h].T
        maps.append({
            "xq": np.ascontiguousarray(x[b, TQ * h:TQ * h + TQ, :]),
            "xTf": xTs[b],
            "pT": np.ascontiguousarray(np.transpose(p[:, b, TQ * h:TQ * h + TQ, :], (0, 2, 1))),
            "wc": wc, "wout": wout_p, "wpg": wpg, "wpe": wpe,
            "pcol": pcol, "prow": prow, "sgwT": sgwT, "cst": cst,
        })
    return maps


class Buf:
    __slots__ = ("name", "w", "r", "excl")

    def __init__(self, name):
        self.name = name
        self.w = None
        self.r = {}
        self.excl = False


class T:
    __slots__ = ("ap", "buf")

    def __init__(self, ap, name="t", buf=None):
        self.ap = ap
        self.buf = buf if buf is not None else Buf(name)

    def __getitem__(self, k):
        return self.ap[k]


class Sched:
    ENGS = ("pe", "act", "dve", "pool", "sp")

    def __init__(self, esem, dsem):
        self.esem = esem
        self.prog = {e: [] for e in self.ENGS}
        self.cnt = {e: 0 for e in esem}
        self.waited = {e: {} for e in self.ENGS}
        self.dq = {q: {"sems": s, "use": [0] * len(s), "next": 0} for q, s in dsem.items()}

    def _need(self, eng, tok, raw):
        if tok is None:
            return
        sem, val, src = tok
        if src == eng:
            if eng == "pe" or not raw or not SAME_ENG_RAW:
                return
        w = self.waited[eng]
        k = id(sem)
        if w.get(k, 0) >= val:
            return
        w[k] = val
        self.prog[eng].append(("wait", sem, val))

    def _deps(self, eng, reads, writes):
        for t in reads:
            self._need(eng, t.buf.w, True)
            if t.buf.excl:
                for tok in t.buf.r.values():
                    self._need(eng, tok, False)
        for t in writes:
            b = t.buf
            self._need(eng, b.w, False)
            for tok in b.r.values():
                self._need(eng, tok, False)

    def _commit(self, tok, reads, writes):
        for t in reads:
            t.buf.r[id(tok[0])] = tok
        for t in writes:
            t.buf.w = tok
            t.buf.r = {}

    def op(self, eng, fn, reads=(), writes=()):
        self._deps(eng, reads, writes)
        self.cnt[eng] += 1
        sem = self.esem[eng]
        tok = (sem, self.cnt[eng], eng)
        self.prog[eng].append(("op", fn, sem, 1))
        self._commit(tok, reads, writes)

    def dma(self, q, fn, reads=(), writes=()):
        self._deps(q, reads, writes)
        d = self.dq[q]
        i = d["next"]
        d["next"] = (i + 1) % len(d["sems"])
        sem = d["sems"][i]
        k = d["use"][i]
        if k > 0:
            self._need(q, (sem, 16 * k, "dma"), True)
        d["use"][i] = k + 1
        tok = (sem, 16 * (k + 1), "dma")
        self.prog[q].append(("op", fn, sem, 16))
        self._commit(tok, reads, writes)

    def barrier_all(self):
        toks = [(self.esem[e], self.cnt[e], e) for e in self.esem if self.cnt[e] > 0]
        for q, d in self.dq.items():
            for sem, k in zip(d["sems"], d["use"]):
                if k > 0:
                    toks.append((sem, 16 * k, "dma"))
        for e in self.ENGS:
            for tok in toks:
                if tok[2] == e and e == "pe":
                    continue
                self._need(e, tok, True)

    def final_wait(self):
        for q, d in self.dq.items():
            for sem, k in zip(d["sems"], d["use"]):
                if k > 0:
                    self._need(q, (sem, 16 * k, "dma"), True)

    def emit(self, eng, e):
        for item in self.prog[eng]:
            if item[0] == "wait":
                e.wait_ge(item[1], item[2])
            else:
                ins = item[1](e)
                ins.then_inc(item[2], item[3])


class Arena:
    def __init__(self, ap, width):
        self.base = ap
        self.width = width
        self.off = 0

    def reset(self, off=0):
        self.off = off

    def f32(self, n, name="t", shape=None):
        n2 = (n + 7) // 8 * 8
        assert self.off + n2 <= self.width, f"arena overflow at {name}: {self.off}+{n2}>{self.width}"
        ap = self.base[:, self.off:self.off + n]
        self.off += n2
        if shape is not None:
            ap = ap.rearrange("p (a b) -> p a b", a=shape[0], b=shape[1])
        return T(ap, name)

    def bf16(self, n, name="t", shape=None):
        nw = (n + 1) // 2
        n2 = (nw + 7) // 8 * 8
        assert self.off + n2 <= self.width, f"arena overflow at {name}: {self.off}+{n2}>{self.width}"
        ap = self.base[:, self.off:self.off + nw].bitcast(BF16)
        self.off += n2
        if shape is not None:
            ap = ap.rearrange("p (a b) -> p a b", a=shape[0], b=shape[1])
        return T(ap, name)


def build_program(debug=None):
    nc = bass.Bass("TRN2", target_bir_lowering=False)
    dt_in = lambda n, s: nc.dram_tensor(n, s, F32, kind="ExternalInput")
    xq = dt_in("xq", [TQ, DM])
    xTf = dt_in("xTf", [DM, SEQ])
    pT = dt_in("pT", [DEPTH, 256, TQ])
    wc = dt_in("wc", [DEPTH, DM, NWC])
    wout = dt_in("wout", [DEPTH, DM, DM])
    wpg = dt_in("wpg", [DEPTH, DM, DM])
    wpe = dt_in("wpe", [DEPTH, 256, DM])
    pcol = dt_in("pcol", [DEPTH, 128, 8])
    prow = dt_in("prow", [DEPTH, 1, NROW])
    sgwT = dt_in("sgwT", [DEPTH, 128, 128])
    cst = dt_in("cst", [128, 1024])
    out = nc.dram_tensor("out", [TQ, DM], F32, kind="ExternalOutput")
    dbg = None
    if debug == "p1":
        dbg = nc.dram_tensor("dbg", [NBLK, 512, 512], BF16, kind="ExternalOutput")
    if debug == "p0":
        dbg = nc.dram_tensor("dbg", [4, 4 * DM, 256], BF16, kind="ExternalOutput")
    xT_send = [nc.dram_tensor(f"xT_send{i}", [4, DM, 256], BF16) for i in range(DEPTH)]
    xT_all = [nc.dram_tensor(f"xT_all{i}", [4, 4 * DM, 256], BF16) for i in range(DEPTH)]
    mix_send = [nc.dram_tensor(f"mix_send{i}", [NBLK, 512, 512], BF16) for i in range(DEPTH)]
    mix_all = [nc.dram_tensor(f"mix_all{i}", [NBLK, 4 * 512, 512], BF16) for i in range(DEPTH)]
    xcur = nc.dram_tensor("xcur", [TQ, DM], F32)
    rscr = nc.dram_tensor("rscr", [TQ, DM], F32)
    xps = nc.dram_tensor("xps", [TQ, DM], F32)
    GROUPS = [[0, 1, 2, 3], [4, 5, 6, 7]]

    with contextlib.ExitStack() as es:
        esem = {e: es.enter_context(nc.semaphore(f"s_{e}")) for e in ("pe", "act", "dve", "pool")}
        dsem = {"sp": [es.enter_context(nc.semaphore(f"d_sp{i}")) for i in range(12)],
                "pool": [es.enter_context(nc.semaphore(f"d_pl{i}")) for i in range(6)]}
        ccsem = [es.enter_context(nc.semaphore(f"cc{i}")) for i in range(12 * DEPTH)]
        arena_t = es.enter_context(nc.sbuf_tensor("arena", [128, AW], F32))
        banks = [T(es.enter_context(nc.psum_tensor(f"bank{i}", [128, 512], F32))[:, :], f"bank{i}") for i in range(8)]
        for bk_ in banks:
            bk_.buf.excl = True
        S = Sched(esem, dsem)
        A = Arena(arena_t[:, :], AW)

        d_xT_send = [[T(None, "xTs") for _ in range(4)] for _ in range(DEPTH)]
        d_xT_all = [[T(None, "xTa") for _ in range(4)] for _ in range(DEPTH)]
        d_mix_send = [[T(None, "mxs") for _ in range(NBLK)] for _ in range(DEPTH)]
        d_mix_all = [[T(None, "mxa") for _ in range(NBLK)] for _ in range(DEPTH)]
        d_xcur = [T(None, "xcur") for _ in range(8)]
        d_rscr = [T(None, "rscr") for _ in range(8)]
        d_xps = [T(None, "xps") for _ in range(8)]
        d_out = T(None, "out")
        cc_count = [0]

        def mm(o, l, r, start, stop, reads, writes):
            S.op("pe", lambda e, o=o, l=l, r=r, st=start, sp=stop: e.matmul(o, l, r, start=st, stop=sp), reads, writes)

        def tr(o, i, ident, reads, writes):
            S.op("pe", lambda e, o=o, i=i, d=ident: e.transpose(o, i, d), reads, writes)

        def act(o, i, func, reads, writes, scale=None, bias=None):
            kw = {}
            if scale is not None:
                kw["scale"] = scale
            if bias is not None:
                kw["bias"] = bias
            S.op("act", lambda e, o=o, i=i, f=func, kw=kw: e.activation(o, i, f, **kw), reads, writes)

        def tt(eng, o, a, b, op, reads, writes):
            S.op(eng, lambda e, o=o, a=a, b=b, op=op: e.tensor_tensor(o, a, b, op), reads, writes)

        def ts(eng, o, a, s1, s2, op0, op1, reads, writes):
            if op1 is None:
                S.op(eng, lambda e, o=o, a=a, s1=s1, op0=op0: e.tensor_scalar(o, a, s1, None, op0), reads, writes)
            else:
                S.op(eng, lambda e, o=o, a=a, s1=s1, s2=s2, op0=op0, op1=op1: e.tensor_scalar(o, a, s1, s2, op0, op1), reads, writes)

        def stt(o, a, s, b, op0, op1, reads, writes):
            S.op("dve", lambda e, o=o, a=a, s=s, b=b, op0=op0, op1=op1: e.scalar_tensor_tensor(o, a, s, b, op0, op1), reads, writes)

        def cp(eng, o, i, reads, writes):
            if eng == "act":
                act(o, i, AF.Copy, reads, writes)
            else:
                S.op(eng, lambda e, o=o, i=i: e.tensor_copy(o, i), reads, writes)

        def dma(q, o, i, reads, writes):
            S.dma(q, lambda e, o=o, i=i: e.dma_start(out=o, in_=i), reads, writes)

        def allgather(src, dst, r, w):
            k = cc_count[0]
            cc_count[0] += 1
            sem = ccsem[k]
            S._deps("pool", r, w)
            S.prog["pool"].append(("op", lambda e, s=src, d=dst: e.collective_compute(
                "AllGather", ALU.bypass, replica_groups=GROUPS, ins=[s.opt()], outs=[d.opt()], dma_qos=CC_QOS), sem, 1))
            tok = (sem, 1, "cc")
            S._commit(tok, r, w)

        c_f32 = A.f32(1024, "cst")
        ident_b = A.bf16(128, "ident")
        tri_b = A.bf16(128, "tri")
        hmask_b = A.bf16(128, "hmask")
        ones_b = A.bf16(128, "ones_b")
        dma("sp", c_f32[:, :], cst[:, :], [], [c_f32])
        cp("dve", ident_b[:, :], c_f32[:, 0:128], [c_f32], [ident_b])
        cp("dve", tri_b[:, :], c_f32[:, 128:256], [c_f32], [tri_b])
        cp("dve", hmask_b[:, :], c_f32[:, 256:384], [c_f32], [hmask_b])
        cp("dve", ones_b[:, :], c_f32[:, 384:512], [c_f32], [ones_b])
        tri_f = c_f32[:, 128:256]
        ones_f = c_f32[:, 384:512]
        rmask_f = c_f32[:, 512:1024]
        A_BASE = A.off

        def transposes_to(xb, dstT, col0, nchunk, kc0, bank_pair, flip, dep=None):
            for half in range((nchunk + 7) // 8):
                bk = banks[bank_pair[half % 2]]
                bkb = bk.ap.bitcast(BF16)
                n_here = min(8, nchunk - 8 * half)
                for j in range(n_here):
                    jj = 8 * half + j
                    tr(bkb[:, j * 128:(j + 1) * 128], xb[:, jj * 128:(jj + 1) * 128], ident_b[:, :], [xb, ident_b], [bk])
                eng = "act" if (half + flip) % 2 == 0 else "dve"
                o = dstT[:, kc0 + 8 * half:kc0 + 8 * half + n_here, col0:col0 + 128]
                i = bkb[:, 0:n_here * 128].rearrange("p (a b) -> p a b", a=n_here, b=128)
                cp(eng, o, i, [bk], [dstT if dep is None else dep])

        deferred_ag = []
        pending_ag = []

        def send_xT(L, q, xT, dep):
            dma("pool", xT_send[L].ap()[q].rearrange("(kc p) t -> p kc t", p=128), xT[:, :, 256 * q:256 * q + 256],
                [dep], [d_xT_send[L][q]])
            ag = lambda L=L, q=q: allgather(xT_send[L].ap()[q], xT_all[L].ap()[q], [d_xT_send[L][q]], [d_xT_all[L][q]])
            if debug == "p0":
                ag()
            elif q < 2:
                pending_ag.append(ag)
            else:
                deferred_ag.append(ag)

        def phase0():
            A.reset(A_BASE)
            xT = A.bf16(16 * TQ, "xT", (16, TQ))
            xbs = [A.bf16(DM, f"xb{i}") for i in range(2)]
            xTc = [T(None, f"xTc{q}") for q in range(4)]
            for i in range(8):
                xb = xbs[i % 2]
                q = i // 2
                dma("pool", xb[:, :], xq[128 * i:128 * i + 128, :], [], [xb])
                transposes_to(xb, xT, 128 * i, 16, 0, (0, 1) if i % 2 == 0 else (2, 3), i, dep=xTc[q])
                if i % 2 == 1:
                    send_xT(0, q, xT, xTc[q])
            S.barrier_all()

        def phase1(L):
            A.reset(A_BASE)
            lam_init = 0.8 - 0.6 * float(np.exp(-0.3 * L))
            WC_PIECES = [(0, 512), (512, 1024), (1024, 1536), (1536, 1792), (1792, 2304)]
            Wc = [A.bf16(16 * (b - a), f"Wc{a}", (16, b - a)) for a, b in WC_PIECES]
            wcv = wc.ap()[L].rearrange("(kc p) n -> p kc n", p=128)
            def load_wc(pi):
                a, b = WC_PIECES[pi]
                dma("pool", Wc[pi][:, :, :], wcv[:, :, a:b], [], [Wc[pi]])

            def wslice(col0, ncol):
                for pi, (a, b) in enumerate(WC_PIECES):
                    if a <= col0 and col0 + ncol <= b:
                        return Wc[pi], (lambda kc, pi=pi, a=a: Wc[pi][:, kc, col0 - a:col0 - a + ncol])
                raise AssertionError("w slice crosses pieces")

            KT = A.bf16(SEQ, "KT")
            KTb = [T(KT.ap, f"KT{t}") for t in range(NBLK)]
            Vt = A.bf16(32 * 128, "V", (32, 128))
            Vb = [T(Vt.ap, f"V{t}") for t in range(NBLK)]
            xTb = [A.bf16(16 * 512, f"xTb{i}", (16, 512)) for i in range(2)]
            mixb = A.bf16(4 * 512, "mixb", (4, 512))
            Sst = A.f32(128, "Sst")
            pc = A.f32(8, "pc")
            lamrow = A.f32(256, "lamrow")
            lamtmp = A.f32(128, "lamtmp")
            sc = A.f32(16, "sc")
            WTm = A.bf16(128, "WTm")
            sgw_f = A.f32(128, "sgw_f")
            bsb = A.f32(128, "bsb")
            lng = A.f32(128, "lng")
            lnb = A.f32(128, "lnb")
            QT = A.bf16(512, "QT")
            qt = A.bf16(512, "qt")
            kt = A.bf16(512, "kt")
            kdT = A.bf16(512, "kdT")
            kd_tm = A.bf16(512, "kd_tm", (4, 128))
            iv_tm = A.bf16(512, "iv_tm", (4, 128))
            am = A.bf16(512, "am", (4, 128))
            Ssc = A.bf16(1024, "Ssc", (8, 128))
            vn = [A.bf16(128, f"vn{i}") for i in range(4)]
            dv2s = A.f32(128, "dv2s")
            P1 = [A.bf16(512, f"P1_{i}") for i in range(2)]
            P2 = [A.bf16(512, f"P2_{i}") for i in range(2)]
            ebm = A.f32(8, "ebm")
            ebl = A.f32(8, "ebl")
            st6 = A.f32(8, "st6")
            mv = A.f32(8, "mv")
            zc = A.f32(516, "zc")
            W = {n: A.f32(512, n) for n in (
                "q_sb", "th", "f", "km", "bT", "E", "ek", "D2",
                "ab", "ac", "acc", "du", "u1", "u2", "dv1",
                "sgA", "sgB", "sgC", "sgD", "thg", "dvs0", "dvs1", "dvs2", "dvs3")}
            dvs = [W[f"dvs{i}"] for i in range(4)]

            dma("sp", pc[:, :], pcol[L, :, :], [], [pc])
            dma("sp", lamrow[:, :], prow[L, 0:1, 0:256].to_broadcast([128, 256]), [], [lamrow])
            dma("sp", sgw_f[:, :], sgwT[L, :, :], [], [sgw_f])
            dma("sp", bsb[:, :], prow[L, 0:1, 256:384].to_broadcast([128, 128]), [], [bsb])
            dma("sp", lng[:, :], prow[L, 0:1, 384:512].to_broadcast([128, 128]), [], [lng])
            dma("sp", lnb[:, :], prow[L, 0:1, 512:640].to_broadcast([128, 128]), [], [lnb])
            if L == 0:
                S.op("dve", lambda e: e.memset(sc[:, 0:1], 0.0), [], [sc])
            else:
                tt("dve", sc[:, 0:1], pc[:, 4:5], pc[:, 3:4], ALU.subtract, [pc], [sc])
                act(sc[:, 0:1], sc[:, 0:1], AF.Tanh, [sc], [sc], scale=0.5)
                ts("dve", sc[:, 0:1], sc[:, 0:1], 0.5, 0.5, ALU.mult, ALU.add, [sc], [sc])
            ts("dve", sc[:, 1:2], sc[:, 0:1], -0.5, 0.5, ALU.mult, ALU.add, [sc], [sc])
            ts("dve", sc[:, 2:3], sc[:, 0:1], 0.5, 0.5, ALU.mult, ALU.add, [sc], [sc])
            ts("dve", sc[:, 3:4], sc[:, 0:1], 0.5, -0.5, ALU.mult, ALU.add, [sc], [sc])
            tt("dve", lamtmp[:, 0:64], lamrow[:, 0:64], lamrow[:, 64:128], ALU.mult, [lamrow], [lamtmp])
            tt("dve", lamtmp[:, 64:128], lamrow[:, 128:192], lamrow[:, 192:256], ALU.mult, [lamrow], [lamtmp])
            S.op("dve", lambda e: e.tensor_reduce(sc[:, 6:8], lamtmp[:, :].rearrange("p (a b) -> p a b", a=2, b=64),
                                                  mybir.AxisListType.X, ALU.add), [lamtmp], [sc])
            act(sc[:, 8:10], sc[:, 6:8], AF.Exp, [sc], [sc])
            tt("dve", sc[:, 4:5], sc[:, 9:10], sc[:, 8:9], ALU.subtract, [sc], [sc])
            ts("dve", sc[:, 4:5], sc[:, 4:5], -lam_init, None, ALU.add, None, [sc], [sc])
            ts("dve", sc[:, 5:6], pc[:, 6:7], 1.0 - lam_init, None, ALU.mult, None, [pc], [sc])
            tt("dve", WTm[:, :], sgw_f[:, :], tri_f, ALU.mult, [sgw_f, c_f32], [WTm])
            S.op("dve", lambda e: e.memset(Sst[:, :], 0.0), [], [Sst])
            S.op("dve", lambda e: e.memset(zc[:, 0:2], 0.0), [], [zc])

            pbank = [0]

            def next_pbank():
                b = banks[pbank[0] % 2]
                pbank[0] += 1
                return b

            def load_xT(t):
                r, half = divmod(t, 2)
                if L == 0:
                    src = xTf.ap()[:, 512 * t:512 * t + 512].rearrange("(kc p) t -> p kc t", p=128)
                    dma("pool", xTb[t % 2][:, :, :], src, [], [xTb[t % 2]])
                    return
                for qq in range(2):
                    q = 2 * half + qq
                    src = xT_all[L].ap()[q][r * DM:(r + 1) * DM, :].rearrange("(kc p) t -> p kc t", p=128)
                    dma("sp", xTb[t % 2][:, :, 256 * qq:256 * qq + 256], src, [d_xT_all[L][q]], [xTb[t % 2]])


            def make_block(t):
                xt = xTb[t % 2]
                mx = mixb
                c0 = 512 * t
                tiles, st3 = [], []
                ch_h, ch_c, ch_u, ch_v = [], [], [], []

                def proj_fm(f):
                    bk = next_pbank()
                    wT, wfn = wslice(128 * f, 128)
                    for kc in range(16):
                        mm(bk[:, :], wfn(kc), xt[:, kc, :], kc == 0, kc == 15, [wT, xt], [bk])
                    return bk

                def proj_tm(tt_i, col0, ncol):
                    bk = next_pbank()
                    wT, wfn = wslice(col0, ncol)
                    for kc in range(16):
                        mm(bk[:, 0:ncol], xt[:, kc, 128 * tt_i:128 * tt_i + 128], wfn(kc), kc == 0, kc == 15, [wT, xt], [bk])
                    return bk

                def t_copy(f, dst):
                    def fn():
                        bk = proj_fm(f)
                        cp("act", dst[:, :], bk[:, :], [bk], [dst])
                    return fn

                def t_bf():
                    bk = proj_fm(FM_BF)
                    act(W["th"][:, :], bk[:, :], AF.Tanh, [bk], [W["th"]], scale=0.5)

                def t_ax():
                    bk = proj_fm(FM_AX)
                    tt("dve", zc[:, 2:514], bk[:, :], W["ac"][:, :], ALU.mult, [bk, W["ac"]], [zc])

                def t_gate(f, dst):
                    def fn():
                        bk = proj_fm(f)
                        act(W["thg"][:, :], bk[:, :], AF.Tanh, [bk], [W["thg"]], scale=0.5)
                        stt(dst[:, :], W["thg"][:, :], 1.0, bk[:, :], ALU.add, ALU.mult, [W["thg"], bk], [dst])
                    return fn

                def t_dv(i):
                    def fn():
                        bk = proj_tm(i, 1792, 512)
                        cp("act", dvs[i][:, :], bk[:, :], [bk], [dvs[i]])
                    return fn

                def t_bicv(i):
                    def fn():
                        bk = proj_tm(i, 1536, 256)
                        cp("act", iv_tm[:, i, :], bk[:, 0:128], [bk], [iv_tm])
                        cp("dve", Vt[:, 4 * t + i, :], bk[:, 128:256], [bk], [Vb[t]])
                    return fn

                def t_ck():
                    bk = proj_fm(FM_CK)
                    cp("act", KT[:, c0:c0 + 512], bk[:, :], [bk], [KTb[t]])

                def t_cq():
                    bk = proj_fm(FM_CQ)
                    cp("dve", QT[:, :], bk[:, :], [bk], [QT])

                tiles += [t_copy(FM_BQ, W["q_sb"]), t_bf, t_copy(FM_AB, W["ab"]), t_copy(FM_AC, W["ac"]), t_ax,
                          t_copy(FM_DU, W["du"])]
                tiles += [t_dv(i) for i in range(4)]
                tiles += [t_gate(FM_GA, W["sgA"]), t_gate(FM_GB, W["sgB"]), t_gate(FM_GD, W["sgD"]), t_gate(FM_GC, W["sgC"])]
                tiles += [t_bicv(i) for i in range(4)]
                tiles += [t_ck, t_cq]

                th, f_, km, bT, E, ek, D2 = (W[n] for n in ("th", "f", "km", "bT", "E", "ek", "D2"))
                bT3 = bT[:, :].rearrange("p (a b) -> p a b", a=8, b=64)
                E3 = E[:, :].rearrange("p (a b) -> p a b", a=8, b=64)
                D23 = D2[:, :].rearrange("p (a b) -> p a b", a=8, b=64)
                H = ch_h.append
                H(lambda: ts("dve", f_[:, :], th[:, :], sc[:, 1:2], sc[:, 2:3], ALU.mult, ALU.add, [th, sc], [f_]))
                H(lambda: ts("dve", km[:, :], th[:, :], sc[:, 3:4], sc[:, 1:2], ALU.mult, ALU.add, [th, sc], [km]))
                H(lambda: S.op("dve", lambda e: e.tensor_scalar_max(f_[:, :], f_[:, :], F_FLOOR), [f_], [f_]))
                H(lambda: act(f_[:, :], f_[:, :], AF.Ln, [f_], [f_]))
                H(lambda: S.op("dve", lambda e: e.tensor_tensor_scan(bT[:, :], rmask_f, f_[:, :], 0.0, ALU.mult, ALU.add),
                               [f_, c_f32], [bT]))
                H(lambda: tt("dve", E3, bT3, bT3[:, :, 31:32].to_broadcast([128, 8, 64]), ALU.subtract, [bT], [E]))
                H(lambda: tt("dve", D23, bT3[:, :, 63:64].to_broadcast([128, 8, 64]), bT3, ALU.subtract, [bT], [D2]))
                H(lambda: act(ek[:, :], E[:, :], AF.Exp, [E], [ek], scale=-1.0))
                H(lambda: act(E[:, :], E[:, :], AF.Exp, [E], [E]))
                H(lambda: act(D2[:, :], D2[:, :], AF.Exp, [D2], [D2]))
                H(lambda: act(ebm[:, :], bT3[:, :, 31], AF.Exp, [bT], [ebm]))
                H(lambda: act(ebl[:, :], bT3[:, :, 63], AF.Exp, [bT], [ebl]))
                H(lambda: tt("dve", qt[:, :], W["q_sb"][:, :], E[:, :], ALU.mult, [W["q_sb"], E], [qt]))
                H(lambda: tt("dve", kt[:, :], km[:, :], ek[:, :], ALU.mult, [km, ek], [kt]))
                H(lambda: tt("dve", kdT[:, :], km[:, :], D2[:, :], ALU.mult, [km, D2], [kdT]))

                acc = W["acc"]
                C = ch_c.append
                C(lambda: ts("dve", acc[:, :], zc[:, 0:512], pc[:, 0:1], None, ALU.mult, None, [zc, pc], [acc]))
                C(lambda: stt(acc[:, :], zc[:, 1:513], pc[:, 1:2], acc[:, :], ALU.mult, ALU.add, [zc, pc, acc], [acc]))
                C(lambda: stt(acc[:, :], zc[:, 2:514], pc[:, 2:3], acc[:, :], ALU.mult, ALU.add, [zc, pc, acc], [acc]))
                C(lambda: tt("dve", acc[:, :], acc[:, :], W["ab"][:, :], ALU.mult, [acc, W["ab"]], [acc]))
                C(lambda: stt(mx[:, 0, :], acc[:, :], 0.5, W["sgA"][:, :], ALU.mult, ALU.mult, [acc, W["sgA"]], [mx]))
                C(lambda: S.op("dve", lambda e: e.tensor_copy(zc[:, 0:2], zc[:, 512:514]), [zc], [zc]))

                du, u1, u2 = W["du"], W["u1"], W["u2"]
                U = ch_u.append
                U(lambda: act(u1[:, :], du[:, :], AF.Square, [du], [u1]))
                U(lambda: ts("dve", u1[:, :], u1[:, :], 0.044715, 1.0, ALU.mult, ALU.add, [u1], [u1]))
                U(lambda: tt("dve", u1[:, :], u1[:, :], du[:, :], ALU.mult, [u1, du], [u1]))
                U(lambda: act(u1[:, :], u1[:, :], AF.Tanh, [u1], [u1], scale=GELU_C))
                U(lambda: stt(u2[:, :], u1[:, :], 1.0, du[:, :], ALU.add, ALU.mult, [u1, du], [u2]))

                dv1, dv2 = W["dv1"], dv2s
                V = ch_v.append
                for i in range(4):
                    d_ = dvs[i]
                    vnt = vn[i]
                    V(lambda d_=d_: act(dv1[:, :], d_[:, :], AF.Square, [d_], [dv1]))
                    V(lambda: ts("dve", dv1[:, :], dv1[:, :], 0.044715, 1.0, ALU.mult, ALU.add, [dv1], [dv1]))
                    V(lambda d_=d_: tt("dve", dv1[:, :], dv1[:, :], d_[:, :], ALU.mult, [dv1, d_], [dv1]))
                    V(lambda: act(dv1[:, :], dv1[:, :], AF.Tanh, [dv1], [dv1], scale=GELU_C))
                    V(lambda d_=d_: stt(d_[:, :], dv1[:, :], 1.0, d_[:, :], ALU.add, ALU.mult, [dv1, d_], [d_]))
                    V(lambda d_=d_: S.op("dve", lambda e: e.bn_stats(st6[:, 0:6], d_[:, :]), [d_], [st6]))
                    V(lambda: S.op("dve", lambda e: e.bn_aggr(mv[:, 0:2], st6[:, 0:6]), [st6], [mv]))
                    V(lambda: act(mv[:, 2:3], mv[:, 1:2], AF.Ln, [mv], [mv], scale=0.25, bias=LN_EPS))
                    V(lambda: act(mv[:, 2:3], mv[:, 2:3], AF.Exp, [mv], [mv], scale=-0.5))
                    V(lambda: ts("dve", mv[:, 3:4], mv[:, 2:3], 0.5, None, ALU.mult, None, [mv], [mv]))
                    V(lambda: stt(mv[:, 4:5], mv[:, 0:1], -1.0, mv[:, 3:4], ALU.mult, ALU.mult, [mv], [mv]))
                    V(lambda d_=d_: ts("dve", dv2[:, :], d_[:, 0:128], mv[:, 3:4], mv[:, 4:5], ALU.mult, ALU.add, [d_, mv], [dv2]))
                    V(lambda: tt("dve", dv2[:, :], dv2[:, :], lng[:, :], ALU.mult, [dv2, lng], [dv2]))
                    V(lambda vnt=vnt: tt("dve", vnt[:, :], dv2[:, :], lnb[:, :], ALU.add, [dv2, lnb], [vnt]))

                chain = []
                srcs = [ch_h, ch_v, ch_c, ch_u]
                while any(srcs):
                    for s_ in srcs:
                        if s_:
                            chain.append(s_.pop(0))
                    if ch_v:
                        chain.append(ch_v.pop(0))

                bS1, bS2, bO1, bO2, bZ1, bZ2 = banks[2], banks[3], banks[4], banks[5], banks[6], banks[7]
                nkb = 4 * (t + 1)

                def s_mm(kb):
                    d = kb - 4 * t
                    q0 = 128 * d if d > 0 else 0
                    kr = [KTb[kb // 4], QT]
                    mm(bS1[:, q0:512], KT[0:64, 128 * kb:128 * kb + 128], QT[0:64, q0:512], True, True, kr, [bS1])
                    mm(bS2[:, q0:512], KT[64:128, 128 * kb:128 * kb + 128], QT[64:128, q0:512], True, True, kr, [bS2])

                def exp_pv(kb):
                    d = kb - 4 * t
                    q0 = 128 * d if d > 0 else 0
                    p1, p2 = P1[kb % 2], P2[kb % 2]
                    act(p1[:, q0:512], bS1[:, q0:512], AF.Exp, [bS1], [p1], scale=0.125)
                    act(p2[:, q0:512], bS2[:, q0:512], AF.Exp, [bS2], [p2], scale=0.125)
                    if d >= 0:
                        tt("dve", p1[:, q0:q0 + 128], p1[:, q0:q0 + 128], tri_b[:, :], ALU.mult, [p1, tri_b], [p1])
                        tt("dve", p2[:, q0:q0 + 128], p2[:, q0:q0 + 128], tri_b[:, :], ALU.mult, [p2, tri_b], [p2])
                    return (kb, q0, p1, p2)

                def pv_mm(info):
                    kb, q0, p1, p2 = info
                    first = kb == 0
                    last = kb == nkb - 1
                    vb = Vb[kb // 4]
                    mm(bO1[:, q0:512], Vt[:, kb, :], p1[:, q0:512], first, last, [vb, p1], [bO1])
                    mm(bZ1[:, q0:512], ones_b[:, :], p1[:, q0:512], first, last, [ones_b, p1], [bZ1])
                    mm(bO2[:, q0:512], Vt[:, kb, :], p2[:, q0:512], first, last, [vb, p2], [bO2])
                    mm(bZ2[:, q0:512], ones_b[:, :], p2[:, q0:512], first, last, [ones_b, p2], [bZ2])

                def stage2():
                    s_mm(0)
                    for kb in range(nkb):
                        info = exp_pv(kb)
                        if kb + 1 < nkb:
                            s_mm(kb + 1)
                        pv_mm(info)
                        k = -(-len(chain) // (nkb - kb))
                        for _ in range(k):
                            chain.pop(0)()
                    while chain:
                        chain.pop(0)()

                r1, r2, oa, ob = W["E"], W["ek"], W["D2"], W["bT"]
                bTr, bSa, bD0, bD1, bOh, bSV = banks[2], banks[3], banks[4], banks[5], banks[6], banks[7]
                bTrb = bTr.ap.bitcast(BF16)

                def a1():
                    S.op("dve", lambda e: e.reciprocal(r1[:, :], bZ1[:, :]), [bZ1], [r1])
                    S.op("dve", lambda e: e.reciprocal(r2[:, :], bZ2[:, :]), [bZ2], [r2])
                    tt("dve", oa[:, :], bO1[:, :], r1[:, :], ALU.mult, [bO1, r1], [oa])
                    tt("dve", ob[:, :], bO2[:, :], r2[:, :], ALU.mult, [bO2, r2], [ob])
                    stt(oa[:, :], ob[:, :], sc[:, 4:5], oa[:, :], ALU.mult, ALU.add, [ob, sc, oa], [oa])
                    act(ob[:, :], oa[:, :], AF.Square, [oa], [ob])

                def h1():
                    for i in range(4):
                        tr(bTrb[:, 128 * i:128 * i + 128], kdT[:, 128 * i:128 * i + 128], ident_b[:, :], [kdT, ident_b], [bTr])
                    cp("act", kd_tm[:, :, :], bTrb[:, 0:512].rearrange("p (a b) -> p a b", a=4, b=128), [bTr], [kd_tm])

                def a2():
                    mm(bSa[:, :], ones_f, ob[:, :], True, True, [c_f32, ob], [bSa])
                    act(ob[:, :], bSa[:, :], AF.Ln, [bSa], [ob], scale=1.0 / 128.0, bias=RMS_EPS)
                    act(ob[:, :], ob[:, :], AF.Exp, [ob], [ob], scale=-0.5)
                    stt(oa[:, :], oa[:, :], sc[:, 5:6], ob[:, :], ALU.mult, ALU.mult, [oa, sc, ob], [oa])
                    stt(mx[:, 2, :], oa[:, :], 0.5, W["sgC"][:, :], ALU.mult, ALU.mult, [oa, W["sgC"]], [mx])

                def h2():
                    for c in range(8):
                        bd = bD0 if c % 2 == 0 else bD1
                        r0 = 64 * (c % 2)
                        mm(bd[:, 128 * (c // 2):128 * (c // 2) + 128], kd_tm[r0:r0 + 64, c // 2, :], iv_tm[r0:r0 + 64, c // 2, :],
                           True, True, [kd_tm, iv_tm], [bd])
                    for i in range(4):
                        mm(bTr[:, 128 * i:128 * i + 128], kt[:, 128 * i:128 * i + 128], qt[:, 128 * i:128 * i + 128],
                           True, True, [kt, qt], [bTr])
                    tt("dve", am[:, :, :], bTr[:, :].rearrange("p (a b) -> p a b", a=4, b=128),
                       hmask_b[:, :].unsqueeze(1).to_broadcast([128, 4, 128]), ALU.mult, [bTr, hmask_b], [am])
                    for c in range(8):
                        bd = bD0 if c % 2 == 0 else bD1
                        ts("dve", Ssc[:, c, :], Sst[:, :], ebm[:, c:c + 1], None, ALU.mult, None, [Sst, ebm], [Ssc])
                        stt(Sst[:, :], Sst[:, :], ebl[:, c:c + 1], bd[:, 128 * (c // 2):128 * (c // 2) + 128], ALU.mult, ALU.add,
                            [Sst, ebl, bd], [Sst])

                def v3():
                    for i in range(4):
                        mm(bSV[:, 128 * i:128 * i + 128], vn[i][:, :], WTm[:, :], i == 0, i == 3, [vn[i], WTm], [bSV])
                    o1 = W["u1"]
                    tt("dve", o1[:, :].rearrange("p (a b) -> p a b", a=4, b=128),
                       bSV[:, :].rearrange("p (a b) -> p a b", a=4, b=128),
                       bsb[:, :].unsqueeze(1).to_broadcast([128, 4, 128]), ALU.add, [bSV, bsb], [o1])
                    tt("dve", o1[:, :], o1[:, :], W["u2"][:, :], ALU.mult, [o1, W["u2"]], [o1])
                    stt(mx[:, 3, :], o1[:, :], 0.25, W["sgD"][:, :], ALU.mult, ALU.mult, [o1, W["sgD"]], [mx])

                def h3():
                    for i in range(4):
                        mm(bOh[:, 128 * i:128 * i + 128], iv_tm[:, i, :], am[:, i, :], i == 0, False, [iv_tm, am], [bOh])
                    for c in range(8):
                        mm(bOh[:, 64 * c:64 * c + 64], Ssc[:, c, :], qt[:, 64 * c:64 * c + 64], False, c == 7, [Ssc, qt], [bOh])
                    act(W["f"][:, :], bOh[:, :], AF.Square, [bOh], [W["f"]])

                def h4():
                    fq = W["f"]
                    mm(bSa[:, :], ones_f, fq[:, :], True, True, [c_f32, fq], [bSa])
                    act(fq[:, :], bSa[:, :], AF.Ln, [bSa], [fq], scale=1.0 / 128.0, bias=RMS_EPS)
                    act(fq[:, :], fq[:, :], AF.Exp, [fq], [fq], scale=-0.5)
                    stt(fq[:, :], bOh[:, :], pc[:, 5:6], fq[:, :], ALU.mult, ALU.mult, [bOh, pc, fq], [fq])
                    stt(mx[:, 1, :], fq[:, :], 0.5, W["sgB"][:, :], ALU.mult, ALU.mult, [fq, W["sgB"]], [mx])

                def fin():
                    dst = mix_send[L].ap()[t].rearrange("(g p) t -> p g t", p=128)
                    dma("sp", dst, mx[:, :, :], [mx], [d_mix_send[L][t]])
                    allgather(mix_send[L].ap()[t], mix_all[L].ap()[t], [d_mix_send[L][t]], [d_mix_all[L][t]])

                st3 += [a1, h1, a2, h2, v3, h3, h4, fin]
                return tiles, stage2, st3

            load_xT(0)
            for pi in (0, 1, 4, 2, 3):
                load_wc(pi)
            while deferred_ag:
                deferred_ag.pop(0)()
            pend3 = []
            for t in range(NBLK):
                if t + 1 < NBLK:
                    load_xT(t + 1)
                tiles, stage2, st3 = make_block(t)
                for k, tile in enumerate(tiles):
                    tile()
                    if pend3:
                        pend3.pop(0)()
                while pend3:
                    pend3.pop(0)()
                stage2()
                pend3 = st3
            while pend3:
                pend3.pop(0)()

            S.barrier_all()

        def phase2(L, rank_q):
            A.reset(A_BASE)
            last = L == DEPTH - 1
            x_src = xq if L == 0 else xcur
            d_xsrc = None if L == 0 else d_xcur
            x_dst = out if last else xcur
            mixT = A.bf16(16 * TQ, "mixT", (16, TQ))
            xpT = A.bf16(16 * TQ, "xpT", (16, TQ))
            wpc = [A.bf16(16 * 512, f"wpc{i}", (16, 512)) for i in range(2)]
            wpe_b = A.bf16(2 * DM, "wpe", (2, DM))
            pTb = A.bf16(2 * TQ, "pTb", (2, TQ))
            lng = A.f32(DM, "lng")
            lnb = A.f32(DM, "lnb")
            xpc = [A.f32(512, f"xpc{i}") for i in range(4)]
            rp = [A.f32(512, f"rp{i}") for i in range(2)]
            rt = [A.f32(DM, f"rt{i}") for i in range(4)]
            xb = [A.bf16(DM, f"xb{i}") for i in range(2)]
            stq = [A.f32(32, f"st{i}") for i in range(2)]
            mvq = [A.f32(8, f"mv{i}") for i in range(2)]
            xnb = [A.bf16(512, f"xnb{i}") for i in range(2)]
            thg = [A.f32(512, f"thg{i}") for i in range(2)]
            st = A.f32(32, "st")
            mv = A.f32(8, "mv")
            xnT = mixT
            xnTc = [T(None, f"xnTc{q}") for q in range(4)]

            mav = mix_all[L].ap()
            wov = wout.ap()[L].rearrange("(kc p) n -> p kc n", p=128)
            wgv = wpg.ap()[L].rearrange("(kc p) n -> p kc n", p=128)
            pieces = [(wov, n) for n in range(4)] + [(wgv, n) for n in range(4)]

            def load_piece(k):
                v, n = pieces[k]
                dma("pool", wpc[k % 2][:, :, :], v[:, :, 512 * n:512 * n + 512], [], [wpc[k % 2]])

            load_piece(0)
            load_piece(1)
            dma("pool", wpe_b[:, :, :], wpe.ap()[L].rearrange("(kc p) n -> p kc n", p=128), [], [wpe_b])
            dma("pool", pTb[:, :, :], pT.ap()[L].rearrange("(kc p) t -> p kc t", p=128), [], [pTb])

            def mk_mix_load(half):
                def fn(e):
                    rq = RANK["q"]
                    src = mav[bass.ds(rq * 2 + half, 1), :, :].rearrange("o (c p) t -> p (o c) t", p=128)
                    return e.dma_start(out=mixT[:, :, 512 * half:512 * half + 512], in_=src)
                return fn
            mixTh = [T(mixT.ap, "mixT0"), T(mixT.ap, "mixT1")]
            S.dma("pool", mk_mix_load(0), d_mix_all[L][0:7], [mixTh[0]])
            S.dma("pool", mk_mix_load(1), d_mix_all[L], [mixTh[1]])
            dma("sp", lng[:, :], prow[L, 0:1, 640:640 + DM].to_broadcast([128, DM]), [], [lng])
            dma("sp", lnb[:, :], prow[L, 0:1, 640 + DM:640 + 2 * DM].to_broadcast([128, DM]), [], [lnb])
            cnt = 0
            itsA = [(n_, i_) for n_ in range(4) for i_ in range(8)]

            def load_xA(k):
                if k < len(itsA):
                    n_, i_ = itsA[k]
                    dma("sp", xpc[k % 4][:, :], x_src[128 * i_:128 * i_ + 128, 512 * n_:512 * n_ + 512],
                        [] if d_xsrc is None else [d_xsrc[i_]], [xpc[k % 4]])

            for k_ in range(3):
                load_xA(k_)
            for n in range(4):
                if n >= 1:
                    load_piece(n + 1)
                wp = wpc[n % 2]
                for i in range(8):
                    bk = banks[cnt % 2]
                    xp_ = xpc[cnt % 4]
                    r_ = rp[cnt % 2]
                    load_xA(cnt + 3)
                    cnt += 1
                    for kc in range(16):
                        mm(bk[:, :], mixT[:, kc, 128 * i:128 * i + 128], wp[:, kc, :], kc == 0, kc == 15, [mixTh[i // 4], wp], [bk])
                    stt(r_[:, :], xp_[:, :], ALPHA, bk[:, :], ALU.mult, ALU.add, [xp_, bk], [r_])
                    dma("sp", rscr[128 * i:128 * i + 128, 512 * n:512 * n + 512], r_[:, :], [r_], [d_rscr[i]])
            for q in range(4):
                hb = mixTh[q // 2].buf
                xnTc[q].buf.w = hb.w
                xnTc[q].buf.r = dict(hb.r)
            def b0(i):
                dma("sp", rt[i % 4][:, :], rscr[128 * i:128 * i + 128, :], [d_rscr[i]], [rt[i % 4]])

            def b1(i):
                r_, st_, mv_ = rt[i % 4], stq[i % 2], mvq[i % 2]
                for c in range(4):
                    S.op("dve", lambda e, c=c, r_=r_, st_=st_: e.bn_stats(st_[:, 6 * c:6 * c + 6], r_[:, 512 * c:512 * c + 512]), [r_], [st_])
                S.op("dve", lambda e, st_=st_, mv_=mv_: e.bn_aggr(mv_[:, 0:2], st_[:, 0:24]), [st_], [mv_])
                act(mv_[:, 2:3], mv_[:, 1:2], AF.Ln, [mv_], [mv_], bias=LN_EPS)
                act(mv_[:, 2:3], mv_[:, 2:3], AF.Exp, [mv_], [mv_], scale=-0.5)
                stt(mv_[:, 3:4], mv_[:, 0:1], -1.0, mv_[:, 2:3], ALU.mult, ALU.mult, [mv_], [mv_])

            def b2(i):
                r_, mv_ = rt[i % 4], mvq[i % 2]
                act(r_[:, :], r_[:, :], AF.Identity, [r_, mv_], [r_], scale=mv_[:, 2:3], bias=mv_[:, 3:4])
                tt("pool", r_[:, :], r_[:, :], lng[:, :], ALU.mult, [r_, lng], [r_])

            def b3(i):
                r_ = rt[i % 4]
                tt("dve", r_[:, :], r_[:, :], lnb[:, :], ALU.add, [r_, lnb], [r_])
                dma("sp", xps[128 * i:128 * i + 128, :], r_[:, :], [r_], [d_xps[i]])
                cp("act", xb[i % 2][:, :], r_[:, :], [r_], [xb[i % 2]])

            def b4(i):
                transposes_to(xb[i % 2], xpT, 128 * i, 16, 0, (4, 5) if i % 2 == 0 else (6, 7), i)

            cstate = {"cnt": 0}
            pend = []
            itsC = [(n_, i_) for n_ in range(4) for i_ in range(8)]

            def load_xC(k):
                if k < len(itsC):
                    n_, i_ = itsC[k]
                    dma("sp", xpc[k % 4][:, :], xps[128 * i_:128 * i_ + 128, 512 * n_:512 * n_ + 512], [d_xps[i_]], [xpc[k % 4]])

            def c_iter(n, i):
                cnt = cstate["cnt"]
                wp = wpc[(4 + n) % 2]
                bkg = banks[cnt % 2]
                bke = banks[2] if n == 0 else banks[2 + cnt % 2]
                bkt = banks[3] if n == 0 else banks[4 + cnt % 2]
                xp_ = xpc[cnt % 4]
                r_ = rp[cnt % 2]
                th_ = thg[cnt % 2]
                xnb_ = xnb[cnt % 2]
                if cnt >= 7:
                    load_xC(cnt + 3)
                    if cnt == 7:
                        load_xC(8)
                        load_xC(9)
                cstate["cnt"] = cnt + 1
                for kc in range(16):
                    mm(bkg[:, :], xpT[:, kc, 128 * i:128 * i + 128], wp[:, kc, :], kc == 0, kc == 15, [xpT, wp], [bkg])
                for kc in range(2):
                    mm(bke[:, :], pTb[:, kc, 128 * i:128 * i + 128], wpe_b[:, kc, 512 * n:512 * n + 512], kc == 0, kc == 1,
                       [pTb, wpe_b], [bke])
                act(th_[:, :], bkg[:, :], AF.Tanh, [bkg], [th_], scale=0.5)
                stt(th_[:, :], th_[:, :], 1.0, bke[:, :], ALU.add, ALU.mult, [th_, bke], [th_])
                stt(r_[:, :], th_[:, :], 0.5, xp_[:, :], ALU.mult, ALU.add, [th_, xp_], [r_])
                dma("sp", x_dst[128 * i:128 * i + 128, 512 * n:512 * n + 512], r_[:, :], [r_],
                    [d_out] if last else [d_xcur[i]])
                if not last:
                    cp("act", xnb_[:, :], r_[:, :], [r_], [xnb_])

                    def fin(n=n, i=i, xnb_=xnb_, bkt=bkt):
                        bktb = bkt.ap.bitcast(BF16)
                        for j in range(4):
                            tr(bktb[:, 128 * j:128 * j + 128], xnb_[:, 128 * j:128 * j + 128], ident_b[:, :], [xnb_, ident_b], [bkt])
                        cp("dve", xnT[:, 4 * n:4 * n + 4, 128 * i:128 * i + 128],
                           bktb[:, 0:512].rearrange("p (a b) -> p a b", a=4, b=128), [bkt], [xnTc[i // 2]])
                        if n == 3 and i % 2 == 1:
                            send_xT(L + 1, i // 2, xnT, xnTc[i // 2])
                    pend.append(fin)
                if len(pend) > 1:
                    pend.pop(0)()

            def b3x(i):
                b3(i)
                load_xC(i)

            load_piece(5)
            for step in range(8 + 5):
                for stage, fn in ((0, b0), (1, b1), (2, b2), (3, b3x), (4, b4)):
                    i = step - stage
                    if 0 <= i < 8:
                        fn(i)
                i = step - 5
                if 0 <= i < 8:
                    c_iter(0, i)
            for n in range(1, 4):
                k = 4 + n
                if k + 1 < 8:
                    load_piece(k + 1)
                for i in range(8):
                    c_iter(n, i)
            while pend:
                pend.pop(0)()
            while pending_ag:
                pending_ag.pop(0)()
            S.barrier_all()

        rank_holder = {}
        if debug == "p0":
            phase0()
            dma("sp", dbg.ap(), xT_all[0].ap(), d_xT_all[0], [d_out])
        else:
            for L in range(DEPTH):
                phase1(L)
                if debug == "p1":
                    dma("sp", dbg.ap(), mix_send[L].ap(), d_mix_send[L], [d_out])
                    break
                phase2(L, "RANKQ")
        S.final_wait()

        with nc.Block() as block:
            @block.sync
            def _(e):
                S.emit("sp", e)

            @block.scalar
            def _(e):
                S.emit("act", e)

            @block.vector
            def _(e):
                S.emit("dve", e)

            @block.tensor
            def _(e):
                S.emit("pe", e)

            @block.gpsimd
            def _(e):
                pid = e.partition_id()
                RANK["q"] = pid % 4
                S.emit("pool", e)
    return nc


RANK = {}


_CACHE = {}


def kernel(**inputs):
    maps = _prep_inputs(inputs)
    if "nc" not in _CACHE:
        _CACHE["nc"] = build_program()
    res = run_bass_kernel_spmd(_CACHE["nc"], maps, core_ids=list(range(NCORE)))
    outp = np.zeros((2, SEQ, DM), np.float32)
    for c in range(NCORE):
        b, h = divmod(c, 4)
        outp[b, TQ * h:TQ * h + TQ, :] = np.asarray(res.results[c]["out"], np.float32)
    return outp
```

```python
import contextlib
import numpy as np
import concourse.bass as bass
import concourse.mybir as mybir
from concourse.bass_utils import run_bass_kernel_spmd

F32 = mybir.dt.float32
BF16 = mybir.dt.bfloat16
AF = mybir.ActivationFunctionType
ALU = mybir.AluOpType

DEPTH = 2
DM = 2048
SEQ = 4096
NCORE = 8
TQ = 1024
NBLK = 8
NWC = 2304
ALPHA = (2 * DEPTH) ** 0.25
LN_EPS = 1e-5
RMS_EPS = 1e-6
F_FLOOR = 1e-30
GELU_C = 0.7978845608028654
NROW = 256 + 3 * 128 + 2 * DM
AW = 51200
SAME_ENG_RAW = True
CC_QOS = "P2"
DBG = {"nblk": NBLK, "parts": {"hprep", "conv", "sg", "attn", "hmm"}, "skip": set()}

FM_BQ, FM_BF, FM_AB, FM_AC, FM_AX, FM_DU, FM_GA, FM_GB, FM_GC, FM_GD, FM_CQ, FM_CK = range(12)
FM_SRC = {FM_BQ: 3, FM_BF: 4, FM_AB: 0, FM_AC: 1, FM_AX: 2, FM_DU: 9, FM_GA: 11, FM_GB: 12, FM_GC: 13,
          FM_GD: 14, FM_CQ: 6, FM_CK: 7}


def _consts():
    c = np.zeros((128, 128 * 4 + 512), np.float32)
    i = np.arange(128)
    c[:, 0:128] = np.eye(128, dtype=np.float32)
    c[:, 128:256] = (i[:, None] <= i[None, :]).astype(np.float32)
    c[:, 256:384] = ((i[:, None] <= i[None, :]) & ((i[:, None] // 64) == (i[None, :] // 64))).astype(np.float32)
    c[:, 384:512] = 1.0
    r = np.ones(512, np.float32)
    r[::64] = 0.0
    c[:, 512:1024] = r[None, :]
    return c


def _prep_inputs(inp):
    x = np.asarray(inp["x"], np.float32)
    p = np.asarray(inp["p"], np.float32)
    w_in = np.asarray(inp["w_in"], np.float32)
    w_out = np.asarray(inp["w_out"], np.float32)
    perm = np.array([g * 512 + 128 * hh + i for hh in range(4) for g in range(4) for i in range(128)])
    wout_p = np.ascontiguousarray(w_out[:, perm, :])
    wpg = np.ascontiguousarray(np.asarray(inp["w_pg"], np.float32))
    wpe = np.ascontiguousarray(np.asarray(inp["w_pe"], np.float32))
    cst = _consts()
    xTs = [np.ascontiguousarray(x[b].T) for b in range(2)]
    maps = []
    for c in range(NCORE):
        b, h = divmod(c, 4)
        sl = slice(128 * h, 128 * h + 128)
        cols = []
        for f in range(12):
            blk = FM_SRC[f]
            cols.append(np.arange(blk * 512 + 128 * h, blk * 512 + 128 * h + 128))
        cols.append(np.arange(5 * 512 + 128 * h, 5 * 512 + 128 * h + 128))
        cols.append(np.arange(8 * 512 + 128 * h, 8 * 512 + 128 * h + 128))
        for g in range(4):
            gg = (h + g) % 4
            cols.append(np.arange(10 * 512 + 128 * gg, 10 * 512 + 128 * gg + 128))
        cols = np.concatenate(cols)
        wc = np.ascontiguousarray(w_in[:, :, cols])
        pcol = np.zeros((DEPTH, 128, 8), np.float32)
        prow = np.zeros((DEPTH, 1, NROW), np.float32)
        sgwT = np.zeros((DEPTH, 128, 128), np.float32)
        for i in range(DEPTH):
            pcol[i, :, 0:3] = np.asarray(inp["conv_w"])[i][:, sl].T
            pcol[i, :, 3] = np.asarray(inp["hgrn_lb"])[0, sl]
            pcol[i, :, 4] = np.asarray(inp["hgrn_lb"])[1, sl]
            pcol[i, :, 5] = np.asarray(inp["hgrn_norm_g"])[i, sl]
            pcol[i, :, 6] = np.asarray(inp["diff_norm_g"])[i, sl]
            prow[i, 0, 0:256] = np.asarray(inp["diff_lambda"])[i].reshape(256)
            prow[i, 0, 256:384] = np.asarray(inp["sg_b"])[i, h]
            prow[i, 0, 384:512] = np.asarray(inp["sg_ln_g"])[i, sl]
            prow[i, 0, 512:640] = np.asarray(inp["sg_ln_b"])[i, sl]
            prow[i, 0, 640:640 + DM] = np.asarray(inp["ln_g"])[i]
            prow[i, 0, 640 + DM:640 + 2 * DM] = np.asarray(inp["ln_b"])[i]
            sgwT[i] = np.asarray(inp["sg_w"])[i, h].T
        maps.append({
            "xq": np.ascontiguousarray(x[b, TQ * h:TQ * h + TQ, :]),
            "xTf": xTs[b],
            "pT": np.ascontiguousarray(np.transpose(p[:, b, TQ * h:TQ * h + TQ, :], (0, 2, 1))),
            "wc": wc, "wout": wout_p, "wpg": wpg, "wpe": wpe,
            "pcol": pcol, "prow": prow, "sgwT": sgwT, "cst": cst,
        })
    return maps


class Buf:
    __slots__ = ("name", "w", "r", "excl")

    def __init__(self, name):
        self.name = name
        self.w = None
        self.r = {}
        self.excl = False


class T:
    __slots__ = ("ap", "buf")

    def __init__(self, ap, name="t", buf=None):
        self.ap = ap
        self.buf = buf if buf is not None else Buf(name)

    def __getitem__(self, k):
        return self.ap[k]


class Sched:
    ENGS = ("pe", "act", "dve", "pool", "sp")

    def __init__(self, esem, dsem):
        self.esem = esem
        self.prog = {e: [] for e in self.ENGS}
        self.cnt = {e: 0 for e in esem}
        self.waited = {e: {} for e in self.ENGS}
        self.dq = {q: {"sems": s, "use": [0] * len(s), "next": 0} for q, s in dsem.items()}

    def _need(self, eng, tok, raw):
        if tok is None:
            return
        sem, val, src = tok
        if src == eng:
            if eng == "pe" or not raw or not SAME_ENG_RAW:
                return
        w = self.waited[eng]
        k = id(sem)
        if w.get(k, 0) >= val:
            return
        w[k] = val
        self.prog[eng].append(("wait", sem, val))

    def _deps(self, eng, reads, writes):
        for t in reads:
            self._need(eng, t.buf.w, True)
            if t.buf.excl:
                for tok in t.buf.r.values():
                    self._need(eng, tok, False)
        for t in writes:
            b = t.buf
            self._need(eng, b.w, False)
            for tok in b.r.values():
                self._need(eng, tok, False)

    def _commit(self, tok, reads, writes):
        for t in reads:
            t.buf.r[id(tok[0])] = tok
        for t in writes:
            t.buf.w = tok
            t.buf.r = {}

    def op(self, eng, fn, reads=(), writes=()):
        self._deps(eng, reads, writes)
        self.cnt[eng] += 1
        sem = self.esem[eng]
        tok = (sem, self.cnt[eng], eng)
        self.prog[eng].append(("op", fn, sem, 1))
        self._commit(tok, reads, writes)

    def dma(self, q, fn, reads=(), writes=()):
        self._deps(q, reads, writes)
        d = self.dq[q]
        i = d["next"]
        d["next"] = (i + 1) % len(d["sems"])
        sem = d["sems"][i]
        k = d["use"][i]
        if k > 0:
            self._need(q, (sem, 16 * k, "dma"), True)
        d["use"][i] = k + 1
        tok = (sem, 16 * (k + 1), "dma")
        self.prog[q].append(("op", fn, sem, 16))
        self._commit(tok, reads, writes)

    def barrier_all(self):
        toks = [(self.esem[e], self.cnt[e], e) for e in self.esem if self.cnt[e] > 0]
        for q, d in self.dq.items():
            for sem, k in zip(d["sems"], d["use"]):
                if k > 0:
                    toks.append((sem, 16 * k, "dma"))
        for e in self.ENGS:
            for tok in toks:
                if tok[2] == e and e == "pe":
                    continue
                self._need(e, tok, True)

    def final_wait(self):
        for q, d in self.dq.items():
            for sem, k in zip(d["sems"], d["use"]):
                if k > 0:
                    self._need(q, (sem, 16 * k, "dma"), True)

    def emit(self, eng, e):
        for item in self.prog[eng]:
            if item[0] == "wait":
                e.wait_ge(item[1], item[2])
            else:
                ins = item[1](e)
                ins.then_inc(item[2], item[3])


class Arena:
    def __init__(self, ap, width):
        self.base = ap
        self.width = width
        self.off = 0

    def reset(self, off=0):
        self.off = off

    def f32(self, n, name="t", shape=None):
        n2 = (n + 7) // 8 * 8
        assert self.off + n2 <= self.width, f"arena overflow at {name}: {self.off}+{n2}>{self.width}"
        ap = self.base[:, self.off:self.off + n]
        self.off += n2
        if shape is not None:
            ap = ap.rearrange("p (a b) -> p a b", a=shape[0], b=shape[1])
        return T(ap, name)

    def bf16(self, n, name="t", shape=None):
        nw = (n + 1) // 2
        n2 = (nw + 7) // 8 * 8
        assert self.off + n2 <= self.width, f"arena overflow at {name}: {self.off}+{n2}>{self.width}"
        ap = self.base[:, self.off:self.off + nw].bitcast(BF16)
        self.off += n2
        if shape is not None:
            ap = ap.rearrange("p (a b) -> p a b", a=shape[0], b=shape[1])
        return T(ap, name)


def build_program(debug=None):
    nc = bass.Bass("TRN2", target_bir_lowering=False)
    dt_in = lambda n, s: nc.dram_tensor(n, s, F32, kind="ExternalInput")
    xq = dt_in("xq", [TQ, DM])
    xTf = dt_in("xTf", [DM, SEQ])
    pT = dt_in("pT", [DEPTH, 256, TQ])
    wc = dt_in("wc", [DEPTH, DM, NWC])
    wout = dt_in("wout", [DEPTH, DM, DM])
    wpg = dt_in("wpg", [DEPTH, DM, DM])
    wpe = dt_in("wpe", [DEPTH, 256, DM])
    pcol = dt_in("pcol", [DEPTH, 128, 8])
    prow = dt_in("prow", [DEPTH, 1, NROW])
    sgwT = dt_in("sgwT", [DEPTH, 128, 128])
    cst = dt_in("cst", [128, 1024])
    out = nc.dram_tensor("out", [TQ, DM], F32, kind="ExternalOutput")
    dbg = None
    if debug == "p1":
        dbg = nc.dram_tensor("dbg", [NBLK, 512, 512], BF16, kind="ExternalOutput")
    if debug == "p0":
        dbg = nc.dram_tensor("dbg", [4, 4 * DM, 256], BF16, kind="ExternalOutput")
    xT_send = [nc.dram_tensor(f"xT_send{i}", [4, DM, 256], BF16) for i in range(DEPTH)]
    xT_all = [nc.dram_tensor(f"xT_all{i}", [4, 4 * DM, 256], BF16) for i in range(DEPTH)]
    mix_send = [nc.dram_tensor(f"mix_send{i}", [NBLK, 512, 512], BF16) for i in range(DEPTH)]
    mix_all = [nc.dram_tensor(f"mix_all{i}", [NBLK, 4 * 512, 512], BF16) for i in range(DEPTH)]
    xcur = nc.dram_tensor("xcur", [TQ, DM], F32)
    rscr = nc.dram_tensor("rscr", [TQ, DM], F32)
    xps = nc.dram_tensor("xps", [TQ, DM], F32)
    GROUPS = [[0, 1, 2, 3], [4, 5, 6, 7]]

    with contextlib.ExitStack() as es:
        esem = {e: es.enter_context(nc.semaphore(f"s_{e}")) for e in ("pe", "act", "dve", "pool")}
        dsem = {"sp": [es.enter_context(nc.semaphore(f"d_sp{i}")) for i in range(12)],
                "pool": [es.enter_context(nc.semaphore(f"d_pl{i}")) for i in range(6)]}
        ccsem = [es.enter_context(nc.semaphore(f"cc{i}")) for i in range(12 * DEPTH)]
        arena_t = es.enter_context(nc.sbuf_tensor("arena", [128, AW], F32))
        banks = [T(es.enter_context(nc.psum_tensor(f"bank{i}", [128, 512], F32))[:, :], f"bank{i}") for i in range(8)]
        for bk_ in banks:
            bk_.buf.excl = True
        S = Sched(esem, dsem)
        A = Arena(arena_t[:, :], AW)

        d_xT_send = [[T(None, "xTs") for _ in range(4)] for _ in range(DEPTH)]
        d_xT_all = [[T(None, "xTa") for _ in range(4)] for _ in range(DEPTH)]
        d_mix_send = [[T(None, "mxs") for _ in range(NBLK)] for _ in range(DEPTH)]
        d_mix_all = [[T(None, "mxa") for _ in range(NBLK)] for _ in range(DEPTH)]
        d_xcur = [T(None, "xcur") for _ in range(8)]
        d_rscr = [T(None, "rscr") for _ in range(8)]
        d_xps = [T(None, "xps") for _ in range(8)]
        d_out = T(None, "out")
        cc_count = [0]

        def mm(o, l, r, start, stop, reads, writes):
            S.op("pe", lambda e, o=o, l=l, r=r, st=start, sp=stop: e.matmul(o, l, r, start=st, stop=sp), reads, writes)

        def tr(o, i, ident, reads, writes):
            S.op("pe", lambda e, o=o, i=i, d=ident: e.transpose(o, i, d), reads, writes)

        def act(o, i, func, reads, writes, scale=None, bias=None):
            kw = {}
            if scale is not None:
                kw["scale"] = scale
            if bias is not None:
                kw["bias"] = bias
            S.op("act", lambda e, o=o, i=i, f=func, kw=kw: e.activation(o, i, f, **kw), reads, writes)

        def tt(eng, o, a, b, op, reads, writes):
            S.op(eng, lambda e, o=o, a=a, b=b, op=op: e.tensor_tensor(o, a, b, op), reads, writes)

        def ts(eng, o, a, s1, s2, op0, op1, reads, writes):
            if op1 is None:
                S.op(eng, lambda e, o=o, a=a, s1=s1, op0=op0: e.tensor_scalar(o, a, s1, None, op0), reads, writes)
            else:
                S.op(eng, lambda e, o=o, a=a, s1=s1, s2=s2, op0=op0, op1=op1: e.tensor_scalar(o, a, s1, s2, op0, op1), reads, writes)

        def stt(o, a, s, b, op0, op1, reads, writes):
            S.op("dve", lambda e, o=o, a=a, s=s, b=b, op0=op0, op1=op1: e.scalar_tensor_tensor(o, a, s, b, op0, op1), reads, writes)

        def cp(eng, o, i, reads, writes):
            if eng == "act":
                act(o, i, AF.Copy, reads, writes)
            else:
                S.op(eng, lambda e, o=o, i=i: e.tensor_copy(o, i), reads, writes)

        def dma(q, o, i, reads, writes):
            S.dma(q, lambda e, o=o, i=i: e.dma_start(out=o, in_=i), reads, writes)

        def allgather(src, dst, r, w):
            k = cc_count[0]
            cc_count[0] += 1
            sem = ccsem[k]
            S._deps("pool", r, w)
            S.prog["pool"].append(("op", lambda e, s=src, d=dst: e.collective_compute(
                "AllGather", ALU.bypass, replica_groups=GROUPS, ins=[s.opt()], outs=[d.opt()], dma_qos=CC_QOS), sem, 1))
            tok = (sem, 1, "cc")
            S._commit(tok, r, w)

        c_f32 = A.f32(1024, "cst")
        ident_b = A.bf16(128, "ident")
        tri_b = A.bf16(128, "tri")
        hmask_b = A.bf16(128, "hmask")
        ones_b = A.bf16(128, "ones_b")
        dma("sp", c_f32[:, :], cst[:, :], [], [c_f32])
        cp("dve", ident_b[:, :], c_f32[:, 0:128], [c_f32], [ident_b])
        cp("dve", tri_b[:, :], c_f32[:, 128:256], [c_f32], [tri_b])
        cp("dve", hmask_b[:, :], c_f32[:, 256:384], [c_f32], [hmask_b])
        cp("dve", ones_b[:, :], c_f32[:, 384:512], [c_f32], [ones_b])
        tri_f = c_f32[:, 128:256]
        ones_f = c_f32[:, 384:512]
        rmask_f = c_f32[:, 512:1024]
        A_BASE = A.off

        def transposes_to(xb, dstT, col0, nchunk, kc0, bank_pair, flip, dep=None):
            for half in range((nchunk + 7) // 8):
                bk = banks[bank_pair[half % 2]]
                bkb = bk.ap.bitcast(BF16)
                n_here = min(8, nchunk - 8 * half)
                for j in range(n_here):
                    jj = 8 * half + j
                    tr(bkb[:, j * 128:(j + 1) * 128], xb[:, jj * 128:(jj + 1) * 128], ident_b[:, :], [xb, ident_b], [bk])
                eng = "act" if (half + flip) % 2 == 0 else "dve"
                o = dstT[:, kc0 + 8 * half:kc0 + 8 * half + n_here, col0:col0 + 128]
                i = bkb[:, 0:n_here * 128].rearrange("p (a b) -> p a b", a=n_here, b=128)
                cp(eng, o, i, [bk], [dstT if dep is None else dep])

        deferred_ag = []
        pending_ag = []

        def send_xT(L, q, xT, dep):
            dma("pool", xT_send[L].ap()[q].rearrange("(kc p) t -> p kc t", p=128), xT[:, :, 256 * q:256 * q + 256],
                [dep], [d_xT_send[L][q]])
            ag = lambda L=L, q=q: allgather(xT_send[L].ap()[q], xT_all[L].ap()[q], [d_xT_send[L][q]], [d_xT_all[L][q]])
            if debug == "p0":
                ag()
            elif q < 2:
                pending_ag.append(ag)
            else:
                deferred_ag.append(ag)

        def phase0():
            A.reset(A_BASE)
            xT = A.bf16(16 * TQ, "xT", (16, TQ))
            xbs = [A.bf16(DM, f"xb{i}") for i in range(2)]
            xTc = [T(None, f"xTc{q}") for q in range(4)]
            for i in range(8):
                xb = xbs[i % 2]
                q = i // 2
                dma("pool", xb[:, :], xq[128 * i:128 * i + 128, :], [], [xb])
                transposes_to(xb, xT, 128 * i, 16, 0, (0, 1) if i % 2 == 0 else (2, 3), i, dep=xTc[q])
                if i % 2 == 1:
                    send_xT(0, q, xT, xTc[q])
            S.barrier_all()

        def phase1(L):
            A.reset(A_BASE)
            lam_init = 0.8 - 0.6 * float(np.exp(-0.3 * L))
            WC_PIECES = [(0, 512), (512, 1024), (1024, 1536), (1536, 1792), (1792, 2304)]
            Wc = [A.bf16(16 * (b - a), f"Wc{a}", (16, b - a)) for a, b in WC_PIECES]
            wcv = wc.ap()[L].rearrange("(kc p) n -> p kc n", p=128)
            def load_wc(pi):
                a, b = WC_PIECES[pi]
                dma("pool", Wc[pi][:, :, :], wcv[:, :, a:b], [], [Wc[pi]])

            def wslice(col0, ncol):
                for pi, (a, b) in enumerate(WC_PIECES):
                    if a <= col0 and col0 + ncol <= b:
                        return Wc[pi], (lambda kc, pi=pi, a=a: Wc[pi][:, kc, col0 - a:col0 - a + ncol])
                raise AssertionError("w slice crosses pieces")

            KT = A.bf16(SEQ, "KT")
            KTb = [T(KT.ap, f"KT{t}") for t in range(NBLK)]
            Vt = A.bf16(32 * 128, "V", (32, 128))
            Vb = [T(Vt.ap, f"V{t}") for t in range(NBLK)]
            xTb = [A.bf16(16 * 512, f"xTb{i}", (16, 512)) for i in range(2)]
            mixb = A.bf16(4 * 512, "mixb", (4, 512))
            Sst = A.f32(128, "Sst")
            pc = A.f32(8, "pc")
            lamrow = A.f32(256, "lamrow")
            lamtmp = A.f32(128, "lamtmp")
            sc = A.f32(16, "sc")
            WTm = A.bf16(128, "WTm")
            sgw_f = A.f32(128, "sgw_f")
            bsb = A.f32(128, "bsb")
            lng = A.f32(128, "lng")
            lnb = A.f32(128, "lnb")
            QT = A.bf16(512, "QT")
            qt = A.bf16(512, "qt")
            kt = A.bf16(512, "kt")
            kdT = A.bf16(512, "kdT")
            kd_tm = A.bf16(512, "kd_tm", (4, 128))
            iv_tm = A.bf16(512, "iv_tm", (4, 128))
            am = A.bf16(512, "am", (4, 128))
            Ssc = A.bf16(1024, "Ssc", (8, 128))
            vn = [A.bf16(128, f"vn{i}") for i in range(4)]
            dv2s = A.f32(128, "dv2s")
            P1 = [A.bf16(512, f"P1_{i}") for i in range(2)]
            P2 = [A.bf16(512, f"P2_{i}") for i in range(2)]
            ebm = A.f32(8, "ebm")
            ebl = A.f32(8, "ebl")
            st6 = A.f32(8, "st6")
            mv = A.f32(8, "mv")
            zc = A.f32(516, "zc")
            W = {n: A.f32(512, n) for n in (
                "q_sb", "th", "f", "km", "bT", "E", "ek", "D2",
                "ab", "ac", "acc", "du", "u1", "u2", "dv1",
                "sgA", "sgB", "sgC", "sgD", "thg", "dvs0", "dvs1", "dvs2", "dvs3")}
            dvs = [W[f"dvs{i}"] for i in range(4)]

            dma("sp", pc[:, :], pcol[L, :, :], [], [pc])
            dma("sp", lamrow[:, :], prow[L, 0:1, 0:256].to_broadcast([128, 256]), [], [lamrow])
            dma("sp", sgw_f[:, :], sgwT[L, :, :], [], [sgw_f])
            dma("sp", bsb[:, :], prow[L, 0:1, 256:384].to_broadcast([128, 128]), [], [bsb])
            dma("sp", lng[:, :], prow[L, 0:1, 384:512].to_broadcast([128, 128]), [], [lng])
            dma("sp", lnb[:, :], prow[L, 0:1, 512:640].to_broadcast([128, 128]), [], [lnb])
            if L == 0:
                S.op("dve", lambda e: e.memset(sc[:, 0:1], 0.0), [], [sc])
            else:
                tt("dve", sc[:, 0:1], pc[:, 4:5], pc[:, 3:4], ALU.subtract, [pc], [sc])
                act(sc[:, 0:1], sc[:, 0:1], AF.Tanh, [sc], [sc], scale=0.5)
                ts("dve", sc[:, 0:1], sc[:, 0:1], 0.5, 0.5, ALU.mult, ALU.add, [sc], [sc])
            ts("dve", sc[:, 1:2], sc[:, 0:1], -0.5, 0.5, ALU.mult, ALU.add, [sc], [sc])
            ts("dve", sc[:, 2:3], sc[:, 0:1], 0.5, 0.5, ALU.mult, ALU.add, [sc], [sc])
            ts("dve", sc[:, 3:4], sc[:, 0:1], 0.5, -0.5, ALU.mult, ALU.add, [sc], [sc])
            tt("dve", lamtmp[:, 0:64], lamrow[:, 0:64], lamrow[:, 64:128], ALU.mult, [lamrow], [lamtmp])
            tt("dve", lamtmp[:, 64:128], lamrow[:, 128:192], lamrow[:, 192:256], ALU.mult, [lamrow], [lamtmp])
            S.op("dve", lambda e: e.tensor_reduce(sc[:, 6:8], lamtmp[:, :].rearrange("p (a b) -> p a b", a=2, b=64),
                                                  mybir.AxisListType.X, ALU.add), [lamtmp], [sc])
            act(sc[:, 8:10], sc[:, 6:8], AF.Exp, [sc], [sc])
            tt("dve", sc[:, 4:5], sc[:, 9:10], sc[:, 8:9], ALU.subtract, [sc], [sc])
            ts("dve", sc[:, 4:5], sc[:, 4:5], -lam_init, None, ALU.add, None, [sc], [sc])
            ts("dve", sc[:, 5:6], pc[:, 6:7], 1.0 - lam_init, None, ALU.mult, None, [pc], [sc])
            tt("dve", WTm[:, :], sgw_f[:, :], tri_f, ALU.mult, [sgw_f, c_f32], [WTm])
            S.op("dve", lambda e: e.memset(Sst[:, :], 0.0), [], [Sst])
            S.op("dve", lambda e: e.memset(zc[:, 0:2], 0.0), [], [zc])

            pbank = [0]

            def next_pbank():
                b = banks[pbank[0] % 2]
                pbank[0] += 1
                return b

            def load_xT(t):
                r, half = divmod(t, 2)
                if L == 0:
                    src = xTf.ap()[:, 512 * t:512 * t + 512].rearrange("(kc p) t -> p kc t", p=128)
                    dma("pool", xTb[t % 2][:, :, :], src, [], [xTb[t % 2]])
                    return
                for qq in range(2):
                    q = 2 * half + qq
                    src = xT_all[L].ap()[q][r * DM:(r + 1) * DM, :].rearrange("(kc p) t -> p kc t", p=128)
                    dma("sp", xTb[t % 2][:, :, 256 * qq:256 * qq + 256], src, [d_xT_all[L][q]], [xTb[t % 2]])


            def make_block(t):
                xt = xTb[t % 2]
                mx = mixb
                c0 = 512 * t
                tiles, st3 = [], []
                ch_h, ch_c, ch_u, ch_v = [], [], [], []

                def proj_fm(f):
                    bk = next_pbank()
                    wT, wfn = wslice(128 * f, 128)
                    for kc in range(16):
                        mm(bk[:, :], wfn(kc), xt[:, kc, :], kc == 0, kc == 15, [wT, xt], [bk])
                    return bk

                def proj_tm(tt_i, col0, ncol):
                    bk = next_pbank()
                    wT, wfn = wslice(col0, ncol)
                    for kc in range(16):
                        mm(bk[:, 0:ncol], xt[:, kc, 128 * tt_i:128 * tt_i + 128], wfn(kc), kc == 0, kc == 15, [wT, xt], [bk])
                    return bk

                def t_copy(f, dst):
                    def fn():
                        bk = proj_fm(f)
                        cp("act", dst[:, :], bk[:, :], [bk], [dst])
                    return fn

                def t_bf():
                    bk = proj_fm(FM_BF)
                    act(W["th"][:, :], bk[:, :], AF.Tanh, [bk], [W["th"]], scale=0.5)

                def t_ax():
                    bk = proj_fm(FM_AX)
                    tt("dve", zc[:, 2:514], bk[:, :], W["ac"][:, :], ALU.mult, [bk, W["ac"]], [zc])

                def t_gate(f, dst):
                    def fn():
                        bk = proj_fm(f)
                        act(W["thg"][:, :], bk[:, :], AF.Tanh, [bk], [W["thg"]], scale=0.5)
                        stt(dst[:, :], W["thg"][:, :], 1.0, bk[:, :], ALU.add, ALU.mult, [W["thg"], bk], [dst])
                    return fn

                def t_dv(i):
                    def fn():
                        bk = proj_tm(i, 1792, 512)
                        cp("act", dvs[i][:, :], bk[:, :], [bk], [dvs[i]])
                    return fn

                def t_bicv(i):
                    def fn():
                        bk = proj_tm(i, 1536, 256)
                        cp("act", iv_tm[:, i, :], bk[:, 0:128], [bk], [iv_tm])
                        cp("dve", Vt[:, 4 * t + i, :], bk[:, 128:256], [bk], [Vb[t]])
                    return fn

                def t_ck():
                    bk = proj_fm(FM_CK)
                    cp("act", KT[:, c0:c0 + 512], bk[:, :], [bk], [KTb[t]])

                def t_cq():
                    bk = proj_fm(FM_CQ)
                    cp("dve", QT[:, :], bk[:, :], [bk], [QT])

                tiles += [t_copy(FM_BQ, W["q_sb"]), t_bf, t_copy(FM_AB, W["ab"]), t_copy(FM_AC, W["ac"]), t_ax,
                          t_copy(FM_DU, W["du"])]
                tiles += [t_dv(i) for i in range(4)]
                tiles += [t_gate(FM_GA, W["sgA"]), t_gate(FM_GB, W["sgB"]), t_gate(FM_GD, W["sgD"]), t_gate(FM_GC, W["sgC"])]
                tiles += [t_bicv(i) for i in range(4)]
                tiles += [t_ck, t_cq]

                th, f_, km, bT, E, ek, D2 = (W[n] for n in ("th", "f", "km", "bT", "E", "ek", "D2"))
                bT3 = bT[:, :].rearrange("p (a b) -> p a b", a=8, b=64)
                E3 = E[:, :].rearrange("p (a b) -> p a b", a=8, b=64)
                D23 = D2[:, :].rearrange("p (a b) -> p a b", a=8, b=64)
                H = ch_h.append
                H(lambda: ts("dve", f_[:, :], th[:, :], sc[:, 1:2], sc[:, 2:3], ALU.mult, ALU.add, [th, sc], [f_]))
                H(lambda: ts("dve", km[:, :], th[:, :], sc[:, 3:4], sc[:, 1:2], ALU.mult, ALU.add, [th, sc], [km]))
                H(lambda: S.op("dve", lambda e: e.tensor_scalar_max(f_[:, :], f_[:, :], F_FLOOR), [f_], [f_]))
                H(lambda: act(f_[:, :], f_[:, :], AF.Ln, [f_], [f_]))
                H(lambda: S.op("dve", lambda e: e.tensor_tensor_scan(bT[:, :], rmask_f, f_[:, :], 0.0, ALU.mult, ALU.add),
                               [f_, c_f32], [bT]))
                H(lambda: tt("dve", E3, bT3, bT3[:, :, 31:32].to_broadcast([128, 8, 64]), ALU.subtract, [bT], [E]))
                H(lambda: tt("dve", D23, bT3[:, :, 63:64].to_broadcast([128, 8, 64]), bT3, ALU.subtract, [bT], [D2]))
                H(lambda: act(ek[:, :], E[:, :], AF.Exp, [E], [ek], scale=-1.0))
                H(lambda: act(E[:, :], E[:, :], AF.Exp, [E], [E]))
                H(lambda: act(D2[:, :], D2[:, :], AF.Exp, [D2], [D2]))
                H(lambda: act(ebm[:, :], bT3[:, :, 31], AF.Exp, [bT], [ebm]))
                H(lambda: act(ebl[:, :], bT3[:, :, 63], AF.Exp, [bT], [ebl]))
                H(lambda: tt("dve", qt[:, :], W["q_sb"][:, :], E[:, :], ALU.mult, [W["q_sb"], E], [qt]))
                H(lambda: tt("dve", kt[:, :], km[:, :], ek[:, :], ALU.mult, [km, ek], [kt]))
                H(lambda: tt("dve", kdT[:, :], km[:, :], D2[:, :], ALU.mult, [km, D2], [kdT]))

                acc = W["acc"]
                C = ch_c.append
                C(lambda: ts("dve", acc[:, :], zc[:, 0:512], pc[:, 0:1], None, ALU.mult, None, [zc, pc], [acc]))
                C(lambda: stt(acc[:, :], zc[:, 1:513], pc[:, 1:2], acc[:, :], ALU.mult, ALU.add, [zc, pc, acc], [acc]))
                C(lambda: stt(acc[:, :], zc[:, 2:514], pc[:, 2:3], acc[:, :], ALU.mult, ALU.add, [zc, pc, acc], [acc]))
                C(lambda: tt("dve", acc[:, :], acc[:, :], W["ab"][:, :], ALU.mult, [acc, W["ab"]], [acc]))
                C(lambda: stt(mx[:, 0, :], acc[:, :], 0.5, W["sgA"][:, :], ALU.mult, ALU.mult, [acc, W["sgA"]], [mx]))
                C(lambda: S.op("dve", lambda e: e.tensor_copy(zc[:, 0:2], zc[:, 512:514]), [zc], [zc]))

                du, u1, u2 = W["du"], W["u1"], W["u2"]
                U = ch_u.append
                U(lambda: act(u1[:, :], du[:, :], AF.Square, [du], [u1]))
                U(lambda: ts("dve", u1[:, :], u1[:, :], 0.044715, 1.0, ALU.mult, ALU.add, [u1], [u1]))
                U(lambda: tt("dve", u1[:, :], u1[:, :], du[:, :], ALU.mult, [u1, du], [u1]))
                U(lambda: act(u1[:, :], u1[:, :], AF.Tanh, [u1], [u1], scale=GELU_C))
                U(lambda: stt(u2[:, :], u1[:, :], 1.0, du[:, :], ALU.add, ALU.mult, [u1, du], [u2]))

                dv1, dv2 = W["dv1"], dv2s
                V = ch_v.append
                for i in range(4):
                    d_ = dvs[i]
                    vnt = vn[i]
                    V(lambda d_=d_: act(dv1[:, :], d_[:, :], AF.Square, [d_], [dv1]))
                    V(lambda: ts("dve", dv1[:, :], dv1[:, :], 0.044715, 1.0, ALU.mult, ALU.add, [dv1], [dv1]))
                    V(lambda d_=d_: tt("dve", dv1[:, :], dv1[:, :], d_[:, :], ALU.mult, [dv1, d_], [dv1]))
                    V(lambda: act(dv1[:, :], dv1[:, :], AF.Tanh, [dv1], [dv1], scale=GELU_C))
                    V(lambda d_=d_: stt(d_[:, :], dv1[:, :], 1.0, d_[:, :], ALU.add, ALU.mult, [dv1, d_], [d_]))
                    V(lambda d_=d_: S.op("dve", lambda e: e.bn_stats(st6[:, 0:6], d_[:, :]), [d_], [st6]))
                    V(lambda: S.op("dve", lambda e: e.bn_aggr(mv[:, 0:2], st6[:, 0:6]), [st6], [mv]))
                    V(lambda: act(mv[:, 2:3], mv[:, 1:2], AF.Ln, [mv], [mv], scale=0.25, bias=LN_EPS))
                    V(lambda: act(mv[:, 2:3], mv[:, 2:3], AF.Exp, [mv], [mv], scale=-0.5))
                    V(lambda: ts("dve", mv[:, 3:4], mv[:, 2:3], 0.5, None, ALU.mult, None, [mv], [mv]))
                    V(lambda: stt(mv[:, 4:5], mv[:, 0:1], -1.0, mv[:, 3:4], ALU.mult, ALU.mult, [mv], [mv]))
                    V(lambda d_=d_: ts("dve", dv2[:, :], d_[:, 0:128], mv[:, 3:4], mv[:, 4:5], ALU.mult, ALU.add, [d_, mv], [dv2]))
                    V(lambda: tt("dve", dv2[:, :], dv2[:, :], lng[:, :], ALU.mult, [dv2, lng], [dv2]))
                    V(lambda vnt=vnt: tt("dve", vnt[:, :], dv2[:, :], lnb[:, :], ALU.add, [dv2, lnb], [vnt]))

                chain = []
                srcs = [ch_h, ch_v, ch_c, ch_u]
                while any(srcs):
                    for s_ in srcs:
                        if s_:
                            chain.append(s_.pop(0))
                    if ch_v:
                        chain.append(ch_v.pop(0))

                bS1, bS2, bO1, bO2, bZ1, bZ2 = banks[2], banks[3], banks[4], banks[5], banks[6], banks[7]
                nkb = 4 * (t + 1)

                def s_mm(kb):
                    d = kb - 4 * t
                    q0 = 128 * d if d > 0 else 0
                    kr = [KTb[kb // 4], QT]
                    mm(bS1[:, q0:512], KT[0:64, 128 * kb:128 * kb + 128], QT[0:64, q0:512], True, True, kr, [bS1])
                    mm(bS2[:, q0:512], KT[64:128, 128 * kb:128 * kb + 128], QT[64:128, q0:512], True, True, kr, [bS2])

                def exp_pv(kb):
                    d = kb - 4 * t
                    q0 = 128 * d if d > 0 else 0
                    p1, p2 = P1[kb % 2], P2[kb % 2]
                    act(p1[:, q0:512], bS1[:, q0:512], AF.Exp, [bS1], [p1], scale=0.125)
                    act(p2[:, q0:512], bS2[:, q0:512], AF.Exp, [bS2], [p2], scale=0.125)
                    if d >= 0:
                        tt("dve", p1[:, q0:q0 + 128], p1[:, q0:q0 + 128], tri_b[:, :], ALU.mult, [p1, tri_b], [p1])
                        tt("dve", p2[:, q0:q0 + 128], p2[:, q0:q0 + 128], tri_b[:, :], ALU.mult, [p2, tri_b], [p2])
                    return (kb, q0, p1, p2)

                def pv_mm(info):
                    kb, q0, p1, p2 = info
                    first = kb == 0
                    last = kb == nkb - 1
                    vb = Vb[kb // 4]
                    mm(bO1[:, q0:512], Vt[:, kb, :], p1[:, q0:512], first, last, [vb, p1], [bO1])
                    mm(bZ1[:, q0:512], ones_b[:, :], p1[:, q0:512], first, last, [ones_b, p1], [bZ1])
                    mm(bO2[:, q0:512], Vt[:, kb, :], p2[:, q0:512], first, last, [vb, p2], [bO2])
                    mm(bZ2[:, q0:512], ones_b[:, :], p2[:, q0:512], first, last, [ones_b, p2], [bZ2])

                def stage2():
                    s_mm(0)
                    for kb in range(nkb):
                        info = exp_pv(kb)
                        if kb + 1 < nkb:
                            s_mm(kb + 1)
                        pv_mm(info)
                        k = -(-len(chain) // (nkb - kb))
                        for _ in range(k):
                            chain.pop(0)()
                    while chain:
                        chain.pop(0)()

                r1, r2, oa, ob = W["E"], W["ek"], W["D2"], W["bT"]
                bTr, bSa, bD0, bD1, bOh, bSV = banks[2], banks[3], banks[4], banks[5], banks[6], banks[7]
                bTrb = bTr.ap.bitcast(BF16)

                def a1():
                    S.op("dve", lambda e: e.reciprocal(r1[:, :], bZ1[:, :]), [bZ1], [r1])
                    S.op("dve", lambda e: e.reciprocal(r2[:, :], bZ2[:, :]), [bZ2], [r2])
                    tt("dve", oa[:, :], bO1[:, :], r1[:, :], ALU.mult, [bO1, r1], [oa])
                    tt("dve", ob[:, :], bO2[:, :], r2[:, :], ALU.mult, [bO2, r2], [ob])
                    stt(oa[:, :], ob[:, :], sc[:, 4:5], oa[:, :], ALU.mult, ALU.add, [ob, sc, oa], [oa])
                    act(ob[:, :], oa[:, :], AF.Square, [oa], [ob])

                def h1():
                    for i in range(4):
                        tr(bTrb[:, 128 * i:128 * i + 128], kdT[:, 128 * i:128 * i + 128], ident_b[:, :], [kdT, ident_b], [bTr])
                    cp("act", kd_tm[:, :, :], bTrb[:, 0:512].rearrange("p (a b) -> p a b", a=4, b=128), [bTr], [kd_tm])

                def a2():
                    mm(bSa[:, :], ones_f, ob[:, :], True, True, [c_f32, ob], [bSa])
                    act(ob[:, :], bSa[:, :], AF.Ln, [bSa], [ob], scale=1.0 / 128.0, bias=RMS_EPS)
                    act(ob[:, :], ob[:, :], AF.Exp, [ob], [ob], scale=-0.5)
                    stt(oa[:, :], oa[:, :], sc[:, 5:6], ob[:, :], ALU.mult, ALU.mult, [oa, sc, ob], [oa])
                    stt(mx[:, 2, :], oa[:, :], 0.5, W["sgC"][:, :], ALU.mult, ALU.mult, [oa, W["sgC"]], [mx])

                def h2():
                    for c in range(8):
                        bd = bD0 if c % 2 == 0 else bD1
                        r0 = 64 * (c % 2)
                        mm(bd[:, 128 * (c // 2):128 * (c // 2) + 128], kd_tm[r0:r0 + 64, c // 2, :], iv_tm[r0:r0 + 64, c // 2, :],
                           True, True, [kd_tm, iv_tm], [bd])
                    for i in range(4):
                        mm(bTr[:, 128 * i:128 * i + 128], kt[:, 128 * i:128 * i + 128], qt[:, 128 * i:128 * i + 128],
                           True, True, [kt, qt], [bTr])
                    tt("dve", am[:, :, :], bTr[:, :].rearrange("p (a b) -> p a b", a=4, b=128),
                       hmask_b[:, :].unsqueeze(1).to_broadcast([128, 4, 128]), ALU.mult, [bTr, hmask_b], [am])
                    for c in range(8):
                        bd = bD0 if c % 2 == 0 else bD1
                        ts("dve", Ssc[:, c, :], Sst[:, :], ebm[:, c:c + 1], None, ALU.mult, None, [Sst, ebm], [Ssc])
                        stt(Sst[:, :], Sst[:, :], ebl[:, c:c + 1], bd[:, 128 * (c // 2):128 * (c // 2) + 128], ALU.mult, ALU.add,
                            [Sst, ebl, bd], [Sst])

                def v3():
                    for i in range(4):
                        mm(bSV[:, 128 * i:128 * i + 128], vn[i][:, :], WTm[:, :], i == 0, i == 3, [vn[i], WTm], [bSV])
                    o1 = W["u1"]
                    tt("dve", o1[:, :].rearrange("p (a b) -> p a b", a=4, b=128),
                       bSV[:, :].rearrange("p (a b) -> p a b", a=4, b=128),
                       bsb[:, :].unsqueeze(1).to_broadcast([128, 4, 128]), ALU.add, [bSV, bsb], [o1])
                    tt("dve", o1[:, :], o1[:, :], W["u2"][:, :], ALU.mult, [o1, W["u2"]], [o1])
                    stt(mx[:, 3, :], o1[:, :], 0.25, W["sgD"][:, :], ALU.mult, ALU.mult, [o1, W["sgD"]], [mx])

                def h3():
                    for i in range(4):
                        mm(bOh[:, 128 * i:128 * i + 128], iv_tm[:, i, :], am[:, i, :], i == 0, False, [iv_tm, am], [bOh])
                    for c in range(8):
                        mm(bOh[:, 64 * c:64 * c + 64], Ssc[:, c, :], qt[:, 64 * c:64 * c + 64], False, c == 7, [Ssc, qt], [bOh])
                    act(W["f"][:, :], bOh[:, :], AF.Square, [bOh], [W["f"]])

                def h4():
                    fq = W["f"]
                    mm(bSa[:, :], ones_f, fq[:, :], True, True, [c_f32, fq], [bSa])
                    act(fq[:, :], bSa[:, :], AF.Ln, [bSa], [fq], scale=1.0 / 128.0, bias=RMS_EPS)
                    act(fq[:, :], fq[:, :], AF.Exp, [fq], [fq], scale=-0.5)
                    stt(fq[:, :], bOh[:, :], pc[:, 5:6], fq[:, :], ALU.mult, ALU.mult, [bOh, pc, fq], [fq])
                    stt(mx[:, 1, :], fq[:, :], 0.5, W["sgB"][:, :], ALU.mult, ALU.mult, [fq, W["sgB"]], [mx])

                def fin():
                    dst = mix_send[L].ap()[t].rearrange("(g p) t -> p g t", p=128)
                    dma("sp", dst, mx[:, :, :], [mx], [d_mix_send[L][t]])
                    allgather(mix_send[L].ap()[t], mix_all[L].ap()[t], [d_mix_send[L][t]], [d_mix_all[L][t]])

                st3 += [a1, h1, a2, h2, v3, h3, h4, fin]
                return tiles, stage2, st3

            load_xT(0)
            for pi in (0, 1, 4, 2, 3):
                load_wc(pi)
            while deferred_ag:
                deferred_ag.pop(0)()
            pend3 = []
            for t in range(NBLK):
                if t + 1 < NBLK:
                    load_xT(t + 1)
                tiles, stage2, st3 = make_block(t)
                for k, tile in enumerate(tiles):
                    tile()
                    if pend3:
                        pend3.pop(0)()
                while pend3:
                    pend3.pop(0)()
                stage2()
                pend3 = st3
            while pend3:
                pend3.pop(0)()

            S.barrier_all()

        def phase2(L, rank_q):
            A.reset(A_BASE)
            last = L == DEPTH - 1
            x_src = xq if L == 0 else xcur
            d_xsrc = None if L == 0 else d_xcur
            x_dst = out if last else xcur
            mixT = A.bf16(16 * TQ, "mixT", (16, TQ))
            xpT = A.bf16(16 * TQ, "xpT", (16, TQ))
            wpc = [A.bf16(16 * 512, f"wpc{i}", (16, 512)) for i in range(2)]
            wpe_b = A.bf16(2 * DM, "wpe", (2, DM))
            pTb = A.bf16(2 * TQ, "pTb", (2, TQ))
            lng = A.f32(DM, "lng")
            lnb = A.f32(DM, "lnb")
            xpc = [A.f32(512, f"xpc{i}") for i in range(4)]
            rp = [A.f32(512, f"rp{i}") for i in range(2)]
            rt = [A.f32(DM, f"rt{i}") for i in range(4)]
            xb = [A.bf16(DM, f"xb{i}") for i in range(2)]
            stq = [A.f32(32, f"st{i}") for i in range(2)]
            mvq = [A.f32(8, f"mv{i}") for i in range(2)]
            xnb = [A.bf16(512, f"xnb{i}") for i in range(2)]
            thg = [A.f32(512, f"thg{i}") for i in range(2)]
            st = A.f32(32, "st")
            mv = A.f32(8, "mv")
            xnT = mixT
            xnTc = [T(None, f"xnTc{q}") for q in range(4)]

            mav = mix_all[L].ap()
            wov = wout.ap()[L].rearrange("(kc p) n -> p kc n", p=128)
            wgv = wpg.ap()[L].rearrange("(kc p) n -> p kc n", p=128)
            pieces = [(wov, n) for n in range(4)] + [(wgv, n) for n in range(4)]

            def load_piece(k):
                v, n = pieces[k]
                dma("pool", wpc[k % 2][:, :, :], v[:, :, 512 * n:512 * n + 512], [], [wpc[k % 2]])

            def mk_mix_load(half):
                def fn(e):
                    rq = RANK["q"]
                    src = mav[bass.ds(rq * 2 + half, 1), :, :].rearrange("o (c p) t -> p (o c) t", p=128)
                    return e.dma_start(out=mixT[:, :, 512 * half:512 * half + 512], in_=src)
                return fn
            mixTh = [T(mixT.ap, "mixT0"), T(mixT.ap, "mixT1")]
            S.dma("pool", mk_mix_load(0), d_mix_all[L][0:7], [mixTh[0]])
            load_piece(0)
            load_piece(1)
            S.dma("pool", mk_mix_load(1), d_mix_all[L], [mixTh[1]])
            dma("pool", wpe_b[:, :, :], wpe.ap()[L].rearrange("(kc p) n -> p kc n", p=128), [], [wpe_b])
            dma("pool", pTb[:, :, :], pT.ap()[L].rearrange("(kc p) t -> p kc t", p=128), [], [pTb])
            dma("sp", lng[:, :], prow[L, 0:1, 640:640 + DM].to_broadcast([128, DM]), [], [lng])
            dma("sp", lnb[:, :], prow[L, 0:1, 640 + DM:640 + 2 * DM].to_broadcast([128, DM]), [], [lnb])
            cnt = 0
            itsA = [(n_, i_) for n_ in range(4) for i_ in range(8)]

            def load_xA(k):
                if k < len(itsA):
                    n_, i_ = itsA[k]
                    dma("sp", xpc[k % 4][:, :], x_src[128 * i_:128 * i_ + 128, 512 * n_:512 * n_ + 512],
                        [] if d_xsrc is None else [d_xsrc[i_]], [xpc[k % 4]])

            for k_ in range(3):
                load_xA(k_)
            for n in range(4):
                if n >= 1:
                    load_piece(n + 1)
                wp = wpc[n % 2]
                for i in range(8):
                    bk = banks[cnt % 2]
                    xp_ = xpc[cnt % 4]
                    r_ = rp[cnt % 2]
                    load_xA(cnt + 3)
                    cnt += 1
                    for kc in range(16):
                        mm(bk[:, :], mixT[:, kc, 128 * i:128 * i + 128], wp[:, kc, :], kc == 0, kc == 15, [mixTh[i // 4], wp], [bk])
                    stt(r_[:, :], xp_[:, :], ALPHA, bk[:, :], ALU.mult, ALU.add, [xp_, bk], [r_])
                    dma("sp", rscr[128 * i:128 * i + 128, 512 * n:512 * n + 512], r_[:, :], [r_], [d_rscr[i]])
            for q in range(4):
                hb = mixTh[q // 2].buf
                xnTc[q].buf.w = hb.w
                xnTc[q].buf.r = dict(hb.r)
            def b0(i):
                dma("sp", rt[i % 4][:, :], rscr[128 * i:128 * i + 128, :], [d_rscr[i]], [rt[i % 4]])

            def b1(i):
                r_, st_, mv_ = rt[i % 4], stq[i % 2], mvq[i % 2]
                for c in range(4):
                    S.op("dve", lambda e, c=c, r_=r_, st_=st_: e.bn_stats(st_[:, 6 * c:6 * c + 6], r_[:, 512 * c:512 * c + 512]), [r_], [st_])
                S.op("dve", lambda e, st_=st_, mv_=mv_: e.bn_aggr(mv_[:, 0:2], st_[:, 0:24]), [st_], [mv_])
                act(mv_[:, 2:3], mv_[:, 1:2], AF.Ln, [mv_], [mv_], bias=LN_EPS)
                act(mv_[:, 2:3], mv_[:, 2:3], AF.Exp, [mv_], [mv_], scale=-0.5)
                stt(mv_[:, 3:4], mv_[:, 0:1], -1.0, mv_[:, 2:3], ALU.mult, ALU.mult, [mv_], [mv_])

            def b2(i):
                r_, mv_ = rt[i % 4], mvq[i % 2]
                act(r_[:, :], r_[:, :], AF.Identity, [r_, mv_], [r_], scale=mv_[:, 2:3], bias=mv_[:, 3:4])
                tt("pool", r_[:, :], r_[:, :], lng[:, :], ALU.mult, [r_, lng], [r_])

            def b3(i):
                r_ = rt[i % 4]
                tt("dve", r_[:, :], r_[:, :], lnb[:, :], ALU.add, [r_, lnb], [r_])
                dma("sp", xps[128 * i:128 * i + 128, :], r_[:, :], [r_], [d_xps[i]])
                cp("act", xb[i % 2][:, :], r_[:, :], [r_], [xb[i % 2]])

            def b4(i):
                transposes_to(xb[i % 2], xpT, 128 * i, 16, 0, (4, 5) if i % 2 == 0 else (6, 7), i)

            for step in range(8 + 4):
                for stage, fn in ((0, b0), (1, b1), (2, b2), (3, b3), (4, b4)):
                    i = step - stage
                    if 0 <= i < 8:
                        fn(i)
            cnt = 0
            pend = []
            itsC = [(n_, i_) for n_ in range(4) for i_ in range(8)]

            def load_xC(k):
                if k < len(itsC):
                    n_, i_ = itsC[k]
                    dma("sp", xpc[k % 4][:, :], xps[128 * i_:128 * i_ + 128, 512 * n_:512 * n_ + 512], [d_xps[i_]], [xpc[k % 4]])

            for k_ in range(3):
                load_xC(k_)
            for n in range(4):
                k = 4 + n
                if k + 1 < 8:
                    load_piece(k + 1)
                wp = wpc[k % 2]
                for i in range(8):
                    bkg = banks[cnt % 2]
                    bke = banks[2 + cnt % 2]
                    bkt = banks[4 + cnt % 2]
                    xp_ = xpc[cnt % 4]
                    r_ = rp[cnt % 2]
                    th_ = thg[cnt % 2]
                    xnb_ = xnb[cnt % 2]
                    load_xC(cnt + 3)
                    cnt += 1
                    for kc in range(16):
                        mm(bkg[:, :], xpT[:, kc, 128 * i:128 * i + 128], wp[:, kc, :], kc == 0, kc == 15, [xpT, wp], [bkg])
                    for kc in range(2):
                        mm(bke[:, :], pTb[:, kc, 128 * i:128 * i + 128], wpe_b[:, kc, 512 * n:512 * n + 512], kc == 0, kc == 1,
                           [pTb, wpe_b], [bke])
                    act(th_[:, :], bkg[:, :], AF.Tanh, [bkg], [th_], scale=0.5)
                    stt(th_[:, :], th_[:, :], 1.0, bke[:, :], ALU.add, ALU.mult, [th_, bke], [th_])
                    stt(r_[:, :], th_[:, :], 0.5, xp_[:, :], ALU.mult, ALU.add, [th_, xp_], [r_])
                    dma("sp", x_dst[128 * i:128 * i + 128, 512 * n:512 * n + 512], r_[:, :], [r_],
                        [d_out] if last else [d_xcur[i]])
                    if not last:
                        cp("act", xnb_[:, :], r_[:, :], [r_], [xnb_])

                        def fin(n=n, i=i, xnb_=xnb_, bkt=bkt):
                            bktb = bkt.ap.bitcast(BF16)
                            for j in range(4):
                                tr(bktb[:, 128 * j:128 * j + 128], xnb_[:, 128 * j:128 * j + 128], ident_b[:, :], [xnb_, ident_b], [bkt])
                            cp("dve", xnT[:, 4 * n:4 * n + 4, 128 * i:128 * i + 128],
                               bktb[:, 0:512].rearrange("p (a b) -> p a b", a=4, b=128), [bkt], [xnTc[i // 2]])
                            if n == 3 and i % 2 == 1:
                                send_xT(L + 1, i // 2, xnT, xnTc[i // 2])
                        pend.append(fin)
                    if len(pend) > 1:
                        pend.pop(0)()
            while pend:
                pend.pop(0)()
            while pending_ag:
                pending_ag.pop(0)()
            S.barrier_all()

        rank_holder = {}
        if debug == "p0":
            phase0()
            dma("sp", dbg.ap(), xT_all[0].ap(), d_xT_all[0], [d_out])
        else:
            for L in range(DEPTH):
                phase1(L)
                if debug == "p1":
                    dma("sp", dbg.ap(), mix_send[L].ap(), d_mix_send[L], [d_out])
                    break
                phase2(L, "RANKQ")
        S.final_wait()

        with nc.Block() as block:
            @block.sync
            def _(e):
                S.emit("sp", e)

            @block.scalar
            def _(e):
                S.emit("act", e)

            @block.vector
            def _(e):
                S.emit("dve", e)

            @block.tensor
            def _(e):
                S.emit("pe", e)

            @block.gpsimd
            def _(e):
                pid = e.partition_id()
                RANK["q"] = pid % 4
                S.emit("pool", e)
    return nc


RANK = {}


_CACHE = {}


def kernel(**inputs):
    maps = _prep_inputs(inputs)
    if "nc" not in _CACHE:
        _CACHE["nc"] = build_program()
    res = run_bass_kernel_spmd(_CACHE["nc"], maps, core_ids=list(range(NCORE)))
    outp = np.zeros((2, SEQ, DM), np.float32)
    for c in range(NCORE):
        b, h = divmod(c, 4)
        outp[b, TQ * h:TQ * h + TQ, :] = np.asarray(res.results[c]["out"], np.float32)
    return outp
```

```python
import contextlib
import numpy as np
import concourse.bass as bass
import concourse.mybir as mybir
from concourse.bass_utils import run_bass_kernel_spmd

F32 = mybir.dt.float32
BF16 = mybir.dt.bfloat16
AF = mybir.ActivationFunctionType
ALU = mybir.AluOpType

DEPTH = 2
DM = 2048
SEQ = 4096
NCORE = 8
TQ = 1024
NBLK = 8
NWC = 2304
ALPHA = (2 * DEPTH) ** 0.25
LN_EPS = 1e-5
RMS_EPS = 1e-6
F_FLOOR = 1e-30
GELU_C = 0.7978845608028654
NROW = 256 + 3 * 128 + 2 * DM
AW = 51200
SAME_ENG_RAW = True
CC_QOS = "P2"
DBG = {"nblk": NBLK, "parts": {"hprep", "conv", "sg", "attn", "hmm"}, "skip": set()}

FM_BQ, FM_BF, FM_AB, FM_AC, FM_AX, FM_DU, FM_GA, FM_GB, FM_GC, FM_GD, FM_CQ, FM_CK = range(12)
FM_SRC = {FM_BQ: 3, FM_BF: 4, FM_AB: 0, FM_AC: 1, FM_AX: 2, FM_DU: 9, FM_GA: 11, FM_GB: 12, FM_GC: 13,
          FM_GD: 14, FM_CQ: 6, FM_CK: 7}


def _consts():
    c = np.zeros((128, 128 * 4 + 512), np.float32)
    i = np.arange(128)
    c[:, 0:128] = np.eye(128, dtype=np.float32)
    c[:, 128:256] = (i[:, None] <= i[None, :]).astype(np.float32)
    c[:, 256:384] = ((i[:, None] <= i[None, :]) & ((i[:, None] // 64) == (i[None, :] // 64))).astype(np.float32)
    c[:, 384:512] = 1.0
    r = np.ones(512, np.float32)
    r[::64] = 0.0
    c[:, 512:1024] = r[None, :]
    return c


def _prep_inputs(inp):
    x = np.asarray(inp["x"], np.float32)
    p = np.asarray(inp["p"], np.float32)
    w_in = np.asarray(inp["w_in"], np.float32)
    w_out = np.asarray(inp["w_out"], np.float32)
    perm = np.array([g * 512 + 128 * hh + i for hh in range(4) for g in range(4) for i in range(128)])
    wout_p = np.ascontiguousarray(w_out[:, perm, :])
    wpg = np.ascontiguousarray(np.asarray(inp["w_pg"], np.float32))
    wpe = np.ascontiguousarray(np.asarray(inp["w_pe"], np.float32))
    cst = _consts()
    xTs = [np.ascontiguousarray(x[b].T) for b in range(2)]
    maps = []
    for c in range(NCORE):
        b, h = divmod(c, 4)
        sl = slice(128 * h, 128 * h + 128)
        cols = []
        for f in range(12):
            blk = FM_SRC[f]
            cols.append(np.arange(blk * 512 + 128 * h, blk * 512 + 128 * h + 128))
        cols.append(np.arange(5 * 512 + 128 * h, 5 * 512 + 128 * h + 128))
        cols.append(np.arange(8 * 512 + 128 * h, 8 * 512 + 128 * h + 128))
        for g in range(4):
            gg = (h + g) % 4
            cols.append(np.arange(10 * 512 + 128 * gg, 10 * 512 + 128 * gg + 128))
        cols = np.concatenate(cols)
        wc = np.ascontiguousarray(w_in[:, :, cols])
        pcol = np.zeros((DEPTH, 128, 8), np.float32)
        prow = np.zeros((DEPTH, 1, NROW), np.float32)
        sgwT = np.zeros((DEPTH, 128, 128), np.float32)
        for i in range(DEPTH):
            pcol[i, :, 0:3] = np.asarray(inp["conv_w"])[i][:, sl].T
            pcol[i, :, 3] = np.asarray(inp["hgrn_lb"])[0, sl]
            pcol[i, :, 4] = np.asarray(inp["hgrn_lb"])[1, sl]
            pcol[i, :, 5] = np.asarray(inp["hgrn_norm_g"])[i, sl]
            pcol[i, :, 6] = np.asarray(inp["diff_norm_g"])[i, sl]
            prow[i, 0, 0:256] = np.asarray(inp["diff_lambda"])[i].reshape(256)
            prow[i, 0, 256:384] = np.asarray(inp["sg_b"])[i, h]
            prow[i, 0, 384:512] = np.asarray(inp["sg_ln_g"])[i, sl]
            prow[i, 0, 512:640] = np.asarray(inp["sg_ln_b"])[i, sl]
            prow[i, 0, 640:640 + DM] = np.asarray(inp["ln_g"])[i]
            prow[i, 0, 640 + DM:640 + 2 * DM] = np.asarray(inp["ln_b"])[i]
            sgwT[i] = np.asarray(inp["sg_w"])[i, h].T
        maps.append({
            "xq": np.ascontiguousarray(x[b, TQ * h:TQ * h + TQ, :]),
            "xTf": xTs[b],
            "pT": np.ascontiguousarray(np.transpose(p[:, b, TQ * h:TQ * h + TQ, :], (0, 2, 1))),
            "wc": wc, "wout": wout_p, "wpg": wpg, "wpe": wpe,
            "pcol": pcol, "prow": prow, "sgwT": sgwT, "cst": cst,
        })
    return maps


class Buf:
    __slots__ = ("name", "w", "r", "excl")

    def __init__(self, name):
        self.name = name
        self.w = None
        self.r = {}
        self.excl = False


class T:
    __slots__ = ("ap", "buf")

    def __init__(self, ap, name="t", buf=None):
        self.ap = ap
        self.buf = buf if buf is not None else Buf(name)

    def __getitem__(self, k):
        return self.ap[k]


class Sched:
    ENGS = ("pe", "act", "dve", "pool", "sp")

    def __init__(self, esem, dsem):
        self.esem = esem
        self.prog = {e: [] for e in self.ENGS}
        self.cnt = {e: 0 for e in esem}
        self.waited = {e: {} for e in self.ENGS}
        self.dq = {q: {"sems": s, "use": [0] * len(s), "next": 0} for q, s in dsem.items()}

    def _need(self, eng, tok, raw):
        if tok is None:
            return
        sem, val, src = tok
        if src == eng:
            if eng == "pe" or not raw or not SAME_ENG_RAW:
                return
        w = self.waited[eng]
        k = id(sem)
        if w.get(k, 0) >= val:
            return
        w[k] = val
        self.prog[eng].append(("wait", sem, val))

    def _deps(self, eng, reads, writes):
        for t in reads:
            self._need(eng, t.buf.w, True)
            if t.buf.excl:
                for tok in t.buf.r.values():
                    self._need(eng, tok, False)
        for t in writes:
            b = t.buf
            self._need(eng, b.w, False)
            for tok in b.r.values():
                self._need(eng, tok, False)

    def _commit(self, tok, reads, writes):
        for t in reads:
            t.buf.r[id(tok[0])] = tok
        for t in writes:
            t.buf.w = tok
            t.buf.r = {}

    def op(self, eng, fn, reads=(), writes=()):
        self._deps(eng, reads, writes)
        self.cnt[eng] += 1
        sem = self.esem[eng]
        tok = (sem, self.cnt[eng], eng)
        self.prog[eng].append(("op", fn, sem, 1))
        self._commit(tok, reads, writes)

    def dma(self, q, fn, reads=(), writes=()):
        self._deps(q, reads, writes)
        d = self.dq[q]
        i = d["next"]
        d["next"] = (i + 1) % len(d["sems"])
        sem = d["sems"][i]
        k = d["use"][i]
        if k > 0:
            self._need(q, (sem, 16 * k, "dma"), True)
        d["use"][i] = k + 1
        tok = (sem, 16 * (k + 1), "dma")
        self.prog[q].append(("op", fn, sem, 16))
        self._commit(tok, reads, writes)

    def barrier_all(self):
        toks = [(self.esem[e], self.cnt[e], e) for e in self.esem if self.cnt[e] > 0]
        for q, d in self.dq.items():
            for sem, k in zip(d["sems"], d["use"]):
                if k > 0:
                    toks.append((sem, 16 * k, "dma"))
        for e in self.ENGS:
            for tok in toks:
                if tok[2] == e and e == "pe":
                    continue
                self._need(e, tok, True)

    def final_wait(self):
        for q, d in self.dq.items():
            for sem, k in zip(d["sems"], d["use"]):
                if k > 0:
                    self._need(q, (sem, 16 * k, "dma"), True)

    def emit(self, eng, e):
        for item in self.prog[eng]:
            if item[0] == "wait":
                e.wait_ge(item[1], item[2])
            else:
                ins = item[1](e)
                ins.then_inc(item[2], item[3])


class Arena:
    def __init__(self, ap, width):
        self.base = ap
        self.width = width
        self.off = 0

    def reset(self, off=0):
        self.off = off

    def f32(self, n, name="t", shape=None):
        n2 = (n + 7) // 8 * 8
        assert self.off + n2 <= self.width, f"arena overflow at {name}: {self.off}+{n2}>{self.width}"
        ap = self.base[:, self.off:self.off + n]
        self.off += n2
        if shape is not None:
            ap = ap.rearrange("p (a b) -> p a b", a=shape[0], b=shape[1])
        return T(ap, name)

    def bf16(self, n, name="t", shape=None):
        nw = (n + 1) // 2
        n2 = (nw + 7) // 8 * 8
        assert self.off + n2 <= self.width, f"arena overflow at {name}: {self.off}+{n2}>{self.width}"
        ap = self.base[:, self.off:self.off + nw].bitcast(BF16)
        self.off += n2
        if shape is not None:
            ap = ap.rearrange("p (a b) -> p a b", a=shape[0], b=shape[1])
        return T(ap, name)


def build_program(debug=None):
    nc = bass.Bass("TRN2", target_bir_lowering=False)
    dt_in = lambda n, s: nc.dram_tensor(n, s, F32, kind="ExternalInput")
    xq = dt_in("xq", [TQ, DM])
    xTf = dt_in("xTf", [DM, SEQ])
    pT = dt_in("pT", [DEPTH, 256, TQ])
    wc = dt_in("wc", [DEPTH, DM, NWC])
    wout = dt_in("wout", [DEPTH, DM, DM])
    wpg = dt_in("wpg", [DEPTH, DM, DM])
    wpe = dt_in("wpe", [DEPTH, 256, DM])
    pcol = dt_in("pcol", [DEPTH, 128, 8])
    prow = dt_in("prow", [DEPTH, 1, NROW])
    sgwT = dt_in("sgwT", [DEPTH, 128, 128])
    cst = dt_in("cst", [128, 1024])
    out = nc.dram_tensor("out", [TQ, DM], F32, kind="ExternalOutput")
    dbg = None
    if debug == "p1":
        dbg = nc.dram_tensor("dbg", [NBLK, 512, 512], BF16, kind="ExternalOutput")
    if debug == "p0":
        dbg = nc.dram_tensor("dbg", [4, 4 * DM, 256], BF16, kind="ExternalOutput")
    xT_send = [nc.dram_tensor(f"xT_send{i}", [4, DM, 256], BF16) for i in range(DEPTH)]
    xT_all = [nc.dram_tensor(f"xT_all{i}", [4, 4 * DM, 256], BF16) for i in range(DEPTH)]
    mix_send = [nc.dram_tensor(f"mix_send{i}", [NBLK, 512, 512], BF16) for i in range(DEPTH)]
    mix_all = [nc.dram_tensor(f"mix_all{i}", [NBLK, 4 * 512, 512], BF16) for i in range(DEPTH)]
    xcur = nc.dram_tensor("xcur", [TQ, DM], F32)
    rscr = nc.dram_tensor("rscr", [TQ, DM], F32)
    xps = nc.dram_tensor("xps", [TQ, DM], F32)
    GROUPS = [[0, 1, 2, 3], [4, 5, 6, 7]]

    with contextlib.ExitStack() as es:
        esem = {e: es.enter_context(nc.semaphore(f"s_{e}")) for e in ("pe", "act", "dve", "pool")}
        dsem = {"sp": [es.enter_context(nc.semaphore(f"d_sp{i}")) for i in range(12)],
                "pool": [es.enter_context(nc.semaphore(f"d_pl{i}")) for i in range(6)]}
        ccsem = [es.enter_context(nc.semaphore(f"cc{i}")) for i in range(12 * DEPTH)]
        arena_t = es.enter_context(nc.sbuf_tensor("arena", [128, AW], F32))
        banks = [T(es.enter_context(nc.psum_tensor(f"bank{i}", [128, 512], F32))[:, :], f"bank{i}") for i in range(8)]
        for bk_ in banks:
            bk_.buf.excl = True
        S = Sched(esem, dsem)
        A = Arena(arena_t[:, :], AW)

        d_xT_send = [[T(None, "xTs") for _ in range(4)] for _ in range(DEPTH)]
        d_xT_all = [[T(None, "xTa") for _ in range(4)] for _ in range(DEPTH)]
        d_mix_send = [[T(None, "mxs") for _ in range(NBLK)] for _ in range(DEPTH)]
        d_mix_all = [[T(None, "mxa") for _ in range(NBLK)] for _ in range(DEPTH)]
        d_xcur = [T(None, "xcur") for _ in range(8)]
        d_rscr = [T(None, "rscr") for _ in range(8)]
        d_xps = [T(None, "xps") for _ in range(8)]
        d_out = T(None, "out")
        cc_count = [0]

        def mm(o, l, r, start, stop, reads, writes):
            S.op("pe", lambda e, o=o, l=l, r=r, st=start, sp=stop: e.matmul(o, l, r, start=st, stop=sp), reads, writes)

        def tr(o, i, ident, reads, writes):
            S.op("pe", lambda e, o=o, i=i, d=ident: e.transpose(o, i, d), reads, writes)

        def act(o, i, func, reads, writes, scale=None, bias=None):
            kw = {}
            if scale is not None:
                kw["scale"] = scale
            if bias is not None:
                kw["bias"] = bias
            S.op("act", lambda e, o=o, i=i, f=func, kw=kw: e.activation(o, i, f, **kw), reads, writes)

        def tt(eng, o, a, b, op, reads, writes):
            S.op(eng, lambda e, o=o, a=a, b=b, op=op: e.tensor_tensor(o, a, b, op), reads, writes)

        def ts(eng, o, a, s1, s2, op0, op1, reads, writes):
            if op1 is None:
                S.op(eng, lambda e, o=o, a=a, s1=s1, op0=op0: e.tensor_scalar(o, a, s1, None, op0), reads, writes)
            else:
                S.op(eng, lambda e, o=o, a=a, s1=s1, s2=s2, op0=op0, op1=op1: e.tensor_scalar(o, a, s1, s2, op0, op1), reads, writes)

        def stt(o, a, s, b, op0, op1, reads, writes):
            S.op("dve", lambda e, o=o, a=a, s=s, b=b, op0=op0, op1=op1: e.scalar_tensor_tensor(o, a, s, b, op0, op1), reads, writes)

        def cp(eng, o, i, reads, writes):
            if eng == "act":
                act(o, i, AF.Copy, reads, writes)
            else:
                S.op(eng, lambda e, o=o, i=i: e.tensor_copy(o, i), reads, writes)

        def dma(q, o, i, reads, writes):
            S.dma(q, lambda e, o=o, i=i: e.dma_start(out=o, in_=i), reads, writes)

        def allgather(src, dst, r, w):
            k = cc_count[0]
            cc_count[0] += 1
            sem = ccsem[k]
            S._deps("pool", r, w)
            S.prog["pool"].append(("op", lambda e, s=src, d=dst: e.collective_compute(
                "AllGather", ALU.bypass, replica_groups=GROUPS, ins=[s.opt()], outs=[d.opt()], dma_qos=CC_QOS), sem, 1))
            tok = (sem, 1, "cc")
            S._commit(tok, r, w)

        c_f32 = A.f32(1024, "cst")
        ident_b = A.bf16(128, "ident")
        tri_b = A.bf16(128, "tri")
        hmask_b = A.bf16(128, "hmask")
        ones_b = A.bf16(128, "ones_b")
        dma("sp", c_f32[:, :], cst[:, :], [], [c_f32])
        cp("dve", ident_b[:, :], c_f32[:, 0:128], [c_f32], [ident_b])
        cp("dve", tri_b[:, :], c_f32[:, 128:256], [c_f32], [tri_b])
        cp("dve", hmask_b[:, :], c_f32[:, 256:384], [c_f32], [hmask_b])
        cp("dve", ones_b[:, :], c_f32[:, 384:512], [c_f32], [ones_b])
        tri_f = c_f32[:, 128:256]
        ones_f = c_f32[:, 384:512]
        rmask_f = c_f32[:, 512:1024]
        A_BASE = A.off

        def transposes_to(xb, dstT, col0, nchunk, kc0, bank_pair, flip, dep=None):
            for half in range((nchunk + 7) // 8):
                bk = banks[bank_pair[half % 2]]
                bkb = bk.ap.bitcast(BF16)
                n_here = min(8, nchunk - 8 * half)
                for j in range(n_here):
                    jj = 8 * half + j
                    tr(bkb[:, j * 128:(j + 1) * 128], xb[:, jj * 128:(jj + 1) * 128], ident_b[:, :], [xb, ident_b], [bk])
                eng = "act" if (half + flip) % 2 == 0 else "dve"
                o = dstT[:, kc0 + 8 * half:kc0 + 8 * half + n_here, col0:col0 + 128]
                i = bkb[:, 0:n_here * 128].rearrange("p (a b) -> p a b", a=n_here, b=128)
                cp(eng, o, i, [bk], [dstT if dep is None else dep])

        def inherit(dst, srcs):
            r = {}
            for s_ in srcs:
                toks = list(s_.buf.r.values()) + ([s_.buf.w] if s_.buf.w is not None else [])
                for tok in toks:
                    k_ = id(tok[0])
                    if k_ not in r or r[k_][1] < tok[1]:
                        r[k_] = tok
            dst.buf.w = None
            dst.buf.r = r

        def mk_mix_load(L, dst_ap, half):
            def fn(e):
                rq = RANK["q"]
                src = mix_all[L].ap()[bass.ds(rq * 2 + half, 1), :, :].rearrange("o (c p) t -> p (o c) t", p=128)
                return e.dma_start(out=dst_ap[:, :, 512 * half:512 * half + 512], in_=src)
            return fn

        def p2_front_aps():
            mixT_ap = A.base[:, A_BASE:A_BASE + 8192].bitcast(BF16).rearrange("p (a b) -> p a b", a=16, b=TQ)
            wpc_ap = [A.base[:, A_BASE + 8192 + 4096 * j:A_BASE + 8192 + 4096 * (j + 1)].bitcast(BF16).rearrange(
                "p (a b) -> p a b", a=16, b=512) for j in range(2)]
            return mixT_ap, wpc_ap

        deferred_ag = []
        pending_ag = []

        def send_xT(L, q, xT, dep):
            dma("pool", xT_send[L].ap()[q].rearrange("(kc p) t -> p kc t", p=128), xT[:, :, 256 * q:256 * q + 256],
                [dep], [d_xT_send[L][q]])
            ag = lambda extra=(), L=L, q=q: allgather(xT_send[L].ap()[q], xT_all[L].ap()[q],
                                                      [d_xT_send[L][q]] + list(extra), [d_xT_all[L][q]])
            if debug == "p0":
                ag()
            elif q < 2:
                pending_ag.append(ag)
            else:
                deferred_ag.append(ag)

        def phase0():
            A.reset(A_BASE)
            xT = A.bf16(16 * TQ, "xT", (16, TQ))
            xbs = [A.bf16(DM, f"xb{i}") for i in range(2)]
            xTc = [T(None, f"xTc{q}") for q in range(4)]
            for i in range(8):
                xb = xbs[i % 2]
                q = i // 2
                dma("pool", xb[:, :], xq[128 * i:128 * i + 128, :], [], [xb])
                transposes_to(xb, xT, 128 * i, 16, 0, (0, 1) if i % 2 == 0 else (2, 3), i, dep=xTc[q])
                if i % 2 == 1:
                    send_xT(0, q, xT, xTc[q])
            S.barrier_all()

        def phase1(L):
            A.reset(A_BASE)
            lam_init = 0.8 - 0.6 * float(np.exp(-0.3 * L))
            WC_PIECES = [(0, 512), (512, 1024), (1024, 1536), (1536, 1792), (1792, 2304)]
            Wc = [A.bf16(16 * (b - a), f"Wc{a}", (16, b - a)) for a, b in WC_PIECES]
            wcv = wc.ap()[L].rearrange("(kc p) n -> p kc n", p=128)
            def load_wc(pi):
                a, b = WC_PIECES[pi]
                dma("pool", Wc[pi][:, :, :], wcv[:, :, a:b], [], [Wc[pi]])

            def wslice(col0, ncol):
                for pi, (a, b) in enumerate(WC_PIECES):
                    if a <= col0 and col0 + ncol <= b:
                        return Wc[pi], (lambda kc, pi=pi, a=a: Wc[pi][:, kc, col0 - a:col0 - a + ncol])
                raise AssertionError("w slice crosses pieces")

            KT = A.bf16(SEQ, "KT")
            KTb = [T(KT.ap, f"KT{t}") for t in range(NBLK)]
            Vt = A.bf16(32 * 128, "V", (32, 128))
            Vb = [T(Vt.ap, f"V{t}") for t in range(NBLK)]
            xTb = [A.bf16(16 * 512, f"xTb{i}", (16, 512)) for i in range(2)]
            mixb = A.bf16(4 * 512, "mixb", (4, 512))
            Sst = A.f32(128, "Sst")
            pc = A.f32(8, "pc")
            lamrow = A.f32(256, "lamrow")
            lamtmp = A.f32(128, "lamtmp")
            sc = A.f32(16, "sc")
            WTm = A.bf16(128, "WTm")
            sgw_f = A.f32(128, "sgw_f")
            bsb = A.f32(128, "bsb")
            lng = A.f32(128, "lng")
            lnb = A.f32(128, "lnb")
            QT = A.bf16(512, "QT")
            qt = A.bf16(512, "qt")
            kt = A.bf16(512, "kt")
            kdT = A.bf16(512, "kdT")
            kd_tm = A.bf16(512, "kd_tm", (4, 128))
            iv_tm = A.bf16(512, "iv_tm", (4, 128))
            am = A.bf16(512, "am", (4, 128))
            Ssc = A.bf16(1024, "Ssc", (8, 128))
            vn = [A.bf16(128, f"vn{i}") for i in range(4)]
            dv2s = A.f32(128, "dv2s")
            P1 = [A.bf16(512, f"P1_{i}") for i in range(2)]
            P2 = [A.bf16(512, f"P2_{i}") for i in range(2)]
            ebm = A.f32(8, "ebm")
            ebl = A.f32(8, "ebl")
            st6 = A.f32(8, "st6")
            mv = A.f32(8, "mv")
            zc = A.f32(516, "zc")
            W = {n: A.f32(512, n) for n in (
                "q_sb", "th", "f", "km", "bT", "E", "ek", "D2",
                "ab", "ac", "acc", "du", "u1", "u2", "dv1",
                "sgA", "sgB", "sgC", "sgD", "thg", "dvs0", "dvs1", "dvs2", "dvs3")}
            dvs = [W[f"dvs{i}"] for i in range(4)]

            dma("sp", pc[:, :], pcol[L, :, :], [], [pc])
            dma("sp", lamrow[:, :], prow[L, 0:1, 0:256].to_broadcast([128, 256]), [], [lamrow])
            dma("sp", sgw_f[:, :], sgwT[L, :, :], [], [sgw_f])
            dma("sp", bsb[:, :], prow[L, 0:1, 256:384].to_broadcast([128, 128]), [], [bsb])
            dma("sp", lng[:, :], prow[L, 0:1, 384:512].to_broadcast([128, 128]), [], [lng])
            dma("sp", lnb[:, :], prow[L, 0:1, 512:640].to_broadcast([128, 128]), [], [lnb])
            if L == 0:
                S.op("dve", lambda e: e.memset(sc[:, 0:1], 0.0), [], [sc])
            else:
                tt("dve", sc[:, 0:1], pc[:, 4:5], pc[:, 3:4], ALU.subtract, [pc], [sc])
                act(sc[:, 0:1], sc[:, 0:1], AF.Tanh, [sc], [sc], scale=0.5)
                ts("dve", sc[:, 0:1], sc[:, 0:1], 0.5, 0.5, ALU.mult, ALU.add, [sc], [sc])
            ts("dve", sc[:, 1:2], sc[:, 0:1], -0.5, 0.5, ALU.mult, ALU.add, [sc], [sc])
            ts("dve", sc[:, 2:3], sc[:, 0:1], 0.5, 0.5, ALU.mult, ALU.add, [sc], [sc])
            ts("dve", sc[:, 3:4], sc[:, 0:1], 0.5, -0.5, ALU.mult, ALU.add, [sc], [sc])
            tt("dve", lamtmp[:, 0:64], lamrow[:, 0:64], lamrow[:, 64:128], ALU.mult, [lamrow], [lamtmp])
            tt("dve", lamtmp[:, 64:128], lamrow[:, 128:192], lamrow[:, 192:256], ALU.mult, [lamrow], [lamtmp])
            S.op("dve", lambda e: e.tensor_reduce(sc[:, 6:8], lamtmp[:, :].rearrange("p (a b) -> p a b", a=2, b=64),
                                                  mybir.AxisListType.X, ALU.add), [lamtmp], [sc])
            act(sc[:, 8:10], sc[:, 6:8], AF.Exp, [sc], [sc])
            tt("dve", sc[:, 4:5], sc[:, 9:10], sc[:, 8:9], ALU.subtract, [sc], [sc])
            ts("dve", sc[:, 4:5], sc[:, 4:5], -lam_init, None, ALU.add, None, [sc], [sc])
            ts("dve", sc[:, 5:6], pc[:, 6:7], 1.0 - lam_init, None, ALU.mult, None, [pc], [sc])
            tt("dve", WTm[:, :], sgw_f[:, :], tri_f, ALU.mult, [sgw_f, c_f32], [WTm])
            S.op("dve", lambda e: e.memset(Sst[:, :], 0.0), [], [Sst])
            S.op("dve", lambda e: e.memset(zc[:, 0:2], 0.0), [], [zc])

            pbank = [0]

            def next_pbank():
                b = banks[pbank[0] % 2]
                pbank[0] += 1
                return b

            def load_xT(t):
                r, half = divmod(t, 2)
                if L == 0:
                    src = xTf.ap()[:, 512 * t:512 * t + 512].rearrange("(kc p) t -> p kc t", p=128)
                    dma("pool", xTb[t % 2][:, :, :], src, [], [xTb[t % 2]])
                    return
                for qq in range(2):
                    q = 2 * half + qq
                    src = xT_all[L].ap()[q][r * DM:(r + 1) * DM, :].rearrange("(kc p) t -> p kc t", p=128)
                    dma("sp", xTb[t % 2][:, :, 256 * qq:256 * qq + 256], src, [d_xT_all[L][q]], [xTb[t % 2]])


            def make_block(t):
                xt = xTb[t % 2]
                mx = mixb
                c0 = 512 * t
                tiles, st3 = [], []
                ch_h, ch_c, ch_u, ch_v = [], [], [], []

                def proj_fm(f):
                    bk = next_pbank()
                    wT, wfn = wslice(128 * f, 128)
                    for kc in range(16):
                        mm(bk[:, :], wfn(kc), xt[:, kc, :], kc == 0, kc == 15, [wT, xt], [bk])
                    return bk

                def proj_tm(tt_i, col0, ncol):
                    bk = next_pbank()
                    wT, wfn = wslice(col0, ncol)
                    for kc in range(16):
                        mm(bk[:, 0:ncol], xt[:, kc, 128 * tt_i:128 * tt_i + 128], wfn(kc), kc == 0, kc == 15, [wT, xt], [bk])
                    return bk

                def t_copy(f, dst):
                    def fn():
                        bk = proj_fm(f)
                        cp("act", dst[:, :], bk[:, :], [bk], [dst])
                    return fn

                def t_bf():
                    bk = proj_fm(FM_BF)
                    act(W["th"][:, :], bk[:, :], AF.Tanh, [bk], [W["th"]], scale=0.5)

                def t_ax():
                    bk = proj_fm(FM_AX)
                    tt("dve", zc[:, 2:514], bk[:, :], W["ac"][:, :], ALU.mult, [bk, W["ac"]], [zc])

                def t_gate(f, dst):
                    def fn():
                        bk = proj_fm(f)
                        act(W["thg"][:, :], bk[:, :], AF.Tanh, [bk], [W["thg"]], scale=0.5)
                        stt(dst[:, :], W["thg"][:, :], 1.0, bk[:, :], ALU.add, ALU.mult, [W["thg"], bk], [dst])
                    return fn

                def t_dv(i):
                    def fn():
                        bk = proj_tm(i, 1792, 512)
                        cp("act", dvs[i][:, :], bk[:, :], [bk], [dvs[i]])
                    return fn

                def t_bicv(i):
                    def fn():
                        bk = proj_tm(i, 1536, 256)
                        cp("act", iv_tm[:, i, :], bk[:, 0:128], [bk], [iv_tm])
                        cp("dve", Vt[:, 4 * t + i, :], bk[:, 128:256], [bk], [Vb[t]])
                    return fn

                def t_ck():
                    bk = proj_fm(FM_CK)
                    cp("act", KT[:, c0:c0 + 512], bk[:, :], [bk], [KTb[t]])

                def t_cq():
                    bk = proj_fm(FM_CQ)
                    cp("dve", QT[:, :], bk[:, :], [bk], [QT])

                tiles += [t_copy(FM_BQ, W["q_sb"]), t_bf, t_copy(FM_AB, W["ab"]), t_copy(FM_AC, W["ac"]), t_ax,
                          t_copy(FM_DU, W["du"])]
                tiles += [t_dv(i) for i in range(4)]
                tiles += [t_gate(FM_GA, W["sgA"]), t_gate(FM_GB, W["sgB"]), t_gate(FM_GD, W["sgD"]), t_gate(FM_GC, W["sgC"])]
                tiles += [t_bicv(i) for i in range(4)]
                tiles += [t_ck, t_cq]

                th, f_, km, bT, E, ek, D2 = (W[n] for n in ("th", "f", "km", "bT", "E", "ek", "D2"))
                bT3 = bT[:, :].rearrange("p (a b) -> p a b", a=8, b=64)
                E3 = E[:, :].rearrange("p (a b) -> p a b", a=8, b=64)
                D23 = D2[:, :].rearrange("p (a b) -> p a b", a=8, b=64)
                H = ch_h.append
                H(lambda: ts("dve", f_[:, :], th[:, :], sc[:, 1:2], sc[:, 2:3], ALU.mult, ALU.add, [th, sc], [f_]))
                H(lambda: ts("dve", km[:, :], th[:, :], sc[:, 3:4], sc[:, 1:2], ALU.mult, ALU.add, [th, sc], [km]))
                H(lambda: S.op("dve", lambda e: e.tensor_scalar_max(f_[:, :], f_[:, :], F_FLOOR), [f_], [f_]))
                H(lambda: act(f_[:, :], f_[:, :], AF.Ln, [f_], [f_]))
                H(lambda: S.op("dve", lambda e: e.tensor_tensor_scan(bT[:, :], rmask_f, f_[:, :], 0.0, ALU.mult, ALU.add),
                               [f_, c_f32], [bT]))
                H(lambda: tt("dve", E3, bT3, bT3[:, :, 31:32].to_broadcast([128, 8, 64]), ALU.subtract, [bT], [E]))
                H(lambda: tt("dve", D23, bT3[:, :, 63:64].to_broadcast([128, 8, 64]), bT3, ALU.subtract, [bT], [D2]))
                H(lambda: act(ek[:, :], E[:, :], AF.Exp, [E], [ek], scale=-1.0))
                H(lambda: act(E[:, :], E[:, :], AF.Exp, [E], [E]))
                H(lambda: act(D2[:, :], D2[:, :], AF.Exp, [D2], [D2]))
                H(lambda: act(ebm[:, :], bT3[:, :, 31], AF.Exp, [bT], [ebm]))
                H(lambda: act(ebl[:, :], bT3[:, :, 63], AF.Exp, [bT], [ebl]))
                H(lambda: tt("dve", qt[:, :], W["q_sb"][:, :], E[:, :], ALU.mult, [W["q_sb"], E], [qt]))
                H(lambda: tt("dve", kt[:, :], km[:, :], ek[:, :], ALU.mult, [km, ek], [kt]))
                H(lambda: tt("dve", kdT[:, :], km[:, :], D2[:, :], ALU.mult, [km, D2], [kdT]))

                acc = W["acc"]
                C = ch_c.append
                C(lambda: ts("dve", acc[:, :], zc[:, 0:512], pc[:, 0:1], None, ALU.mult, None, [zc, pc], [acc]))
                C(lambda: stt(acc[:, :], zc[:, 1:513], pc[:, 1:2], acc[:, :], ALU.mult, ALU.add, [zc, pc, acc], [acc]))
                C(lambda: stt(acc[:, :], zc[:, 2:514], pc[:, 2:3], acc[:, :], ALU.mult, ALU.add, [zc, pc, acc], [acc]))
                C(lambda: tt("dve", acc[:, :], acc[:, :], W["ab"][:, :], ALU.mult, [acc, W["ab"]], [acc]))
                C(lambda: stt(mx[:, 0, :], acc[:, :], 0.5, W["sgA"][:, :], ALU.mult, ALU.mult, [acc, W["sgA"]], [mx]))
                C(lambda: S.op("dve", lambda e: e.tensor_copy(zc[:, 0:2], zc[:, 512:514]), [zc], [zc]))

                du, u1, u2 = W["du"], W["u1"], W["u2"]
                U = ch_u.append
                U(lambda: act(u1[:, :], du[:, :], AF.Square, [du], [u1]))
                U(lambda: ts("dve", u1[:, :], u1[:, :], 0.044715, 1.0, ALU.mult, ALU.add, [u1], [u1]))
                U(lambda: tt("dve", u1[:, :], u1[:, :], du[:, :], ALU.mult, [u1, du], [u1]))
                U(lambda: act(u1[:, :], u1[:, :], AF.Tanh, [u1], [u1], scale=GELU_C))
                U(lambda: stt(u2[:, :], u1[:, :], 1.0, du[:, :], ALU.add, ALU.mult, [u1, du], [u2]))

                dv1, dv2 = W["dv1"], dv2s
                V = ch_v.append
                for i in range(4):
                    d_ = dvs[i]
                    vnt = vn[i]
                    V(lambda d_=d_: act(dv1[:, :], d_[:, :], AF.Square, [d_], [dv1]))
                    V(lambda: ts("dve", dv1[:, :], dv1[:, :], 0.044715, 1.0, ALU.mult, ALU.add, [dv1], [dv1]))
                    V(lambda d_=d_: tt("dve", dv1[:, :], dv1[:, :], d_[:, :], ALU.mult, [dv1, d_], [dv1]))
                    V(lambda: act(dv1[:, :], dv1[:, :], AF.Tanh, [dv1], [dv1], scale=GELU_C))
                    V(lambda d_=d_: stt(d_[:, :], dv1[:, :], 1.0, d_[:, :], ALU.add, ALU.mult, [dv1, d_], [d_]))
                    V(lambda d_=d_: S.op("dve", lambda e: e.bn_stats(st6[:, 0:6], d_[:, :]), [d_], [st6]))
                    V(lambda: S.op("dve", lambda e: e.bn_aggr(mv[:, 0:2], st6[:, 0:6]), [st6], [mv]))
                    V(lambda: act(mv[:, 2:3], mv[:, 1:2], AF.Ln, [mv], [mv], scale=0.25, bias=LN_EPS))
                    V(lambda: act(mv[:, 2:3], mv[:, 2:3], AF.Exp, [mv], [mv], scale=-0.5))
                    V(lambda: ts("dve", mv[:, 3:4], mv[:, 2:3], 0.5, None, ALU.mult, None, [mv], [mv]))
                    V(lambda: stt(mv[:, 4:5], mv[:, 0:1], -1.0, mv[:, 3:4], ALU.mult, ALU.mult, [mv], [mv]))
                    V(lambda d_=d_: ts("dve", dv2[:, :], d_[:, 0:128], mv[:, 3:4], mv[:, 4:5], ALU.mult, ALU.add, [d_, mv], [dv2]))
                    V(lambda: tt("dve", dv2[:, :], dv2[:, :], lng[:, :], ALU.mult, [dv2, lng], [dv2]))
                    V(lambda vnt=vnt: tt("dve", vnt[:, :], dv2[:, :], lnb[:, :], ALU.add, [dv2, lnb], [vnt]))

                chain = []
                srcs = [ch_h, ch_v, ch_c, ch_u]
                while any(srcs):
                    for s_ in srcs:
                        if s_:
                            chain.append(s_.pop(0))
                    if ch_v:
                        chain.append(ch_v.pop(0))

                bS1, bS2, bO1, bO2, bZ1, bZ2 = banks[2], banks[3], banks[4], banks[5], banks[6], banks[7]
                nkb = 4 * (t + 1)

                def s_mm(kb):
                    d = kb - 4 * t
                    q0 = 128 * d if d > 0 else 0
                    kr = [KTb[kb // 4], QT]
                    mm(bS1[:, q0:512], KT[0:64, 128 * kb:128 * kb + 128], QT[0:64, q0:512], True, True, kr, [bS1])
                    mm(bS2[:, q0:512], KT[64:128, 128 * kb:128 * kb + 128], QT[64:128, q0:512], True, True, kr, [bS2])

                def exp_pv(kb):
                    d = kb - 4 * t
                    q0 = 128 * d if d > 0 else 0
                    p1, p2 = P1[kb % 2], P2[kb % 2]
                    act(p1[:, q0:512], bS1[:, q0:512], AF.Exp, [bS1], [p1], scale=0.125)
                    act(p2[:, q0:512], bS2[:, q0:512], AF.Exp, [bS2], [p2], scale=0.125)
                    if d >= 0:
                        tt("dve", p1[:, q0:q0 + 128], p1[:, q0:q0 + 128], tri_b[:, :], ALU.mult, [p1, tri_b], [p1])
                        tt("dve", p2[:, q0:q0 + 128], p2[:, q0:q0 + 128], tri_b[:, :], ALU.mult, [p2, tri_b], [p2])
                    return (kb, q0, p1, p2)

                def pv_mm(info):
                    kb, q0, p1, p2 = info
                    first = kb == 0
                    last = kb == nkb - 1
                    vb = Vb[kb // 4]
                    mm(bO1[:, q0:512], Vt[:, kb, :], p1[:, q0:512], first, last, [vb, p1], [bO1])
                    mm(bZ1[:, q0:512], ones_b[:, :], p1[:, q0:512], first, last, [ones_b, p1], [bZ1])
                    mm(bO2[:, q0:512], Vt[:, kb, :], p2[:, q0:512], first, last, [vb, p2], [bO2])
                    mm(bZ2[:, q0:512], ones_b[:, :], p2[:, q0:512], first, last, [ones_b, p2], [bZ2])

                def stage2():
                    s_mm(0)
                    for kb in range(nkb):
                        info = exp_pv(kb)
                        if kb + 1 < nkb:
                            s_mm(kb + 1)
                        pv_mm(info)
                        k = -(-len(chain) // (nkb - kb))
                        for _ in range(k):
                            chain.pop(0)()
                    while chain:
                        chain.pop(0)()

                r1, r2, oa, ob = W["E"], W["ek"], W["D2"], W["bT"]
                bTr, bSa, bD0, bD1, bOh, bSV = banks[2], banks[3], banks[4], banks[5], banks[6], banks[7]
                bTrb = bTr.ap.bitcast(BF16)

                def a1():
                    S.op("dve", lambda e: e.reciprocal(r1[:, :], bZ1[:, :]), [bZ1], [r1])
                    S.op("dve", lambda e: e.reciprocal(r2[:, :], bZ2[:, :]), [bZ2], [r2])
                    tt("dve", oa[:, :], bO1[:, :], r1[:, :], ALU.mult, [bO1, r1], [oa])
                    tt("dve", ob[:, :], bO2[:, :], r2[:, :], ALU.mult, [bO2, r2], [ob])
                    stt(oa[:, :], ob[:, :], sc[:, 4:5], oa[:, :], ALU.mult, ALU.add, [ob, sc, oa], [oa])
                    act(ob[:, :], oa[:, :], AF.Square, [oa], [ob])

                def h1():
                    for i in range(4):
                        tr(bTrb[:, 128 * i:128 * i + 128], kdT[:, 128 * i:128 * i + 128], ident_b[:, :], [kdT, ident_b], [bTr])
                    cp("act", kd_tm[:, :, :], bTrb[:, 0:512].rearrange("p (a b) -> p a b", a=4, b=128), [bTr], [kd_tm])

                def a2():
                    mm(bSa[:, :], ones_f, ob[:, :], True, True, [c_f32, ob], [bSa])
                    act(ob[:, :], bSa[:, :], AF.Ln, [bSa], [ob], scale=1.0 / 128.0, bias=RMS_EPS)
                    act(ob[:, :], ob[:, :], AF.Exp, [ob], [ob], scale=-0.5)
                    stt(oa[:, :], oa[:, :], sc[:, 5:6], ob[:, :], ALU.mult, ALU.mult, [oa, sc, ob], [oa])
                    stt(mx[:, 2, :], oa[:, :], 0.5, W["sgC"][:, :], ALU.mult, ALU.mult, [oa, W["sgC"]], [mx])

                def h2():
                    for c in range(8):
                        bd = bD0 if c % 2 == 0 else bD1
                        r0 = 64 * (c % 2)
                        mm(bd[:, 128 * (c // 2):128 * (c // 2) + 128], kd_tm[r0:r0 + 64, c // 2, :], iv_tm[r0:r0 + 64, c // 2, :],
                           True, True, [kd_tm, iv_tm], [bd])
                    for i in range(4):
                        mm(bTr[:, 128 * i:128 * i + 128], kt[:, 128 * i:128 * i + 128], qt[:, 128 * i:128 * i + 128],
                           True, True, [kt, qt], [bTr])
                    tt("dve", am[:, :, :], bTr[:, :].rearrange("p (a b) -> p a b", a=4, b=128),
                       hmask_b[:, :].unsqueeze(1).to_broadcast([128, 4, 128]), ALU.mult, [bTr, hmask_b], [am])
                    for c in range(8):
                        bd = bD0 if c % 2 == 0 else bD1
                        ts("dve", Ssc[:, c, :], Sst[:, :], ebm[:, c:c + 1], None, ALU.mult, None, [Sst, ebm], [Ssc])
                        stt(Sst[:, :], Sst[:, :], ebl[:, c:c + 1], bd[:, 128 * (c // 2):128 * (c // 2) + 128], ALU.mult, ALU.add,
                            [Sst, ebl, bd], [Sst])

                def v3():
                    for i in range(4):
                        mm(bSV[:, 128 * i:128 * i + 128], vn[i][:, :], WTm[:, :], i == 0, i == 3, [vn[i], WTm], [bSV])
                    o1 = W["u1"]
                    tt("dve", o1[:, :].rearrange("p (a b) -> p a b", a=4, b=128),
                       bSV[:, :].rearrange("p (a b) -> p a b", a=4, b=128),
                       bsb[:, :].unsqueeze(1).to_broadcast([128, 4, 128]), ALU.add, [bSV, bsb], [o1])
                    tt("dve", o1[:, :], o1[:, :], W["u2"][:, :], ALU.mult, [o1, W["u2"]], [o1])
                    stt(mx[:, 3, :], o1[:, :], 0.25, W["sgD"][:, :], ALU.mult, ALU.mult, [o1, W["sgD"]], [mx])

                def h3():
                    for i in range(4):
                        mm(bOh[:, 128 * i:128 * i + 128], iv_tm[:, i, :], am[:, i, :], i == 0, False, [iv_tm, am], [bOh])
                    for c in range(8):
                        mm(bOh[:, 64 * c:64 * c + 64], Ssc[:, c, :], qt[:, 64 * c:64 * c + 64], False, c == 7, [Ssc, qt], [bOh])
                    act(W["f"][:, :], bOh[:, :], AF.Square, [bOh], [W["f"]])

                def h4():
                    fq = W["f"]
                    mm(bSa[:, :], ones_f, fq[:, :], True, True, [c_f32, fq], [bSa])
                    act(fq[:, :], bSa[:, :], AF.Ln, [bSa], [fq], scale=1.0 / 128.0, bias=RMS_EPS)
                    act(fq[:, :], fq[:, :], AF.Exp, [fq], [fq], scale=-0.5)
                    stt(fq[:, :], bOh[:, :], pc[:, 5:6], fq[:, :], ALU.mult, ALU.mult, [bOh, pc, fq], [fq])
                    stt(mx[:, 1, :], fq[:, :], 0.5, W["sgB"][:, :], ALU.mult, ALU.mult, [fq, W["sgB"]], [mx])

                def fin():
                    dst = mix_send[L].ap()[t].rearrange("(g p) t -> p g t", p=128)
                    dma("sp", dst, mx[:, :, :], [mx], [d_mix_send[L][t]])
                    allgather(mix_send[L].ap()[t], mix_all[L].ap()[t], [d_mix_send[L][t]], [d_mix_all[L][t]])

                st3 += [a1, h1, a2, h2, v3, h3, h4, fin]
                return tiles, stage2, st3

            load_xT(0)
            for pi in (0, 1, 4, 2, 3):
                load_wc(pi)
            while deferred_ag:
                deferred_ag.pop(0)(extra=[xTb[0]])
            pend3 = []
            for t in range(NBLK):
                if t + 1 < NBLK:
                    load_xT(t + 1)
                tiles, stage2, st3 = make_block(t)
                for k, tile in enumerate(tiles):
                    tile()
                    if pend3:
                        pend3.pop(0)()
                while pend3:
                    pend3.pop(0)()
                stage2()
                pend3 = st3
                if t == NBLK - 1 and debug is None:
                    mixT_ap, wpc_ap = p2_front_aps()
                    tm = T(mixT_ap, "pf_mix")
                    inherit(tm, [Wc[0], Wc[1]])
                    tw = [T(wpc_ap[0], "pf_w0"), T(wpc_ap[1], "pf_w1")]
                    inherit(tw[0], [Wc[2]])
                    inherit(tw[1], [Wc[3], Wc[4]])
                    S.dma("pool", mk_mix_load(L, mixT_ap, 0), d_mix_all[L][0:7], [tm])
                    wov_ = wout.ap()[L].rearrange("(kc p) n -> p kc n", p=128)
                    for j in range(2):
                        dma("pool", wpc_ap[j], wov_[:, :, 512 * j:512 * j + 512], [], [tw[j]])
            while pend3:
                pend3.pop(0)()

            S.barrier_all()

        def phase2(L, rank_q):
            A.reset(A_BASE)
            last = L == DEPTH - 1
            x_src = xq if L == 0 else xcur
            d_xsrc = None if L == 0 else d_xcur
            x_dst = out if last else xcur
            mixT = A.bf16(16 * TQ, "mixT", (16, TQ))
            wpc = [A.bf16(16 * 512, f"wpc{i}", (16, 512)) for i in range(2)]
            xpT = A.bf16(16 * TQ, "xpT", (16, TQ))
            wpe_b = A.bf16(2 * DM, "wpe", (2, DM))
            pTb = A.bf16(2 * TQ, "pTb", (2, TQ))
            lng = A.f32(DM, "lng")
            lnb = A.f32(DM, "lnb")
            xpc = [A.f32(512, f"xpc{i}") for i in range(4)]
            rp = [A.f32(512, f"rp{i}") for i in range(2)]
            rt = [A.f32(DM, f"rt{i}") for i in range(4)]
            xb = [A.bf16(DM, f"xb{i}") for i in range(2)]
            stq = [A.f32(32, f"st{i}") for i in range(2)]
            mvq = [A.f32(8, f"mv{i}") for i in range(2)]
            xnb = [A.bf16(512, f"xnb{i}") for i in range(2)]
            thg = [A.f32(512, f"thg{i}") for i in range(2)]
            st = A.f32(32, "st")
            mv = A.f32(8, "mv")
            xnT = mixT
            xnTc = [T(None, f"xnTc{q}") for q in range(4)]

            mav = mix_all[L].ap()
            wov = wout.ap()[L].rearrange("(kc p) n -> p kc n", p=128)
            wgv = wpg.ap()[L].rearrange("(kc p) n -> p kc n", p=128)
            pieces = [(wov, n) for n in range(4)] + [(wgv, n) for n in range(4)]

            def load_piece(k):
                v, n = pieces[k]
                dma("pool", wpc[k % 2][:, :, :], v[:, :, 512 * n:512 * n + 512], [], [wpc[k % 2]])

            mixTh = [T(mixT.ap, "mixT0"), T(mixT.ap, "mixT1")]
            S.dma("pool", mk_mix_load(L, mixT.ap, 1), d_mix_all[L], [mixTh[1]])
            dma("pool", wpe_b[:, :, :], wpe.ap()[L].rearrange("(kc p) n -> p kc n", p=128), [], [wpe_b])
            dma("pool", pTb[:, :, :], pT.ap()[L].rearrange("(kc p) t -> p kc t", p=128), [], [pTb])
            dma("sp", lng[:, :], prow[L, 0:1, 640:640 + DM].to_broadcast([128, DM]), [], [lng])
            dma("sp", lnb[:, :], prow[L, 0:1, 640 + DM:640 + 2 * DM].to_broadcast([128, DM]), [], [lnb])
            cnt = 0
            itsA = [(n_, i_) for n_ in range(4) for i_ in range(8)]

            def load_xA(k):
                if k < len(itsA):
                    n_, i_ = itsA[k]
                    dma("sp", xpc[k % 4][:, :], x_src[128 * i_:128 * i_ + 128, 512 * n_:512 * n_ + 512],
                        [] if d_xsrc is None else [d_xsrc[i_]], [xpc[k % 4]])

            for k_ in range(3):
                load_xA(k_)
            for n in range(4):
                if n >= 1:
                    load_piece(n + 1)
                wp = wpc[n % 2]
                for i in range(8):
                    bk = banks[cnt % 2]
                    xp_ = xpc[cnt % 4]
                    r_ = rp[cnt % 2]
                    load_xA(cnt + 3)
                    cnt += 1
                    for kc in range(16):
                        mm(bk[:, :], mixT[:, kc, 128 * i:128 * i + 128], wp[:, kc, :], kc == 0, kc == 15, [mixTh[i // 4], wp], [bk])
                    stt(r_[:, :], xp_[:, :], ALPHA, bk[:, :], ALU.mult, ALU.add, [xp_, bk], [r_])
                    dma("sp", rscr[128 * i:128 * i + 128, 512 * n:512 * n + 512], r_[:, :], [r_], [d_rscr[i]])
            for q in range(4):
                hb = mixTh[q // 2].buf
                xnTc[q].buf.w = hb.w
                xnTc[q].buf.r = dict(hb.r)
            def b0(i):
                dma("sp", rt[i % 4][:, :], rscr[128 * i:128 * i + 128, :], [d_rscr[i]], [rt[i % 4]])

            def b1(i):
                r_, st_, mv_ = rt[i % 4], stq[i % 2], mvq[i % 2]
                for c in range(4):
                    S.op("dve", lambda e, c=c, r_=r_, st_=st_: e.bn_stats(st_[:, 6 * c:6 * c + 6], r_[:, 512 * c:512 * c + 512]), [r_], [st_])
                S.op("dve", lambda e, st_=st_, mv_=mv_: e.bn_aggr(mv_[:, 0:2], st_[:, 0:24]), [st_], [mv_])
                act(mv_[:, 2:3], mv_[:, 1:2], AF.Ln, [mv_], [mv_], bias=LN_EPS)
                act(mv_[:, 2:3], mv_[:, 2:3], AF.Exp, [mv_], [mv_], scale=-0.5)
                stt(mv_[:, 3:4], mv_[:, 0:1], -1.0, mv_[:, 2:3], ALU.mult, ALU.mult, [mv_], [mv_])

            def b2(i):
                r_, mv_ = rt[i % 4], mvq[i % 2]
                act(r_[:, :], r_[:, :], AF.Identity, [r_, mv_], [r_], scale=mv_[:, 2:3], bias=mv_[:, 3:4])
                tt("pool", r_[:, :], r_[:, :], lng[:, :], ALU.mult, [r_, lng], [r_])

            def b3(i):
                r_ = rt[i % 4]
                tt("dve", r_[:, :], r_[:, :], lnb[:, :], ALU.add, [r_, lnb], [r_])
                dma("sp", xps[128 * i:128 * i + 128, :], r_[:, :], [r_], [d_xps[i]])
                cp("act", xb[i % 2][:, :], r_[:, :], [r_], [xb[i % 2]])

            def b4(i):
                transposes_to(xb[i % 2], xpT, 128 * i, 16, 0, (4, 5) if i % 2 == 0 else (6, 7), i)

            for step in range(8 + 4):
                for stage, fn in ((0, b0), (1, b1), (2, b2), (3, b3), (4, b4)):
                    i = step - stage
                    if 0 <= i < 8:
                        fn(i)
            cnt = 0
            pend = []
            itsC = [(n_, i_) for n_ in range(4) for i_ in range(8)]

            def load_xC(k):
                if k < len(itsC):
                    n_, i_ = itsC[k]
                    dma("sp", xpc[k % 4][:, :], xps[128 * i_:128 * i_ + 128, 512 * n_:512 * n_ + 512], [d_xps[i_]], [xpc[k % 4]])

            for k_ in range(3):
                load_xC(k_)
            for n in range(4):
                k = 4 + n
                if k + 1 < 8:
                    load_piece(k + 1)
                wp = wpc[k % 2]
                for i in range(8):
                    bkg = banks[cnt % 2]
                    bke = banks[2 + cnt % 2]
                    bkt = banks[4 + cnt % 2]
                    xp_ = xpc[cnt % 4]
                    r_ = rp[cnt % 2]
                    th_ = thg[cnt % 2]
                    xnb_ = xnb[cnt % 2]
                    load_xC(cnt + 3)
                    cnt += 1
                    for kc in range(16):
                        mm(bkg[:, :], xpT[:, kc, 128 * i:128 * i + 128], wp[:, kc, :], kc == 0, kc == 15, [xpT, wp], [bkg])
                    for kc in range(2):
                        mm(bke[:, :], pTb[:, kc, 128 * i:128 * i + 128], wpe_b[:, kc, 512 * n:512 * n + 512], kc == 0, kc == 1,
                           [pTb, wpe_b], [bke])
                    act(th_[:, :], bkg[:, :], AF.Tanh, [bkg], [th_], scale=0.5)
                    stt(th_[:, :], th_[:, :], 1.0, bke[:, :], ALU.add, ALU.mult, [th_, bke], [th_])
                    stt(r_[:, :], th_[:, :], 0.5, xp_[:, :], ALU.mult, ALU.add, [th_, xp_], [r_])
                    dma("sp", x_dst[128 * i:128 * i + 128, 512 * n:512 * n + 512], r_[:, :], [r_],
                        [d_out] if last else [d_xcur[i]])
                    if not last:
                        cp("act", xnb_[:, :], r_[:, :], [r_], [xnb_])

                        def fin(n=n, i=i, xnb_=xnb_, bkt=bkt):
                            bktb = bkt.ap.bitcast(BF16)
                            for j in range(4):
                                tr(bktb[:, 128 * j:128 * j + 128], xnb_[:, 128 * j:128 * j + 128], ident_b[:, :], [xnb_, ident_b], [bkt])
                            cp("dve", xnT[:, 4 * n:4 * n + 4, 128 * i:128 * i + 128],
                               bktb[:, 0:512].rearrange("p (a b) -> p a b", a=4, b=128), [bkt], [xnTc[i // 2]])
                            if n == 3 and i % 2 == 1:
                                send_xT(L + 1, i // 2, xnT, xnTc[i // 2])
                        pend.append(fin)
                    if len(pend) > 1:
                        pend.pop(0)()
            while pend:
                pend.pop(0)()
            while pending_ag:
                pending_ag.pop(0)()
            S.barrier_all()

        rank_holder = {}
        if debug == "p0":
            phase0()
            dma("sp", dbg.ap(), xT_all[0].ap(), d_xT_all[0], [d_out])
        else:
            for L in range(DEPTH):
                phase1(L)
                if debug == "p1":
                    dma("sp", dbg.ap(), mix_send[L].ap(), d_mix_send[L], [d_out])
                    break
                phase2(L, "RANKQ")
        S.final_wait()

        with nc.Block() as block:
            @block.sync
            def _(e):
                S.emit("sp", e)

            @block.scalar
            def _(e):
                S.emit("act", e)

            @block.vector
            def _(e):
                S.emit("dve", e)

            @block.tensor
            def _(e):
                S.emit("pe", e)

            @block.gpsimd
            def _(e):
                pid = e.partition_id()
                RANK["q"] = pid % 4
                S.emit("pool", e)
    return nc


RANK = {}


_CACHE = {}


def kernel(**inputs):
    maps = _prep_inputs(inputs)
    if "nc" not in _CACHE:
        _CACHE["nc"] = build_program()
    res = run_bass_kernel_spmd(_CACHE["nc"], maps, core_ids=list(range(NCORE)))
    outp = np.zeros((2, SEQ, DM), np.float32)
    for c in range(NCORE):
        b, h = divmod(c, 4)
        outp[b, TQ * h:TQ * h + TQ, :] = np.asarray(res.results[c]["out"], np.float32)
    return outp
```

```python
import contextlib
import numpy as np
import concourse.bass as bass
import concourse.mybir as mybir
from concourse.bass_utils import run_bass_kernel_spmd

F32 = mybir.dt.float32
BF16 = mybir.dt.bfloat16
AF = mybir.ActivationFunctionType
ALU = mybir.AluOpType

DEPTH = 2
DM = 2048
SEQ = 4096
NCORE = 8
TQ = 1024
NBLK = 8
NWC = 2304
ALPHA = (2 * DEPTH) ** 0.25
LN_EPS = 1e-5
RMS_EPS = 1e-6
F_FLOOR = 1e-30
GELU_C = 0.7978845608028654
NROW = 256 + 3 * 128 + 2 * DM
AW = 51200
SAME_ENG_RAW = True
CC_QOS = "P2"
DBG = {"nblk": NBLK, "parts": {"hprep", "conv", "sg", "attn", "hmm"}, "skip": set()}

FM_BQ, FM_BF, FM_AB, FM_AC, FM_AX, FM_DU, FM_GA, FM_GB, FM_GC, FM_GD, FM_CQ, FM_CK = range(12)
FM_SRC = {FM_BQ: 3, FM_BF: 4, FM_AB: 0, FM_AC: 1, FM_AX: 2, FM_DU: 9, FM_GA: 11, FM_GB: 12, FM_GC: 13,
          FM_GD: 14, FM_CQ: 6, FM_CK: 7}


def _consts():
    c = np.zeros((128, 128 * 4 + 512), np.float32)
    i = np.arange(128)
    c[:, 0:128] = np.eye(128, dtype=np.float32)
    c[:, 128:256] = (i[:, None] <= i[None, :]).astype(np.float32)
    c[:, 256:384] = ((i[:, None] <= i[None, :]) & ((i[:, None] // 64) == (i[None, :] // 64))).astype(np.float32)
    c[:, 384:512] = 1.0
    r = np.ones(512, np.float32)
    r[::64] = 0.0
    c[:, 512:1024] = r[None, :]
    return c


def _prep_inputs(inp):
    x = np.asarray(inp["x"], np.float32)
    p = np.asarray(inp["p"], np.float32)
    w_in = np.asarray(inp["w_in"], np.float32)
    w_out = np.asarray(inp["w_out"], np.float32)
    perm = np.array([g * 512 + 128 * hh + i for hh in range(4) for g in range(4) for i in range(128)])
    wout_p = np.ascontiguousarray(w_out[:, perm, :])
    wpg = np.ascontiguousarray(np.asarray(inp["w_pg"], np.float32))
    wpe = np.ascontiguousarray(np.asarray(inp["w_pe"], np.float32))
    cst = _consts()
    xTs = [np.ascontiguousarray(x[b].T) for b in range(2)]
    maps = []
    for c in range(NCORE):
        b, h = divmod(c, 4)
        sl = slice(128 * h, 128 * h + 128)
        cols = []
        for f in range(12):
            blk = FM_SRC[f]
            cols.append(np.arange(blk * 512 + 128 * h, blk * 512 + 128 * h + 128))
        cols.append(np.arange(5 * 512 + 128 * h, 5 * 512 + 128 * h + 128))
        cols.append(np.arange(8 * 512 + 128 * h, 8 * 512 + 128 * h + 128))
        for g in range(4):
            gg = (h + g) % 4
            cols.append(np.arange(10 * 512 + 128 * gg, 10 * 512 + 128 * gg + 128))
        cols = np.concatenate(cols)
        wc = np.ascontiguousarray(w_in[:, :, cols])
        pcol = np.zeros((DEPTH, 128, 8), np.float32)
        prow = np.zeros((DEPTH, 1, NROW), np.float32)
        sgwT = np.zeros((DEPTH, 128, 128), np.float32)
        for i in range(DEPTH):
            pcol[i, :, 0:3] = np.asarray(inp["conv_w"])[i][:, sl].T
            pcol[i, :, 3] = np.asarray(inp["hgrn_lb"])[0, sl]
            pcol[i, :, 4] = np.asarray(inp["hgrn_lb"])[1, sl]
            pcol[i, :, 5] = np.asarray(inp["hgrn_norm_g"])[i, sl]
            pcol[i, :, 6] = np.asarray(inp["diff_norm_g"])[i, sl]
            prow[i, 0, 0:256] = np.asarray(inp["diff_lambda"])[i].reshape(256)
            prow[i, 0, 256:384] = np.asarray(inp["sg_b"])[i, h]
            prow[i, 0, 384:512] = np.asarray(inp["sg_ln_g"])[i, sl]
            prow[i, 0, 512:640] = np.asarray(inp["sg_ln_b"])[i, sl]
            prow[i, 0, 640:640 + DM] = np.asarray(inp["ln_g"])[i]
            prow[i, 0, 640 + DM:640 + 2 * DM] = np.asarray(inp["ln_b"])[i]
            sgwT[i] = np.asarray(inp["sg_w"])[i, h].T
        maps.append({
            "xq": np.ascontiguousarray(x[b, TQ * h:TQ * h + TQ, :]),
            "xTf": xTs[b],
            "pT": np.ascontiguousarray(np.transpose(p[:, b, TQ * h:TQ * h + TQ, :], (0, 2, 1))),
            "wc": wc, "wout": wout_p, "wpg": wpg, "wpe": wpe,
            "pcol": pcol, "prow": prow, "sgwT": sgwT, "cst": cst,
        })
    return maps


class Buf:
    __slots__ = ("name", "w", "r", "excl")

    def __init__(self, name):
        self.name = name
        self.w = None
        self.r = {}
        self.excl = False


class T:
    __slots__ = ("ap", "buf")

    def __init__(self, ap, name="t", buf=None):
        self.ap = ap
        self.buf = buf if buf is not None else Buf(name)

    def __getitem__(self, k):
        return self.ap[k]


class Sched:
    ENGS = ("pe", "act", "dve", "pool", "sp")

    def __init__(self, esem, dsem):
        self.esem = esem
        self.prog = {e: [] for e in self.ENGS}
        self.cnt = {e: 0 for e in esem}
        self.waited = {e: {} for e in self.ENGS}
        self.dq = {q: {"sems": s, "use": [0] * len(s), "next": 0} for q, s in dsem.items()}

    def _need(self, eng, tok, raw):
        if tok is None:
            return
        sem, val, src = tok
        if src == eng:
            if eng == "pe" or not raw or not SAME_ENG_RAW:
                return
        w = self.waited[eng]
        k = id(sem)
        if w.get(k, 0) >= val:
            return
        w[k] = val
        self.prog[eng].append(("wait", sem, val))

    def _deps(self, eng, reads, writes):
        for t in reads:
            self._need(eng, t.buf.w, True)
            if t.buf.excl:
                for tok in t.buf.r.values():
                    self._need(eng, tok, False)
        for t in writes:
            b = t.buf
            self._need(eng, b.w, False)
            for tok in b.r.values():
                self._need(eng, tok, False)

    def _commit(self, tok, reads, writes):
        for t in reads:
            t.buf.r[id(tok[0])] = tok
        for t in writes:
            t.buf.w = tok
            t.buf.r = {}

    def op(self, eng, fn, reads=(), writes=()):
        self._deps(eng, reads, writes)
        self.cnt[eng] += 1
        sem = self.esem[eng]
        tok = (sem, self.cnt[eng], eng)
        self.prog[eng].append(("op", fn, sem, 1))
        self._commit(tok, reads, writes)

    def dma(self, q, fn, reads=(), writes=()):
        self._deps(q, reads, writes)
        d = self.dq[q]
        i = d["next"]
        d["next"] = (i + 1) % len(d["sems"])
        sem = d["sems"][i]
        k = d["use"][i]
        if k > 0:
            self._need(q, (sem, 16 * k, "dma"), True)
        d["use"][i] = k + 1
        tok = (sem, 16 * (k + 1), "dma")
        self.prog[q].append(("op", fn, sem, 16))
        self._commit(tok, reads, writes)

    def barrier_all(self):
        toks = [(self.esem[e], self.cnt[e], e) for e in self.esem if self.cnt[e] > 0]
        for q, d in self.dq.items():
            for sem, k in zip(d["sems"], d["use"]):
                if k > 0:
                    toks.append((sem, 16 * k, "dma"))
        for e in self.ENGS:
            for tok in toks:
                if tok[2] == e and e == "pe":
                    continue
                self._need(e, tok, True)

    def final_wait(self):
        for q, d in self.dq.items():
            for sem, k in zip(d["sems"], d["use"]):
                if k > 0:
                    self._need(q, (sem, 16 * k, "dma"), True)

    def emit(self, eng, e):
        for item in self.prog[eng]:
            if item[0] == "wait":
                e.wait_ge(item[1], item[2])
            else:
                ins = item[1](e)
                ins.then_inc(item[2], item[3])


class Arena:
    def __init__(self, ap, width):
        self.base = ap
        self.width = width
        self.off = 0

    def reset(self, off=0):
        self.off = off

    def f32(self, n, name="t", shape=None):
        n2 = (n + 7) // 8 * 8
        assert self.off + n2 <= self.width, f"arena overflow at {name}: {self.off}+{n2}>{self.width}"
        ap = self.base[:, self.off:self.off + n]
        self.off += n2
        if shape is not None:
            ap = ap.rearrange("p (a b) -> p a b", a=shape[0], b=shape[1])
        return T(ap, name)

    def bf16(self, n, name="t", shape=None):
        nw = (n + 1) // 2
        n2 = (nw + 7) // 8 * 8
        assert self.off + n2 <= self.width, f"arena overflow at {name}: {self.off}+{n2}>{self.width}"
        ap = self.base[:, self.off:self.off + nw].bitcast(BF16)
        self.off += n2
        if shape is not None:
            ap = ap.rearrange("p (a b) -> p a b", a=shape[0], b=shape[1])
        return T(ap, name)


def build_program(debug=None):
    nc = bass.Bass("TRN2", target_bir_lowering=False)
    dt_in = lambda n, s: nc.dram_tensor(n, s, F32, kind="ExternalInput")
    xq = dt_in("xq", [TQ, DM])
    xTf = dt_in("xTf", [DM, SEQ])
    pT = dt_in("pT", [DEPTH, 256, TQ])
    wc = dt_in("wc", [DEPTH, DM, NWC])
    wout = dt_in("wout", [DEPTH, DM, DM])
    wpg = dt_in("wpg", [DEPTH, DM, DM])
    wpe = dt_in("wpe", [DEPTH, 256, DM])
    pcol = dt_in("pcol", [DEPTH, 128, 8])
    prow = dt_in("prow", [DEPTH, 1, NROW])
    sgwT = dt_in("sgwT", [DEPTH, 128, 128])
    cst = dt_in("cst", [128, 1024])
    out = nc.dram_tensor("out", [TQ, DM], F32, kind="ExternalOutput")
    dbg = None
    if debug == "p1":
        dbg = nc.dram_tensor("dbg", [NBLK, 512, 512], BF16, kind="ExternalOutput")
    if debug == "p0":
        dbg = nc.dram_tensor("dbg", [4, 4 * DM, 256], BF16, kind="ExternalOutput")
    xT_send = [nc.dram_tensor(f"xT_send{i}", [4, DM, 256], BF16) for i in range(DEPTH)]
    xT_all = [nc.dram_tensor(f"xT_all{i}", [4, 4 * DM, 256], BF16) for i in range(DEPTH)]
    mix_send = [nc.dram_tensor(f"mix_send{i}", [NBLK, 512, 512], BF16) for i in range(DEPTH)]
    mix_all = [nc.dram_tensor(f"mix_all{i}", [NBLK, 4 * 512, 512], BF16) for i in range(DEPTH)]
    xcur = nc.dram_tensor("xcur", [TQ, DM], F32)
    rscr = nc.dram_tensor("rscr", [TQ, DM], F32)
    xps = nc.dram_tensor("xps", [TQ, DM], F32)
    GROUPS = [[0, 1, 2, 3], [4, 5, 6, 7]]

    with contextlib.ExitStack() as es:
        esem = {e: es.enter_context(nc.semaphore(f"s_{e}")) for e in ("pe", "act", "dve", "pool")}
        dsem = {"sp": [es.enter_context(nc.semaphore(f"d_sp{i}")) for i in range(12)],
                "pool": [es.enter_context(nc.semaphore(f"d_pl{i}")) for i in range(6)]}
        ccsem = [es.enter_context(nc.semaphore(f"cc{i}")) for i in range(12 * DEPTH)]
        arena_t = es.enter_context(nc.sbuf_tensor("arena", [128, AW], F32))
        banks = [T(es.enter_context(nc.psum_tensor(f"bank{i}", [128, 512], F32))[:, :], f"bank{i}") for i in range(8)]
        for bk_ in banks:
            bk_.buf.excl = True
        S = Sched(esem, dsem)
        A = Arena(arena_t[:, :], AW)

        d_xT_send = [[T(None, "xTs") for _ in range(4)] for _ in range(DEPTH)]
        d_xT_all = [[T(None, "xTa") for _ in range(4)] for _ in range(DEPTH)]
        d_mix_send = [[T(None, "mxs") for _ in range(NBLK)] for _ in range(DEPTH)]
        d_mix_all = [[T(None, "mxa") for _ in range(NBLK)] for _ in range(DEPTH)]
        d_xcur = [T(None, "xcur") for _ in range(8)]
        d_rscr = [T(None, "rscr") for _ in range(8)]
        d_xps = [T(None, "xps") for _ in range(8)]
        d_out = T(None, "out")
        cc_count = [0]

        def mm(o, l, r, start, stop, reads, writes):
            S.op("pe", lambda e, o=o, l=l, r=r, st=start, sp=stop: e.matmul(o, l, r, start=st, stop=sp), reads, writes)

        def tr(o, i, ident, reads, writes):
            S.op("pe", lambda e, o=o, i=i, d=ident: e.transpose(o, i, d), reads, writes)

        def act(o, i, func, reads, writes, scale=None, bias=None):
            kw = {}
            if scale is not None:
                kw["scale"] = scale
            if bias is not None:
                kw["bias"] = bias
            S.op("act", lambda e, o=o, i=i, f=func, kw=kw: e.activation(o, i, f, **kw), reads, writes)

        def tt(eng, o, a, b, op, reads, writes):
            S.op(eng, lambda e, o=o, a=a, b=b, op=op: e.tensor_tensor(o, a, b, op), reads, writes)

        def ts(eng, o, a, s1, s2, op0, op1, reads, writes):
            if op1 is None:
                S.op(eng, lambda e, o=o, a=a, s1=s1, op0=op0: e.tensor_scalar(o, a, s1, None, op0), reads, writes)
            else:
                S.op(eng, lambda e, o=o, a=a, s1=s1, s2=s2, op0=op0, op1=op1: e.tensor_scalar(o, a, s1, s2, op0, op1), reads, writes)

        def stt(o, a, s, b, op0, op1, reads, writes):
            S.op("dve", lambda e, o=o, a=a, s=s, b=b, op0=op0, op1=op1: e.scalar_tensor_tensor(o, a, s, b, op0, op1), reads, writes)

        def cp(eng, o, i, reads, writes):
            if eng == "act":
                act(o, i, AF.Copy, reads, writes)
            else:
                S.op(eng, lambda e, o=o, i=i: e.tensor_copy(o, i), reads, writes)

        def dma(q, o, i, reads, writes):
            S.dma(q, lambda e, o=o, i=i: e.dma_start(out=o, in_=i), reads, writes)

        def allgather(src, dst, r, w):
            k = cc_count[0]
            cc_count[0] += 1
            sem = ccsem[k]
            S._deps("pool", r, w)
            S.prog["pool"].append(("op", lambda e, s=src, d=dst: e.collective_compute(
                "AllGather", ALU.bypass, replica_groups=GROUPS, ins=[s.opt()], outs=[d.opt()], dma_qos=CC_QOS), sem, 1))
            tok = (sem, 1, "cc")
            S._commit(tok, r, w)

        c_f32 = A.f32(1024, "cst")
        ident_b = A.bf16(128, "ident")
        tri_b = A.bf16(128, "tri")
        hmask_b = A.bf16(128, "hmask")
        ones_b = A.bf16(128, "ones_b")
        dma("sp", c_f32[:, :], cst[:, :], [], [c_f32])
        cp("dve", ident_b[:, :], c_f32[:, 0:128], [c_f32], [ident_b])
        cp("dve", tri_b[:, :], c_f32[:, 128:256], [c_f32], [tri_b])
        cp("dve", hmask_b[:, :], c_f32[:, 256:384], [c_f32], [hmask_b])
        cp("dve", ones_b[:, :], c_f32[:, 384:512], [c_f32], [ones_b])
        tri_f = c_f32[:, 128:256]
        ones_f = c_f32[:, 384:512]
        rmask_f = c_f32[:, 512:1024]
        A_BASE = A.off

        def transposes_to(xb, dstT, col0, nchunk, kc0, bank_pair, flip, dep=None):
            for half in range((nchunk + 7) // 8):
                bk = banks[bank_pair[half % 2]]
                bkb = bk.ap.bitcast(BF16)
                n_here = min(8, nchunk - 8 * half)
                for j in range(n_here):
                    jj = 8 * half + j
                    tr(bkb[:, j * 128:(j + 1) * 128], xb[:, jj * 128:(jj + 1) * 128], ident_b[:, :], [xb, ident_b], [bk])
                eng = "act" if (half + flip) % 2 == 0 else "dve"
                o = dstT[:, kc0 + 8 * half:kc0 + 8 * half + n_here, col0:col0 + 128]
                i = bkb[:, 0:n_here * 128].rearrange("p (a b) -> p a b", a=n_here, b=128)
                cp(eng, o, i, [bk], [dstT if dep is None else dep])

        def inherit(dst, srcs):
            r = {}
            for s_ in srcs:
                toks = list(s_.buf.r.values()) + ([s_.buf.w] if s_.buf.w is not None else [])
                for tok in toks:
                    k_ = id(tok[0])
                    if k_ not in r or r[k_][1] < tok[1]:
                        r[k_] = tok
            dst.buf.w = None
            dst.buf.r = r

        def mk_mix_load(L, dst_ap, half):
            def fn(e):
                rq = RANK["q"]
                src = mix_all[L].ap()[bass.ds(rq * 2 + half, 1), :, :].rearrange("o (c p) t -> p (o c) t", p=128)
                return e.dma_start(out=dst_ap[:, :, 512 * half:512 * half + 512], in_=src)
            return fn

        def p2_front_aps():
            mixT_ap = A.base[:, A_BASE:A_BASE + 8192].bitcast(BF16).rearrange("p (a b) -> p a b", a=16, b=TQ)
            wpc_ap = [A.base[:, A_BASE + 8192 + 4096 * j:A_BASE + 8192 + 4096 * (j + 1)].bitcast(BF16).rearrange(
                "p (a b) -> p a b", a=16, b=512) for j in range(2)]
            return mixT_ap, wpc_ap

        deferred_ag = []
        pending_ag = []

        def send_xT(L, q, xT, dep):
            dma("pool", xT_send[L].ap()[q].rearrange("(kc p) t -> p kc t", p=128), xT[:, :, 256 * q:256 * q + 256],
                [dep], [d_xT_send[L][q]])
            ag = lambda extra=(), L=L, q=q: allgather(xT_send[L].ap()[q], xT_all[L].ap()[q],
                                                      [d_xT_send[L][q]] + list(extra), [d_xT_all[L][q]])
            if debug == "p0":
                ag()
            elif q < 2:
                pending_ag.append(ag)
            else:
                deferred_ag.append(ag)

        def phase0():
            A.reset(A_BASE)
            xT = A.bf16(16 * TQ, "xT", (16, TQ))
            xbs = [A.bf16(DM, f"xb{i}") for i in range(2)]
            xTc = [T(None, f"xTc{q}") for q in range(4)]
            for i in range(8):
                xb = xbs[i % 2]
                q = i // 2
                dma("pool", xb[:, :], xq[128 * i:128 * i + 128, :], [], [xb])
                transposes_to(xb, xT, 128 * i, 16, 0, (0, 1) if i % 2 == 0 else (2, 3), i, dep=xTc[q])
                if i % 2 == 1:
                    send_xT(0, q, xT, xTc[q])
            S.barrier_all()

        def phase1(L):
            A.reset(A_BASE)
            lam_init = 0.8 - 0.6 * float(np.exp(-0.3 * L))
            WC_PIECES = [(0, 512), (512, 1024), (1024, 1536), (1536, 1792), (1792, 2304)]
            Wc = [A.bf16(16 * (b - a), f"Wc{a}", (16, b - a)) for a, b in WC_PIECES]
            wcv = wc.ap()[L].rearrange("(kc p) n -> p kc n", p=128)
            def load_wc(pi):
                a, b = WC_PIECES[pi]
                dma("pool", Wc[pi][:, :, :], wcv[:, :, a:b], [], [Wc[pi]])

            def wslice(col0, ncol):
                for pi, (a, b) in enumerate(WC_PIECES):
                    if a <= col0 and col0 + ncol <= b:
                        return Wc[pi], (lambda kc, pi=pi, a=a: Wc[pi][:, kc, col0 - a:col0 - a + ncol])
                raise AssertionError("w slice crosses pieces")

            KT = A.bf16(SEQ, "KT")
            KTb = [T(KT.ap, f"KT{t}") for t in range(NBLK)]
            Vt = A.bf16(32 * 128, "V", (32, 128))
            Vb = [T(Vt.ap, f"V{t}") for t in range(NBLK)]
            xTb = [A.bf16(16 * 512, f"xTb{i}", (16, 512)) for i in range(2)]
            mixb = A.bf16(4 * 512, "mixb", (4, 512))
            Sst = A.f32(128, "Sst")
            pc = A.f32(8, "pc")
            lamrow = A.f32(256, "lamrow")
            lamtmp = A.f32(128, "lamtmp")
            sc = A.f32(16, "sc")
            WTm = A.bf16(128, "WTm")
            sgw_f = A.f32(128, "sgw_f")
            bsb = A.f32(128, "bsb")
            lng = A.f32(128, "lng")
            lnb = A.f32(128, "lnb")
            QT = A.bf16(512, "QT")
            qt = A.bf16(512, "qt")
            kt = A.bf16(512, "kt")
            kdT = A.bf16(512, "kdT")
            kd_tm = A.bf16(512, "kd_tm", (4, 128))
            iv_tm = A.bf16(512, "iv_tm", (4, 128))
            am = A.bf16(512, "am", (4, 128))
            Ssc = A.bf16(1024, "Ssc", (8, 128))
            vn = [A.bf16(128, f"vn{i}") for i in range(4)]
            dv2s = A.f32(128, "dv2s")
            P1 = [A.bf16(512, f"P1_{i}") for i in range(2)]
            P2 = [A.bf16(512, f"P2_{i}") for i in range(2)]
            ebm = A.f32(8, "ebm")
            ebl = A.f32(8, "ebl")
            st6 = A.f32(8, "st6")
            mv = A.f32(8, "mv")
            zc = A.f32(516, "zc")
            W = {n: A.f32(512, n) for n in (
                "q_sb", "th", "f", "km", "bT", "E", "ek", "D2",
                "ab", "ac", "acc", "du", "u1", "u2", "dv1",
                "sgA", "sgB", "sgC", "sgD", "thg", "dvs0", "dvs1", "dvs2", "dvs3")}
            dvs = [W[f"dvs{i}"] for i in range(4)]

            dma("sp", pc[:, :], pcol[L, :, :], [], [pc])
            dma("sp", lamrow[:, :], prow[L, 0:1, 0:256].to_broadcast([128, 256]), [], [lamrow])
            dma("sp", sgw_f[:, :], sgwT[L, :, :], [], [sgw_f])
            dma("sp", bsb[:, :], prow[L, 0:1, 256:384].to_broadcast([128, 128]), [], [bsb])
            dma("sp", lng[:, :], prow[L, 0:1, 384:512].to_broadcast([128, 128]), [], [lng])
            dma("sp", lnb[:, :], prow[L, 0:1, 512:640].to_broadcast([128, 128]), [], [lnb])
            if L == 0:
                S.op("dve", lambda e: e.memset(sc[:, 0:1], 0.0), [], [sc])
            else:
                tt("dve", sc[:, 0:1], pc[:, 4:5], pc[:, 3:4], ALU.subtract, [pc], [sc])
                act(sc[:, 0:1], sc[:, 0:1], AF.Tanh, [sc], [sc], scale=0.5)
                ts("dve", sc[:, 0:1], sc[:, 0:1], 0.5, 0.5, ALU.mult, ALU.add, [sc], [sc])
            ts("dve", sc[:, 1:2], sc[:, 0:1], -0.5, 0.5, ALU.mult, ALU.add, [sc], [sc])
            ts("dve", sc[:, 2:3], sc[:, 0:1], 0.5, 0.5, ALU.mult, ALU.add, [sc], [sc])
            ts("dve", sc[:, 3:4], sc[:, 0:1], 0.5, -0.5, ALU.mult, ALU.add, [sc], [sc])
            tt("dve", lamtmp[:, 0:64], lamrow[:, 0:64], lamrow[:, 64:128], ALU.mult, [lamrow], [lamtmp])
            tt("dve", lamtmp[:, 64:128], lamrow[:, 128:192], lamrow[:, 192:256], ALU.mult, [lamrow], [lamtmp])
            S.op("dve", lambda e: e.tensor_reduce(sc[:, 6:8], lamtmp[:, :].rearrange("p (a b) -> p a b", a=2, b=64),
                                                  mybir.AxisListType.X, ALU.add), [lamtmp], [sc])
            act(sc[:, 8:10], sc[:, 6:8], AF.Exp, [sc], [sc])
            tt("dve", sc[:, 4:5], sc[:, 9:10], sc[:, 8:9], ALU.subtract, [sc], [sc])
            ts("dve", sc[:, 4:5], sc[:, 4:5], -lam_init, None, ALU.add, None, [sc], [sc])
            ts("dve", sc[:, 5:6], pc[:, 6:7], 1.0 - lam_init, None, ALU.mult, None, [pc], [sc])
            tt("dve", WTm[:, :], sgw_f[:, :], tri_f, ALU.mult, [sgw_f, c_f32], [WTm])
            S.op("dve", lambda e: e.memset(Sst[:, :], 0.0), [], [Sst])
            S.op("dve", lambda e: e.memset(zc[:, 0:2], 0.0), [], [zc])

            pbank = [0]

            def next_pbank():
                b = banks[pbank[0] % 2]
                pbank[0] += 1
                return b

            def load_xT(t):
                r, half = divmod(t, 2)
                if L == 0:
                    src = xTf.ap()[:, 512 * t:512 * t + 512].rearrange("(kc p) t -> p kc t", p=128)
                    dma("pool", xTb[t % 2][:, :, :], src, [], [xTb[t % 2]])
                    return
                for qq in range(2):
                    q = 2 * half + qq
                    src = xT_all[L].ap()[q][r * DM:(r + 1) * DM, :].rearrange("(kc p) t -> p kc t", p=128)
                    dma("sp", xTb[t % 2][:, :, 256 * qq:256 * qq + 256], src, [d_xT_all[L][q]], [xTb[t % 2]])


            def make_block(t):
                xt = xTb[t % 2]
                mx = mixb
                c0 = 512 * t
                tiles, st3 = [], []
                ch_h, ch_c, ch_u, ch_v = [], [], [], []

                def proj_fm(f):
                    bk = next_pbank()
                    wT, wfn = wslice(128 * f, 128)
                    for kc in range(16):
                        mm(bk[:, :], wfn(kc), xt[:, kc, :], kc == 0, kc == 15, [wT, xt], [bk])
                    return bk

                def proj_tm(tt_i, col0, ncol):
                    bk = next_pbank()
                    wT, wfn = wslice(col0, ncol)
                    for kc in range(16):
                        mm(bk[:, 0:ncol], xt[:, kc, 128 * tt_i:128 * tt_i + 128], wfn(kc), kc == 0, kc == 15, [wT, xt], [bk])
                    return bk

                def t_copy(f, dst):
                    def fn():
                        bk = proj_fm(f)
                        cp("act", dst[:, :], bk[:, :], [bk], [dst])
                    return fn

                def t_bf():
                    bk = proj_fm(FM_BF)
                    act(W["th"][:, :], bk[:, :], AF.Tanh, [bk], [W["th"]], scale=0.5)

                def t_ax():
                    bk = proj_fm(FM_AX)
                    tt("dve", zc[:, 2:514], bk[:, :], W["ac"][:, :], ALU.mult, [bk, W["ac"]], [zc])

                def t_gate(f, dst):
                    def fn():
                        bk = proj_fm(f)
                        act(W["thg"][:, :], bk[:, :], AF.Tanh, [bk], [W["thg"]], scale=0.5)
                        stt(dst[:, :], W["thg"][:, :], 1.0, bk[:, :], ALU.add, ALU.mult, [W["thg"], bk], [dst])
                    return fn

                def t_dv(i):
                    def fn():
                        bk = proj_tm(i, 1792, 512)
                        cp("act", dvs[i][:, :], bk[:, :], [bk], [dvs[i]])
                    return fn

                def t_bicv(i):
                    def fn():
                        bk = proj_tm(i, 1536, 256)
                        cp("act", iv_tm[:, i, :], bk[:, 0:128], [bk], [iv_tm])
                        cp("dve", Vt[:, 4 * t + i, :], bk[:, 128:256], [bk], [Vb[t]])
                    return fn

                def t_ck():
                    bk = proj_fm(FM_CK)
                    cp("act", KT[:, c0:c0 + 512], bk[:, :], [bk], [KTb[t]])

                def t_cq():
                    bk = proj_fm(FM_CQ)
                    cp("dve", QT[:, :], bk[:, :], [bk], [QT])

                tiles += [t_copy(FM_BQ, W["q_sb"]), t_bf, t_copy(FM_AB, W["ab"]), t_copy(FM_AC, W["ac"]), t_ax,
                          t_copy(FM_DU, W["du"])]
                tiles += [t_dv(i) for i in range(4)]
                tiles += [t_gate(FM_GA, W["sgA"]), t_gate(FM_GB, W["sgB"]), t_gate(FM_GD, W["sgD"]), t_gate(FM_GC, W["sgC"])]
                tiles += [t_bicv(i) for i in range(4)]
                tiles += [t_ck, t_cq]

                th, f_, km, bT, E, ek, D2 = (W[n] for n in ("th", "f", "km", "bT", "E", "ek", "D2"))
                bT3 = bT[:, :].rearrange("p (a b) -> p a b", a=8, b=64)
                E3 = E[:, :].rearrange("p (a b) -> p a b", a=8, b=64)
                D23 = D2[:, :].rearrange("p (a b) -> p a b", a=8, b=64)
                H = ch_h.append
                H(lambda: ts("dve", f_[:, :], th[:, :], sc[:, 1:2], sc[:, 2:3], ALU.mult, ALU.add, [th, sc], [f_]))
                H(lambda: ts("dve", km[:, :], th[:, :], sc[:, 3:4], sc[:, 1:2], ALU.mult, ALU.add, [th, sc], [km]))
                H(lambda: S.op("dve", lambda e: e.tensor_scalar_max(f_[:, :], f_[:, :], F_FLOOR), [f_], [f_]))
                H(lambda: act(f_[:, :], f_[:, :], AF.Ln, [f_], [f_]))
                H(lambda: S.op("dve", lambda e: e.tensor_tensor_scan(bT[:, :], rmask_f, f_[:, :], 0.0, ALU.mult, ALU.add),
                               [f_, c_f32], [bT]))
                H(lambda: tt("dve", E3, bT3, bT3[:, :, 31:32].to_broadcast([128, 8, 64]), ALU.subtract, [bT], [E]))
                H(lambda: tt("dve", D23, bT3[:, :, 63:64].to_broadcast([128, 8, 64]), bT3, ALU.subtract, [bT], [D2]))
                H(lambda: act(ek[:, :], E[:, :], AF.Exp, [E], [ek], scale=-1.0))
                H(lambda: act(E[:, :], E[:, :], AF.Exp, [E], [E]))
                H(lambda: act(D2[:, :], D2[:, :], AF.Exp, [D2], [D2]))
                H(lambda: act(ebm[:, :], bT3[:, :, 31], AF.Exp, [bT], [ebm]))
                H(lambda: act(ebl[:, :], bT3[:, :, 63], AF.Exp, [bT], [ebl]))
                H(lambda: tt("dve", qt[:, :], W["q_sb"][:, :], E[:, :], ALU.mult, [W["q_sb"], E], [qt]))
                H(lambda: tt("dve", kt[:, :], km[:, :], ek[:, :], ALU.mult, [km, ek], [kt]))
                H(lambda: tt("dve", kdT[:, :], km[:, :], D2[:, :], ALU.mult, [km, D2], [kdT]))

                acc = W["acc"]
                C = ch_c.append
                C(lambda: ts("dve", acc[:, :], zc[:, 0:512], pc[:, 0:1], None, ALU.mult, None, [zc, pc], [acc]))
                C(lambda: stt(acc[:, :], zc[:, 1:513], pc[:, 1:2], acc[:, :], ALU.mult, ALU.add, [zc, pc, acc], [acc]))
                C(lambda: stt(acc[:, :], zc[:, 2:514], pc[:, 2:3], acc[:, :], ALU.mult, ALU.add, [zc, pc, acc], [acc]))
                C(lambda: tt("dve", acc[:, :], acc[:, :], W["ab"][:, :], ALU.mult, [acc, W["ab"]], [acc]))
                C(lambda: stt(mx[:, 0, :], acc[:, :], 0.5, W["sgA"][:, :], ALU.mult, ALU.mult, [acc, W["sgA"]], [mx]))
                C(lambda: S.op("dve", lambda e: e.tensor_copy(zc[:, 0:2], zc[:, 512:514]), [zc], [zc]))

                du, u1, u2 = W["du"], W["u1"], W["u2"]
                U = ch_u.append
                U(lambda: act(u1[:, :], du[:, :], AF.Square, [du], [u1]))
                U(lambda: ts("dve", u1[:, :], u1[:, :], 0.044715, 1.0, ALU.mult, ALU.add, [u1], [u1]))
                U(lambda: tt("dve", u1[:, :], u1[:, :], du[:, :], ALU.mult, [u1, du], [u1]))
                U(lambda: act(u1[:, :], u1[:, :], AF.Tanh, [u1], [u1], scale=GELU_C))
                U(lambda: stt(u2[:, :], u1[:, :], 1.0, du[:, :], ALU.add, ALU.mult, [u1, du], [u2]))

                dv1, dv2 = W["dv1"], dv2s
                V = ch_v.append
                for i in range(4):
                    d_ = dvs[i]
                    vnt = vn[i]
                    V(lambda d_=d_: act(dv1[:, :], d_[:, :], AF.Square, [d_], [dv1]))
                    V(lambda: ts("dve", dv1[:, :], dv1[:, :], 0.044715, 1.0, ALU.mult, ALU.add, [dv1], [dv1]))
                    V(lambda d_=d_: tt("dve", dv1[:, :], dv1[:, :], d_[:, :], ALU.mult, [dv1, d_], [dv1]))
                    V(lambda: act(dv1[:, :], dv1[:, :], AF.Tanh, [dv1], [dv1], scale=GELU_C))
                    V(lambda d_=d_: stt(d_[:, :], dv1[:, :], 1.0, d_[:, :], ALU.add, ALU.mult, [dv1, d_], [d_]))
                    V(lambda d_=d_: S.op("dve", lambda e: e.bn_stats(st6[:, 0:6], d_[:, :]), [d_], [st6]))
                    V(lambda: S.op("dve", lambda e: e.bn_aggr(mv[:, 0:2], st6[:, 0:6]), [st6], [mv]))
                    V(lambda: act(mv[:, 2:3], mv[:, 1:2], AF.Ln, [mv], [mv], scale=0.25, bias=LN_EPS))
                    V(lambda: act(mv[:, 2:3], mv[:, 2:3], AF.Exp, [mv], [mv], scale=-0.5))
                    V(lambda: ts("dve", mv[:, 3:4], mv[:, 2:3], 0.5, None, ALU.mult, None, [mv], [mv]))
                    V(lambda: stt(mv[:, 4:5], mv[:, 0:1], -1.0, mv[:, 3:4], ALU.mult, ALU.mult, [mv], [mv]))
                    V(lambda d_=d_: ts("dve", dv2[:, :], d_[:, 0:128], mv[:, 3:4], mv[:, 4:5], ALU.mult, ALU.add, [d_, mv], [dv2]))
                    V(lambda: tt("dve", dv2[:, :], dv2[:, :], lng[:, :], ALU.mult, [dv2, lng], [dv2]))
                    V(lambda vnt=vnt: tt("dve", vnt[:, :], dv2[:, :], lnb[:, :], ALU.add, [dv2, lnb], [vnt]))

                chain = []
                srcs = [ch_h, ch_v, ch_c, ch_u]
                while any(srcs):
                    for s_ in srcs:
                        if s_:
                            chain.append(s_.pop(0))
                    if ch_v:
                        chain.append(ch_v.pop(0))

                bS1, bS2, bO1, bO2, bZ1, bZ2 = banks[2], banks[3], banks[4], banks[5], banks[6], banks[7]
                nkb = 4 * (t + 1)

                def s_mm(kb):
                    d = kb - 4 * t
                    q0 = 128 * d if d > 0 else 0
                    kr = [KTb[kb // 4], QT]
                    mm(bS1[:, q0:512], KT[0:64, 128 * kb:128 * kb + 128], QT[0:64, q0:512], True, True, kr, [bS1])
                    mm(bS2[:, q0:512], KT[64:128, 128 * kb:128 * kb + 128], QT[64:128, q0:512], True, True, kr, [bS2])

                def exp_pv(kb):
                    d = kb - 4 * t
                    q0 = 128 * d if d > 0 else 0
                    p1, p2 = P1[kb % 2], P2[kb % 2]
                    act(p1[:, q0:512], bS1[:, q0:512], AF.Exp, [bS1], [p1], scale=0.125)
                    act(p2[:, q0:512], bS2[:, q0:512], AF.Exp, [bS2], [p2], scale=0.125)
                    if d >= 0:
                        tt("dve", p1[:, q0:q0 + 128], p1[:, q0:q0 + 128], tri_b[:, :], ALU.mult, [p1, tri_b], [p1])
                        tt("dve", p2[:, q0:q0 + 128], p2[:, q0:q0 + 128], tri_b[:, :], ALU.mult, [p2, tri_b], [p2])
                    return (kb, q0, p1, p2)

                def pv_mm(info):
                    kb, q0, p1, p2 = info
                    first = kb == 0
                    last = kb == nkb - 1
                    vb = Vb[kb // 4]
                    mm(bO1[:, q0:512], Vt[:, kb, :], p1[:, q0:512], first, last, [vb, p1], [bO1])
                    mm(bZ1[:, q0:512], ones_b[:, :], p1[:, q0:512], first, last, [ones_b, p1], [bZ1])
                    mm(bO2[:, q0:512], Vt[:, kb, :], p2[:, q0:512], first, last, [vb, p2], [bO2])
                    mm(bZ2[:, q0:512], ones_b[:, :], p2[:, q0:512], first, last, [ones_b, p2], [bZ2])

                def stage2():
                    s_mm(0)
                    for kb in range(nkb):
                        info = exp_pv(kb)
                        if kb + 1 < nkb:
                            s_mm(kb + 1)
                        pv_mm(info)
                        k = -(-len(chain) // (nkb - kb))
                        for _ in range(k):
                            chain.pop(0)()
                    while chain:
                        chain.pop(0)()

                r1, r2, oa, ob = W["E"], W["ek"], W["D2"], W["bT"]
                bTr, bSa, bD0, bD1, bOh, bSV = banks[2], banks[3], banks[4], banks[5], banks[6], banks[7]
                bTrb = bTr.ap.bitcast(BF16)

                def a1():
                    S.op("dve", lambda e: e.reciprocal(r1[:, :], bZ1[:, :]), [bZ1], [r1])
                    S.op("dve", lambda e: e.reciprocal(r2[:, :], bZ2[:, :]), [bZ2], [r2])
                    tt("dve", oa[:, :], bO1[:, :], r1[:, :], ALU.mult, [bO1, r1], [oa])
                    tt("dve", ob[:, :], bO2[:, :], r2[:, :], ALU.mult, [bO2, r2], [ob])
                    stt(oa[:, :], ob[:, :], sc[:, 4:5], oa[:, :], ALU.mult, ALU.add, [ob, sc, oa], [oa])
                    act(ob[:, :], oa[:, :], AF.Square, [oa], [ob])

                def h1():
                    for i in range(4):
                        tr(bTrb[:, 128 * i:128 * i + 128], kdT[:, 128 * i:128 * i + 128], ident_b[:, :], [kdT, ident_b], [bTr])
                    cp("act", kd_tm[:, :, :], bTrb[:, 0:512].rearrange("p (a b) -> p a b", a=4, b=128), [bTr], [kd_tm])

                def a2():
                    mm(bSa[:, :], ones_f, ob[:, :], True, True, [c_f32, ob], [bSa])
                    act(ob[:, :], bSa[:, :], AF.Ln, [bSa], [ob], scale=1.0 / 128.0, bias=RMS_EPS)
                    act(ob[:, :], ob[:, :], AF.Exp, [ob], [ob], scale=-0.5)
                    stt(oa[:, :], oa[:, :], sc[:, 5:6], ob[:, :], ALU.mult, ALU.mult, [oa, sc, ob], [oa])
                    stt(mx[:, 2, :], oa[:, :], 0.5, W["sgC"][:, :], ALU.mult, ALU.mult, [oa, W["sgC"]], [mx])

                def h2():
                    for c in range(8):
                        bd = bD0 if c % 2 == 0 else bD1
                        r0 = 64 * (c % 2)
                        mm(bd[:, 128 * (c // 2):128 * (c // 2) + 128], kd_tm[r0:r0 + 64, c // 2, :], iv_tm[r0:r0 + 64, c // 2, :],
                           True, True, [kd_tm, iv_tm], [bd])
                    for i in range(4):
                        mm(bTr[:, 128 * i:128 * i + 128], kt[:, 128 * i:128 * i + 128], qt[:, 128 * i:128 * i + 128],
                           True, True, [kt, qt], [bTr])
                    tt("dve", am[:, :, :], bTr[:, :].rearrange("p (a b) -> p a b", a=4, b=128),
                       hmask_b[:, :].unsqueeze(1).to_broadcast([128, 4, 128]), ALU.mult, [bTr, hmask_b], [am])
                    for c in range(8):
                        bd = bD0 if c % 2 == 0 else bD1
                        ts("dve", Ssc[:, c, :], Sst[:, :], ebm[:, c:c + 1], None, ALU.mult, None, [Sst, ebm], [Ssc])
                        stt(Sst[:, :], Sst[:, :], ebl[:, c:c + 1], bd[:, 128 * (c // 2):128 * (c // 2) + 128], ALU.mult, ALU.add,
                            [Sst, ebl, bd], [Sst])

                def v3():
                    for i in range(4):
                        mm(bSV[:, 128 * i:128 * i + 128], vn[i][:, :], WTm[:, :], i == 0, i == 3, [vn[i], WTm], [bSV])
                    o1 = W["u1"]
                    tt("dve", o1[:, :].rearrange("p (a b) -> p a b", a=4, b=128),
                       bSV[:, :].rearrange("p (a b) -> p a b", a=4, b=128),
                       bsb[:, :].unsqueeze(1).to_broadcast([128, 4, 128]), ALU.add, [bSV, bsb], [o1])
                    tt("dve", o1[:, :], o1[:, :], W["u2"][:, :], ALU.mult, [o1, W["u2"]], [o1])
                    stt(mx[:, 3, :], o1[:, :], 0.25, W["sgD"][:, :], ALU.mult, ALU.mult, [o1, W["sgD"]], [mx])

                def h3():
                    for i in range(4):
                        mm(bOh[:, 128 * i:128 * i + 128], iv_tm[:, i, :], am[:, i, :], i == 0, False, [iv_tm, am], [bOh])
                    for c in range(8):
                        mm(bOh[:, 64 * c:64 * c + 64], Ssc[:, c, :], qt[:, 64 * c:64 * c + 64], False, c == 7, [Ssc, qt], [bOh])
                    act(W["f"][:, :], bOh[:, :], AF.Square, [bOh], [W["f"]])

                def h4():
                    fq = W["f"]
                    mm(bSa[:, :], ones_f, fq[:, :], True, True, [c_f32, fq], [bSa])
                    act(fq[:, :], bSa[:, :], AF.Ln, [bSa], [fq], scale=1.0 / 128.0, bias=RMS_EPS)
                    act(fq[:, :], fq[:, :], AF.Exp, [fq], [fq], scale=-0.5)
                    stt(fq[:, :], bOh[:, :], pc[:, 5:6], fq[:, :], ALU.mult, ALU.mult, [bOh, pc, fq], [fq])
                    stt(mx[:, 1, :], fq[:, :], 0.5, W["sgB"][:, :], ALU.mult, ALU.mult, [fq, W["sgB"]], [mx])

                def fin():
                    dst = mix_send[L].ap()[t].rearrange("(g p) t -> p g t", p=128)
                    dma("sp", dst, mx[:, :, :], [mx], [d_mix_send[L][t]])
                    allgather(mix_send[L].ap()[t], mix_all[L].ap()[t], [d_mix_send[L][t]], [d_mix_all[L][t]])

                st3 += [a1, h1, a2, h2, v3, h3, h4, fin]
                return tiles, stage2, st3

            load_xT(0)
            for pi in (0, 1, 4, 2, 3):
                load_wc(pi)
            while deferred_ag:
                deferred_ag.pop(0)(extra=[xTb[0]])
            pend3 = []
            for t in range(NBLK):
                if t + 1 < NBLK:
                    load_xT(t + 1)
                tiles, stage2, st3 = make_block(t)
                for k, tile in enumerate(tiles):
                    tile()
                    if pend3:
                        pend3.pop(0)()
                while pend3:
                    pend3.pop(0)()
                stage2()
                pend3 = st3
                if t == NBLK - 1 and debug is None:
                    mixT_ap, wpc_ap = p2_front_aps()
                    tm = T(mixT_ap, "pf_mix")
                    inherit(tm, [Wc[0], Wc[1]])
                    tw = [T(wpc_ap[0], "pf_w0"), T(wpc_ap[1], "pf_w1")]
                    inherit(tw[0], [Wc[2]])
                    inherit(tw[1], [Wc[3], Wc[4]])
                    S.dma("pool", mk_mix_load(L, mixT_ap, 0), d_mix_all[L][0:7], [tm])
                    wov_ = wout.ap()[L].rearrange("(kc p) n -> p kc n", p=128)
                    for j in range(2):
                        dma("pool", wpc_ap[j], wov_[:, :, 512 * j:512 * j + 512], [], [tw[j]])
            while pend3:
                pend3.pop(0)()

            S.barrier_all()

        def phase2(L, rank_q):
            A.reset(A_BASE)
            last = L == DEPTH - 1
            x_src = xq if L == 0 else xcur
            d_xsrc = None if L == 0 else d_xcur
            x_dst = out if last else xcur
            mixT = A.bf16(16 * TQ, "mixT", (16, TQ))
            wpc = [A.bf16(16 * 512, f"wpc{i}", (16, 512)) for i in range(2)]
            xpT = A.bf16(16 * TQ, "xpT", (16, TQ))
            wpe_b = A.bf16(2 * DM, "wpe", (2, DM))
            pTb = A.bf16(2 * TQ, "pTb", (2, TQ))
            lng = A.f32(DM, "lng")
            lnb = A.f32(DM, "lnb")
            xpc = [A.f32(512, f"xpc{i}") for i in range(4)]
            rp = [A.f32(512, f"rp{i}") for i in range(2)]
            rt = [A.f32(DM, f"rt{i}") for i in range(4)]
            xb = [A.bf16(DM, f"xb{i}") for i in range(2)]
            stq = [A.f32(32, f"st{i}") for i in range(2)]
            mvq = [A.f32(8, f"mv{i}") for i in range(2)]
            xnb = [A.bf16(512, f"xnb{i}") for i in range(2)]
            thg = [A.f32(512, f"thg{i}") for i in range(2)]
            st = A.f32(32, "st")
            mv = A.f32(8, "mv")
            xnT = mixT
            xnTc = [T(None, f"xnTc{q}") for q in range(4)]

            mav = mix_all[L].ap()
            wov = wout.ap()[L].rearrange("(kc p) n -> p kc n", p=128)
            wgv = wpg.ap()[L].rearrange("(kc p) n -> p kc n", p=128)
            pieces = [(wov, n) for n in range(4)] + [(wgv, n) for n in range(4)]

            def load_piece(k):
                v, n = pieces[k]
                dma("pool", wpc[k % 2][:, :, :], v[:, :, 512 * n:512 * n + 512], [], [wpc[k % 2]])

            mixTh = [T(mixT.ap, "mixT0"), T(mixT.ap, "mixT1")]
            S.dma("pool", mk_mix_load(L, mixT.ap, 1), d_mix_all[L], [mixTh[1]])
            dma("pool", wpe_b[:, :, :], wpe.ap()[L].rearrange("(kc p) n -> p kc n", p=128), [], [wpe_b])
            dma("pool", pTb[:, :, :], pT.ap()[L].rearrange("(kc p) t -> p kc t", p=128), [], [pTb])
            dma("sp", lng[:, :], prow[L, 0:1, 640:640 + DM].to_broadcast([128, DM]), [], [lng])
            dma("sp", lnb[:, :], prow[L, 0:1, 640 + DM:640 + 2 * DM].to_broadcast([128, DM]), [], [lnb])
            def b0(i):
                dma("sp", rt[i % 4][:, :], rscr[128 * i:128 * i + 128, :], [d_rscr[i]], [rt[i % 4]])

            def b1(i):
                r_, st_, mv_ = rt[i % 4], stq[i % 2], mvq[i % 2]
                for c in range(4):
                    S.op("dve", lambda e, c=c, r_=r_, st_=st_: e.bn_stats(st_[:, 6 * c:6 * c + 6], r_[:, 512 * c:512 * c + 512]), [r_], [st_])
                S.op("dve", lambda e, st_=st_, mv_=mv_: e.bn_aggr(mv_[:, 0:2], st_[:, 0:24]), [st_], [mv_])
                act(mv_[:, 2:3], mv_[:, 1:2], AF.Ln, [mv_], [mv_], bias=LN_EPS)
                act(mv_[:, 2:3], mv_[:, 2:3], AF.Exp, [mv_], [mv_], scale=-0.5)
                stt(mv_[:, 3:4], mv_[:, 0:1], -1.0, mv_[:, 2:3], ALU.mult, ALU.mult, [mv_], [mv_])

            def b2(i):
                r_, mv_ = rt[i % 4], mvq[i % 2]
                act(r_[:, :], r_[:, :], AF.Identity, [r_, mv_], [r_], scale=mv_[:, 2:3], bias=mv_[:, 3:4])
                tt("pool", r_[:, :], r_[:, :], lng[:, :], ALU.mult, [r_, lng], [r_])

            def b3(i):
                r_ = rt[i % 4]
                tt("dve", r_[:, :], r_[:, :], lnb[:, :], ALU.add, [r_, lnb], [r_])
                dma("sp", xps[128 * i:128 * i + 128, :], r_[:, :], [r_], [d_xps[i]])
                cp("act", xb[i % 2][:, :], r_[:, :], [r_], [xb[i % 2]])

            def b4(i):
                transposes_to(xb[i % 2], xpT, 128 * i, 16, 0, (4, 5) if i % 2 == 0 else (6, 7), i)

            def pstep(step):
                for stage, fn in ((0, b0), (1, b1), (2, b2), (3, b3), (4, b4)):
                    i_ = step - stage
                    if 0 <= i_ < 8:
                        fn(i_)

            cnt = 0
            itsA = [(n_, i_) for n_ in range(4) for i_ in range(8)]

            def load_xA(k):
                if k < len(itsA):
                    n_, i_ = itsA[k]
                    dma("sp", xpc[k % 4][:, :], x_src[128 * i_:128 * i_ + 128, 512 * n_:512 * n_ + 512],
                        [] if d_xsrc is None else [d_xsrc[i_]], [xpc[k % 4]])

            for k_ in range(3):
                load_xA(k_)
            for n in range(4):
                if n >= 1:
                    load_piece(n + 1)
                wp = wpc[n % 2]
                for i in range(8):
                    bk = banks[cnt % 2]
                    xp_ = xpc[cnt % 4]
                    r_ = rp[cnt % 2]
                    load_xA(cnt + 3)
                    cnt += 1
                    for kc in range(16):
                        mm(bk[:, :], mixT[:, kc, 128 * i:128 * i + 128], wp[:, kc, :], kc == 0, kc == 15, [mixTh[i // 4], wp], [bk])
                    stt(r_[:, :], xp_[:, :], ALPHA, bk[:, :], ALU.mult, ALU.add, [xp_, bk], [r_])
                    dma("sp", rscr[128 * i:128 * i + 128, 512 * n:512 * n + 512], r_[:, :], [r_], [d_rscr[i]])
                    if n == 3:
                        pstep(i)
            for q in range(4):
                hb = mixTh[q // 2].buf
                xnTc[q].buf.w = hb.w
                xnTc[q].buf.r = dict(hb.r)
            for step in range(8, 8 + 4):
                pstep(step)
            cnt = 0
            pend = []
            itsC = [(n_, i_) for n_ in range(4) for i_ in range(8)]

            def load_xC(k):
                if k < len(itsC):
                    n_, i_ = itsC[k]
                    dma("sp", xpc[k % 4][:, :], xps[128 * i_:128 * i_ + 128, 512 * n_:512 * n_ + 512], [d_xps[i_]], [xpc[k % 4]])

            for k_ in range(3):
                load_xC(k_)
            for n in range(4):
                k = 4 + n
                if k + 1 < 8:
                    load_piece(k + 1)
                wp = wpc[k % 2]
                for i in range(8):
                    bkg = banks[cnt % 2]
                    bke = banks[2 + cnt % 2]
                    bkt = banks[4 + cnt % 2]
                    xp_ = xpc[cnt % 4]
                    r_ = rp[cnt % 2]
                    th_ = thg[cnt % 2]
                    xnb_ = xnb[cnt % 2]
                    load_xC(cnt + 3)
                    cnt += 1
                    for kc in range(16):
                        mm(bkg[:, :], xpT[:, kc, 128 * i:128 * i + 128], wp[:, kc, :], kc == 0, kc == 15, [xpT, wp], [bkg])
                    for kc in range(2):
                        mm(bke[:, :], pTb[:, kc, 128 * i:128 * i + 128], wpe_b[:, kc, 512 * n:512 * n + 512], kc == 0, kc == 1,
                           [pTb, wpe_b], [bke])
                    act(th_[:, :], bkg[:, :], AF.Tanh, [bkg], [th_], scale=0.5)
                    stt(th_[:, :], th_[:, :], 1.0, bke[:, :], ALU.add, ALU.mult, [th_, bke], [th_])
                    stt(r_[:, :], th_[:, :], 0.5, xp_[:, :], ALU.mult, ALU.add, [th_, xp_], [r_])
                    dma("sp", x_dst[128 * i:128 * i + 128, 512 * n:512 * n + 512], r_[:, :], [r_],
                        [d_out] if last else [d_xcur[i]])
                    if not last:
                        cp("act", xnb_[:, :], r_[:, :], [r_], [xnb_])

                        def fin(n=n, i=i, xnb_=xnb_, bkt=bkt):
                            bktb = bkt.ap.bitcast(BF16)
                            for j in range(4):
                                tr(bktb[:, 128 * j:128 * j + 128], xnb_[:, 128 * j:128 * j + 128], ident_b[:, :], [xnb_, ident_b], [bkt])
                            cp("dve", xnT[:, 4 * n:4 * n + 4, 128 * i:128 * i + 128],
                               bktb[:, 0:512].rearrange("p (a b) -> p a b", a=4, b=128), [bkt], [xnTc[i // 2]])
                            if n == 3 and i % 2 == 1:
                                send_xT(L + 1, i // 2, xnT, xnTc[i // 2])
                        pend.append(fin)
                    if len(pend) > 1:
                        pend.pop(0)()
            while pend:
                pend.pop(0)()
            while pending_ag:
                pending_ag.pop(0)()
            S.barrier_all()

        rank_holder = {}
        if debug == "p0":
            phase0()
            dma("sp", dbg.ap(), xT_all[0].ap(), d_xT_all[0], [d_out])
        else:
            for L in range(DEPTH):
                phase1(L)
                if debug == "p1":
                    dma("sp", dbg.ap(), mix_send[L].ap(), d_mix_send[L], [d_out])
                    break
                phase2(L, "RANKQ")
        S.final_wait()

        with nc.Block() as block:
            @block.sync
            def _(e):
                S.emit("sp", e)

            @block.scalar
            def _(e):
                S.emit("act", e)

            @block.vector
            def _(e):
                S.emit("dve", e)

            @block.tensor
            def _(e):
                S.emit("pe", e)

            @block.gpsimd
            def _(e):
                pid = e.partition_id()
                RANK["q"] = pid % 4
                S.emit("pool", e)
    return nc


RANK = {}


_CACHE = {}


def kernel(**inputs):
    maps = _prep_inputs(inputs)
    if "nc" not in _CACHE:
        _CACHE["nc"] = build_program()
    res = run_bass_kernel_spmd(_CACHE["nc"], maps, core_ids=list(range(NCORE)))
    outp = np.zeros((2, SEQ, DM), np.float32)
    for c in range(NCORE):
        b, h = divmod(c, 4)
        outp[b, TQ * h:TQ * h + TQ, :] = np.asarray(res.results[c]["out"], np.float32)
    return outp
```
